# Optimizing a Trainium2 kernel written in Bass

```python
import math
import jax, jax.numpy as jnp
from jax import lax
import numpy as np

D_MODEL = 1024
BATCH = 16
SEQ = 2048
DEPTH = 2
DEC_BATCH = 16
DEC_SEQ = 16
PAST_LEN = 1024

CHUNK = 64
Q_BLOCK = 128
N_EVEN = (DEPTH + 1) // 2
N_ODD = DEPTH // 2
EPS = 1e-6
NEG = -1e30
D_FF = 2816
A_HEADS = 8
A_HEAD_DIM = 64
A_PAST_CHUNKS = 8
A_BAND = (A_PAST_CHUNKS + 1) * CHUNK
A_REL_CLIP = 64
B_HEADS = 4
B_HEAD_DIM = 64
T5_BUCKETS = 32
T5_MAX_DIST = 128
C_HEADS = 16
C_NOPE = 64
C_ROPE = 32
C_V = 64
C_Q_LORA = 384
C_KV_LORA = 256
ROPE_THETA = 10000.0

A_W = A_HEADS * A_HEAD_DIM
B_W = B_HEADS * 2 * B_HEAD_DIM
E_IN_W = 3 * A_W + 3 * B_W
E_OUT_W = A_W + B_W
C_IN_W = C_Q_LORA + C_KV_LORA + C_ROPE

kernel_name = 'chunked_relbias_diffattn_mla_macaron_stream_step'

F32 = jnp.float32


def rmsnorm(x, g):
    x32 = x.astype(F32)
    y = x32 * lax.rsqrt(jnp.mean(x32 * x32, axis=-1, keepdims=True) + EPS)
    return (y * g.astype(F32)).astype(x.dtype)


def swiglu_ffn(x, g, w_gu, w_down):
    h = rmsnorm(x, g) @ w_gu
    gate, up = jnp.split(h, 2, axis=-1)
    return (jax.nn.silu(gate) * up) @ w_down


def masked_softmax(logits, mask):
    return jax.nn.softmax(jnp.where(mask, logits, NEG), axis=-1)


def chunk_causal_mask(q_pos, k_pos):
    return (k_pos[None, :] // CHUNK) <= (q_pos[:, None] // CHUNK)


def t5_bucket(rel):
    nb = T5_BUCKETS // 2
    max_exact = nb // 2
    ret = jnp.where(rel > 0, nb, 0)
    n = jnp.abs(rel)
    n_f = jnp.maximum(n, 1).astype(F32)
    large = max_exact + (jnp.log(n_f / max_exact) / math.log(T5_MAX_DIST / max_exact)
                         * (nb - max_exact)).astype(jnp.int32)
    large = jnp.minimum(large, nb - 1)
    return ret + jnp.where(n < max_exact, n, large)


def rope(x, pos):
    half = x.shape[-1] // 2
    inv = ROPE_THETA ** (-jnp.arange(half, dtype=F32) / half)
    ang = pos.astype(F32)[:, None] * inv[None, :]
    cos = jnp.cos(ang)[None, :, None, :]
    sin = jnp.sin(ang)[None, :, None, :]
    x32 = x.astype(F32)
    x1, x2 = x32[..., :half], x32[..., half:]
    return jnp.concatenate([x1 * cos - x2 * sin, x1 * sin + x2 * cos], axis=-1).astype(x.dtype)


def sweep_queries(fn, n_q, *q_arrays):
    def blk(i):
        start = i * Q_BLOCK
        qs = [lax.dynamic_slice_in_dim(a, start, Q_BLOCK, axis=1) for a in q_arrays]
        return fn(start + jnp.arange(Q_BLOCK), *qs)
    out = lax.map(blk, jnp.arange(n_q // Q_BLOCK))
    out = jnp.moveaxis(out, 0, 1)
    return out.reshape((out.shape[0], n_q) + out.shape[3:])


def a_attend(q, k, v, q_pos, k_pos, rel_table):
    logits = jnp.einsum('bqhd,bkhd->bhqk', q, k, preferred_element_type=F32) * (A_HEAD_DIM ** -0.5)
    rel = jnp.clip(q_pos[:, None] - k_pos[None, :], -A_REL_CLIP, A_REL_CLIP) + A_REL_CLIP
    logits = logits + rel_table[:, rel].astype(F32)
    qc, kc = q_pos[:, None] // CHUNK, k_pos[None, :] // CHUNK
    mask = (k_pos[None, :] >= 0) & (kc <= qc) & (kc >= qc - A_PAST_CHUNKS)
    p = masked_softmax(logits, mask)
    return jnp.einsum('bhqk,bkhd->bqhd', p.astype(v.dtype), v)


def a_prompt(q, k, v, rel_table):
    b, s = q.shape[:2]
    pad = A_PAST_CHUNKS * CHUNK
    kp = jnp.pad(k, ((0, 0), (pad, 0), (0, 0), (0, 0)))
    vp = jnp.pad(v, ((0, 0), (pad, 0), (0, 0), (0, 0)))

    def one_chunk(c):
        start = c * CHUNK
        qc = lax.dynamic_slice_in_dim(q, start, CHUNK, axis=1)
        kb = lax.dynamic_slice_in_dim(kp, start, A_BAND, axis=1)
        vb = lax.dynamic_slice_in_dim(vp, start, A_BAND, axis=1)
        q_pos = start + jnp.arange(CHUNK)
        k_pos = start - pad + jnp.arange(A_BAND)
        return a_attend(qc, kb, vb, q_pos, k_pos, rel_table)

    out = lax.map(one_chunk, jnp.arange(s // CHUNK))
    return jnp.moveaxis(out, 0, 1).reshape(b, s, A_HEADS, A_HEAD_DIM)


def a_sample(q, k, v, cache_k, cache_v, q_pos, rel_table):
    n_c = cache_k.shape[1]
    k_all = jnp.concatenate([cache_k, k], axis=1)
    v_all = jnp.concatenate([cache_v, v], axis=1)
    k_pos = jnp.concatenate([PAST_LEN - n_c + jnp.arange(n_c), q_pos])
    return a_attend(q, k_all, v_all, q_pos, k_pos, rel_table)


def diff_lambda(lq1, lk1, lq2, lk2, lam_init):
    return (jnp.exp(jnp.sum(lq1.astype(F32) * lk1.astype(F32)))
            - jnp.exp(jnp.sum(lq2.astype(F32) * lk2.astype(F32))) + lam_init)


def b_attend(q, k, v, q_pos, k_pos, t5_table, lam):
    q2 = q.reshape(q.shape[:3] + (2, B_HEAD_DIM))
    k2 = k.reshape(k.shape[:3] + (2, B_HEAD_DIM))
    logits = jnp.einsum('bqhmd,bkhmd->bhmqk', q2, k2, preferred_element_type=F32) * (B_HEAD_DIM ** -0.5)
    bias = t5_table[t5_bucket(k_pos[None, :] - q_pos[:, None])]
    logits = logits + jnp.transpose(bias, (2, 0, 1))[:, None].astype(F32)
    p = masked_softmax(logits, chunk_causal_mask(q_pos, k_pos))
    w = p[:, :, 0] - lam * p[:, :, 1]
    return jnp.einsum('bhqk,bkhe->bqhe', w.astype(v.dtype), v)


def even_project(h, w_in):
    b, s, _ = h.shape
    cuts = [A_W, 2 * A_W, 3 * A_W, 3 * A_W + B_W, 3 * A_W + 2 * B_W]
    aq, ak, av, bq, bk, bv = jnp.split(h @ w_in, cuts, axis=-1)
    ra = lambda t: t.reshape(b, s, A_HEADS, A_HEAD_DIM)
    rb = lambda t: t.reshape(b, s, B_HEADS, 2 * B_HEAD_DIM)
    return ra(aq), ra(ak), ra(av), rb(bq), rb(bk), rb(bv)


def even_merge(a_out, b_out, lam_init, subln_g, w_out):
    b, s = a_out.shape[:2]
    b_n = rmsnorm(b_out, subln_g) * (1.0 - lam_init)
    o = jnp.concatenate([a_out.reshape(b, s, A_W), b_n.reshape(b, s, B_W)], axis=-1)
    return o @ w_out


def c_project(h, pos, w_in, g_q, g_kv, w_q_up):
    b, s, _ = h.shape
    cq, ckv, kr = jnp.split(h @ w_in, [C_Q_LORA, C_Q_LORA + C_KV_LORA], axis=-1)
    q = (rmsnorm(cq, g_q) @ w_q_up).reshape(b, s, C_HEADS, C_NOPE + C_ROPE)
    q_nope = q[..., :C_NOPE]
    q_rope = rope(q[..., C_NOPE:], pos)
    ckv = rmsnorm(ckv, g_kv)
    kr = rope(kr[:, :, None, :], pos)[:, :, 0, :]
    return q_nope, q_rope, ckv, kr


def c_expand(c_kv, w_kv_up):
    b, k, _ = c_kv.shape
    kv = (c_kv @ w_kv_up).reshape(b, k, C_HEADS, C_NOPE + C_V)
    return kv[..., :C_NOPE], kv[..., C_NOPE:]


def c_attend(q_nope, q_rope, k_nope, v, k_rope, q_pos, k_pos):
    scale = (C_NOPE + C_ROPE) ** -0.5
    logits = (jnp.einsum('bqhd,bkhd->bhqk', q_nope, k_nope, preferred_element_type=F32)
              + jnp.einsum('bqhr,bkr->bhqk', q_rope, k_rope, preferred_element_type=F32)) * scale
    p = masked_softmax(logits, chunk_causal_mask(q_pos, k_pos))
    return jnp.einsum('bhqk,bkhd->bqhd', p.astype(v.dtype), v)


def setup_inputs(seed: int = 0) -> dict:
    key = jax.random.key(seed)
    ks = iter(jax.random.split(key, 40))
    nrm = lambda shape, scale: jax.random.normal(next(ks), shape, F32) * scale
    gain = lambda shape: 1.0 + 0.01 * jax.random.normal(next(ks), shape, F32)
    a_cache = min(A_PAST_CHUNKS * CHUNK, PAST_LEN)
    return {
        'x_prompt': nrm((BATCH, SEQ, D_MODEL), 1.0),
        'x_sample': nrm((DEC_BATCH, DEC_SEQ, D_MODEL), 1.0),
        'cache_a_k': nrm((N_EVEN, DEC_BATCH, a_cache, A_HEADS, A_HEAD_DIM), 1.0),
        'cache_a_v': nrm((N_EVEN, DEC_BATCH, a_cache, A_HEADS, A_HEAD_DIM), 1.0),
        'cache_b_k': nrm((N_EVEN, DEC_BATCH, PAST_LEN, B_HEADS, 2 * B_HEAD_DIM), 1.0),
        'cache_b_v': nrm((N_EVEN, DEC_BATCH, PAST_LEN, B_HEADS, 2 * B_HEAD_DIM), 1.0),
        'cache_c_kv': nrm((N_ODD, DEC_BATCH, PAST_LEN, C_KV_LORA), 1.0),
        'cache_c_kr': nrm((N_ODD, DEC_BATCH, PAST_LEN, C_ROPE), 1.0),
        't5_bias': nrm((T5_BUCKETS, B_HEADS), 0.2),
        'ffn1_norm': gain((DEPTH, D_MODEL)),
        'ffn1_w_gu': nrm((DEPTH, D_MODEL, 2 * D_FF), D_MODEL ** -0.5),
        'ffn1_w_down': nrm((DEPTH, D_FF, D_MODEL), D_FF ** -0.5),
        'mix_norm': gain((DEPTH, D_MODEL)),
        'ffn2_norm': gain((DEPTH, D_MODEL)),
        'ffn2_w_gu': nrm((DEPTH, D_MODEL, 2 * D_FF), D_MODEL ** -0.5),
        'ffn2_w_down': nrm((DEPTH, D_FF, D_MODEL), D_FF ** -0.5),
        'e_w_in': nrm((N_EVEN, D_MODEL, E_IN_W), D_MODEL ** -0.5),
        'a_rel_bias': nrm((N_EVEN, A_HEADS, 2 * A_REL_CLIP + 1), 0.2),
        'b_lambda_q1': nrm((N_EVEN, B_HEAD_DIM), 0.1),
        'b_lambda_k1': nrm((N_EVEN, B_HEAD_DIM), 0.1),
        'b_lambda_q2': nrm((N_EVEN, B_HEAD_DIM), 0.1),
        'b_lambda_k2': nrm((N_EVEN, B_HEAD_DIM), 0.1),
        'b_subln': gain((N_EVEN, 2 * B_HEAD_DIM)),
        'e_w_out': nrm((N_EVEN, E_OUT_W, D_MODEL), E_OUT_W ** -0.5),
        'c_w_in': nrm((N_ODD, D_MODEL, C_IN_W), D_MODEL ** -0.5),
        'c_q_norm': gain((N_ODD, C_Q_LORA)),
        'c_kv_norm': gain((N_ODD, C_KV_LORA)),
        'c_w_q_up': nrm((N_ODD, C_Q_LORA, C_HEADS * (C_NOPE + C_ROPE)), C_Q_LORA ** -0.5),
        'c_w_kv_up': nrm((N_ODD, C_KV_LORA, C_HEADS * (C_NOPE + C_V)), C_KV_LORA ** -0.5),
        'c_w_out': nrm((N_ODD, C_HEADS * C_V, D_MODEL), (C_HEADS * C_V) ** -0.5),
        'final_norm': gain((D_MODEL,)),
    }


def reference(x_prompt, x_sample, cache_a_k, cache_a_v, cache_b_k, cache_b_v, cache_c_kv, cache_c_kr,
              t5_bias, ffn1_norm, ffn1_w_gu, ffn1_w_down, mix_norm, ffn2_norm, ffn2_w_gu, ffn2_w_down,
              e_w_in, a_rel_bias, b_lambda_q1, b_lambda_k1, b_lambda_q2, b_lambda_k2, b_subln, e_w_out,
              c_w_in, c_q_norm, c_kv_norm, c_w_q_up, c_w_kv_up, c_w_out, final_norm):
    yp, ys = x_prompt, x_sample
    seq, t_new = x_prompt.shape[1], x_sample.shape[1]
    pos_p = jnp.arange(seq)
    pos_s = PAST_LEN + jnp.arange(t_new)
    pos_all_s = jnp.arange(PAST_LEN + t_new)
    a_keep = min(A_PAST_CHUNKS * CHUNK, seq)
    pak, pav, pbk, pbv, pckv, pckr = [], [], [], [], [], []
    sak, sav, sbk, sbv, sckv, sckr = [], [], [], [], [], []

    for layer in range(DEPTH):
        yp = yp + 0.5 * swiglu_ffn(yp, ffn1_norm[layer], ffn1_w_gu[layer], ffn1_w_down[layer])
        ys = ys + 0.5 * swiglu_ffn(ys, ffn1_norm[layer], ffn1_w_gu[layer], ffn1_w_down[layer])
        hp = rmsnorm(yp, mix_norm[layer])
        hs = rmsnorm(ys, mix_norm[layer])
        if layer % 2 == 0:
            e = layer // 2
            lam_init = 0.8 - 0.6 * math.exp(-0.3 * layer)
            lam = diff_lambda(b_lambda_q1[e], b_lambda_k1[e], b_lambda_q2[e], b_lambda_k2[e], lam_init)
            aq, ak, av, bq, bk, bv = even_project(hp, e_w_in[e])
            a_out = a_prompt(aq, ak, av, a_rel_bias[e])
            b_out = sweep_queries(lambda qp, qb: b_attend(qb, bk, bv, qp, pos_p, t5_bias, lam), seq, bq)
            yp = yp + even_merge(a_out, b_out, lam_init, b_subln[e], e_w_out[e])
            pak.append(ak[:, seq - a_keep:])
            pav.append(av[:, seq - a_keep:])
            pbk.append(bk)
            pbv.append(bv)
            aq, ak, av, bq, bk, bv = even_project(hs, e_w_in[e])
            a_out = a_sample(aq, ak, av, cache_a_k[e], cache_a_v[e], pos_s, a_rel_bias[e])
            kb_all = jnp.concatenate([cache_b_k[e], bk], axis=1)
            vb_all = jnp.concatenate([cache_b_v[e], bv], axis=1)
            b_out = b_attend(bq, kb_all, vb_all, pos_s, pos_all_s, t5_bias, lam)
            ys = ys + even_merge(a_out, b_out, lam_init, b_subln[e], e_w_out[e])
            sak.append(ak)
            sav.append(av)
            sbk.append(bk)
            sbv.append(bv)
        else:
            o = layer // 2
            qn, qr, ckv, kr = c_project(hp, pos_p, c_w_in[o], c_q_norm[o], c_kv_norm[o], c_w_q_up[o])
            kn, vv = c_expand(ckv, c_w_kv_up[o])
            c_out = sweep_queries(lambda qp, a, b: c_attend(a, b, kn, vv, kr, qp, pos_p), seq, qn, qr)
            yp = yp + c_out.reshape(c_out.shape[0], seq, C_HEADS * C_V) @ c_w_out[o]
            pckv.append(ckv)
            pckr.append(kr)
            qn, qr, ckv, kr = c_project(hs, pos_s, c_w_in[o], c_q_norm[o], c_kv_norm[o], c_w_q_up[o])
            ckv_all = jnp.concatenate([cache_c_kv[o], ckv], axis=1)
            kr_all = jnp.concatenate([cache_c_kr[o], kr], axis=1)
            kn, vv = c_expand(ckv_all, c_w_kv_up[o])
            c_out = c_attend(qn, qr, kn, vv, kr_all, pos_s, pos_all_s)
            ys = ys + c_out.reshape(c_out.shape[0], t_new, C_HEADS * C_V) @ c_w_out[o]
            sckv.append(ckv)
            sckr.append(kr)
        yp = yp + 0.5 * swiglu_ffn(yp, ffn2_norm[layer], ffn2_w_gu[layer], ffn2_w_down[layer])
        ys = ys + 0.5 * swiglu_ffn(ys, ffn2_norm[layer], ffn2_w_gu[layer], ffn2_w_down[layer])

    y_prompt = rmsnorm(yp, final_norm)
    y_sample = rmsnorm(ys, final_norm)
    p_a_k, p_a_v = jnp.stack(pak), jnp.stack(pav)
    p_b_k, p_b_v = jnp.stack(pbk), jnp.stack(pbv)
    p_c_kv, p_c_kr = jnp.stack(pckv), jnp.stack(pckr)
    s_a_k, s_a_v = jnp.stack(sak), jnp.stack(sav)
    s_b_k, s_b_v = jnp.stack(sbk), jnp.stack(sbv)
    s_c_kv, s_c_kr = jnp.stack(sckv), jnp.stack(sckr)
    return (y_prompt, y_sample, p_a_k, p_a_v, p_b_k, p_b_v, p_c_kv, p_c_kr,
            s_a_k, s_a_v, s_b_k, s_b_v, s_c_kv, s_c_kr)
```

```python
import math
import os
import numpy as np
import concourse.bass as bass
import concourse.mybir as mybir
from concourse.bass_utils import run_bass_kernel_spmd

F32 = mybir.dt.float32
BF16 = mybir.dt.bfloat16
AF = mybir.ActivationFunctionType
ALU = mybir.AluOpType

ENGS = ('pe', 'act', 'dve', 'pool', 'sp')

D = 1024
SEQ = 2048
NSEQ = 2
TS = 16
PAST = 1024
NTOK = NSEQ * SEQ + NSEQ * TS
SOFF = NSEQ * SEQ
DFF = 2816
NJ = DFF // 128
EPS = 1e-6
CHUNK = 64
A_H, A_D = 8, 64
B_H, B_D = 4, 64
C_H, C_NOPE, C_ROPE, C_V = 16, 64, 32, 64
C_QL, C_KVL = 384, 256
LT = 1152
WM = 1024


class Buf:
    __slots__ = ('name', 'w', 'r', 'x')

    def __init__(self, name='', x=False):
        self.name = name
        self.w = None
        self.r = {}
        self.x = x


class Prog:
    def __init__(self, nc, n_dma_sems=(('sp', 48), ('pool', 24))):
        self.nc = nc
        self.q = {e: [] for e in ENGS}
        self.cnt = {e: 0 for e in ENGS}
        self.sem = {e: nc.alloc_semaphore("s_" + e) for e in ENGS}
        self.seen = {e: {} for e in ENGS}
        self.dsem, self.dcnt, self.dpool, self.drr = {}, {}, {}, {}
        for e, n in n_dma_sems:
            ids = []
            for i in range(n):
                k = (e, i)
                self.dsem[k] = nc.alloc_semaphore("d_%s_%d" % (e, i))
                self.dcnt[k] = 0
                ids.append(k)
            self.dpool[e] = ids
            self.drr[e] = 0
        self.nops = 0
        self.hist = {}

    def _need(self, eng, ev, waits):
        if ev is None:
            return
        kind, key, val = ev
        if kind == 'e' and key == eng:
            if eng == 'pe':
                return
            if val < self.cnt[eng] - 1:
                return
        k = (kind, key)
        if self.seen[eng].get(k, 0) >= val:
            return
        self.seen[eng][k] = val
        waits.append(ev)
        snap = self.hist.get(ev)
        if snap is not None:
            mine = self.seen[eng]
            for k2, v2 in snap.items():
                if mine.get(k2, 0) < v2:
                    mine[k2] = v2

    def _deps(self, eng, reads, writes):
        waits = []
        for b in reads:
            self._need(eng, b.w, waits)
            if b.x:
                for ev in b.r.values():
                    self._need(eng, ev, waits)
        for b in writes:
            self._need(eng, b.w, waits)
            for ev in b.r.values():
                self._need(eng, ev, waits)
        return waits

    def _mark(self, ev, rkey, reads, writes):
        for b in reads:
            b.r[rkey] = ev
        for b in writes:
            b.w = ev
            b.r = {}

    def op(self, eng, fn, reads=(), writes=()):
        waits = self._deps(eng, reads, writes)
        self.cnt[eng] += 1
        ev = ('e', eng, self.cnt[eng])
        snap = dict(self.seen[eng])
        snap[('e', eng)] = self.cnt[eng] - 1
        self.hist[ev] = snap
        self.q[eng].append((waits, fn, (self.sem[eng], 1)))
        self._mark(ev, ('e', eng), reads, writes)
        self.nops += 1
        return ev

    def dma(self, eng, out, in_, reads=(), writes=(), **kw):
        pool = self.dpool[eng]
        k = pool[self.drr[eng] % len(pool)]
        self.drr[eng] += 1
        waits = self._deps(eng, reads, writes)
        if self.dcnt[k] > 0:
            self._need(eng, ('d', k, self.dcnt[k]), waits)
        self.dcnt[k] += 16
        ev = ('d', k, self.dcnt[k])
        snap = dict(self.seen[eng])
        snap.pop(('e', eng), None)
        self.hist[ev] = snap

        def fn(e, out=out, in_=in_, kw=kw):
            return e.dma_start(out=out, in_=in_, **kw)
        self.q[eng].append((waits, fn, (self.dsem[k], 16)))
        self._mark(ev, ('d', k), reads, writes)
        self.nops += 1
        return ev

    def barrier(self):
        evs = [('e', e, self.cnt[e]) for e in ENGS if self.cnt[e] > 0]
        evs += [('d', kk, v) for kk, v in self.dcnt.items() if v > 0]
        for eng in ENGS:
            waits = []
            for ev in evs:
                if ev[0] == 'e' and ev[1] == eng:
                    continue
                self._need(eng, ev, waits)
            if waits:
                self.q[eng].append((waits, None, None))

    def _sem_of(self, ev):
        kind, key, val = ev
        return (self.sem[key] if kind == 'e' else self.dsem[key]), val

    def final_wait_all(self, eng='sp'):
        waits = []
        for e in ENGS:
            if e != eng and self.cnt[e] > 0:
                waits.append(('e', e, self.cnt[e]))
        for k, v in self.dcnt.items():
            if v > 0:
                waits.append(('d', k, v))
        self.q[eng].append((waits, None, None))

    def emit(self):
        nc = self.nc
        prog = self

        def run(e, name):
            for waits, fn, inc in prog.q[name]:
                for ev in waits:
                    s, v = prog._sem_of(ev)
                    e.wait_ge(s, v)
                if fn is not None:
                    ins = fn(e)
                    ins.then_inc(inc[0], inc[1])

        with nc.Block() as block:
            @block.tensor
            def _(e):
                run(e, 'pe')

            @block.scalar
            def _(e):
                run(e, 'act')

            @block.vector
            def _(e):
                run(e, 'dve')

            @block.gpsimd
            def _(e):
                run(e, 'pool')

            @block.sync
            def _(e):
                run(e, 'sp')


class SB:
    def __init__(self, nc, base=16512, limit=229344):
        self.nc = nc
        self.off = base
        self.limit = limit
        self.n = 0
        self.peak = 0
        self.on_release = None

    def alloc(self, name, shape, dtype):
        esz = 2 if dtype == BF16 else 4
        free = int(np.prod(shape[1:])) * esz
        self.off = (self.off + 63) // 64 * 64
        self.n += 1
        t = self.nc.alloc_sbuf_tensor_at("%s_%d" % (name, self.n), list(shape), dtype, offset=self.off)
        self.off += free
        self.peak = max(self.peak, self.off)
        assert self.off <= self.limit, ("SBUF overflow", name, self.off, self.limit)
        return t

    def mark(self):
        return self.off

    def release(self, m):
        self.off = m
        if self.on_release is not None:
            self.on_release()


class Ring:
    def __init__(self, items):
        self.items = items
        self.i = 0

    def next(self):
        it = self.items[self.i % len(self.items)]
        self.i += 1
        return it


class K:
    def __init__(self, nc):
        self.nc = nc
        self.P = Prog(nc)
        self.sb = SB(nc)
        self.sb.on_release = self.P.barrier
        self.cp_i = 0
        self.act_every = 2

    def mm(self, out, pairs, reads, wbuf, start=True, stop=True):
        def fn(e, pairs=list(pairs), out=out):
            n = len(pairs)
            ins = None
            for i, (l, r) in enumerate(pairs):
                ins = e.matmul(out, l, r, start=(start and i == 0), stop=(stop and i == n - 1))
            return ins
        return self.P.op('pe', fn, reads=reads, writes=[wbuf])

    def mm_multi(self, items, reads, wbufs):
        def fn(e, items=list(items)):
            ins = None
            for (o, l, r, st_, sp_) in items:
                ins = e.matmul(o, l, r, start=st_, stop=sp_)
            return ins
        return self.P.op('pe', fn, reads=reads, writes=wbufs)

    def tr(self, out, in_, ident, reads, wbuf):
        return self.P.op('pe', lambda e: e.transpose(out, in_, ident), reads=reads, writes=[wbuf])

    def trs(self, items, reads, wbuf):
        def fn(e, items=list(items)):
            ins = None
            for (o, i, idn) in items:
                ins = e.transpose(o, i, idn)
            return ins
        return self.P.op('pe', fn, reads=reads, writes=[wbuf])

    def act(self, out, in_, func, reads, writes, scale=1.0, bias=0.0, accum_out=None):
        def fn(e):
            if accum_out is not None:
                return e.activation(out=out, in_=in_, func=func, bias=bias, scale=scale, accum_out=accum_out)
            return e.activation(out=out, in_=in_, func=func, bias=bias, scale=scale)
        return self.P.op('act', fn, reads=reads, writes=writes)

    def stt(self, out, in0, scalar, in1, op0, op1, reads, writes):
        return self.P.op('dve', lambda e: e.scalar_tensor_tensor(out=out, in0=in0, scalar=scalar, in1=in1, op0=op0, op1=op1),
                         reads=reads, writes=writes)

    def ts(self, out, in0, s1, s2, op0, op1, reads, writes, eng='dve'):
        return self.P.op(eng, lambda e: e.tensor_scalar(out=out, in0=in0, scalar1=s1, scalar2=s2, op0=op0, op1=op1),
                         reads=reads, writes=writes)

    def tt(self, out, in0, in1, op, reads, writes, eng='dve'):
        return self.P.op(eng, lambda e: e.tensor_tensor(out=out, in0=in0, in1=in1, op=op), reads=reads, writes=writes)

    def recip(self, out, in_, reads, writes):
        self.act(out, in_, AF.Ln, reads, writes)
        return self.act(out, out, AF.Exp, list(writes), writes, scale=-1.0)

    def rstd(self, out, in_, reads, writes, scale, eps_ap):
        self.act(out, in_, AF.Ln, reads, writes, scale=scale, bias=eps_ap)
        return self.act(out, out, AF.Exp, list(writes), writes, scale=-0.5)

    def recip_dve(self, out, in_, reads, writes):
        return self.P.op('dve', lambda e: e.reciprocal(out=out, in_=in_), reads=reads, writes=writes)

    def copy(self, eng, out, in_, reads, writes):
        if eng == 'act':
            return self.act(out, in_, AF.Copy, reads, writes)
        return self.P.op(eng, lambda e: e.tensor_copy(out=out, in_=in_), reads=reads, writes=writes)

    def copy_rr(self, out, in_, reads, writes):
        self.cp_i += 1
        n = self.act_every
        return self.copy('act' if (n > 0 and self.cp_i % n == 0) else 'dve', out, in_, reads, writes)

    def memset(self, eng, ap, val, writes):
        return self.P.op(eng, lambda e: e.memset(ap, val), writes=writes)

    def dma(self, out, in_, reads=(), writes=(), eng='sp', **kw):
        return self.P.dma(eng, out, in_, reads=reads, writes=writes, **kw)

    def tile(self, name, shape, dtype):
        return self.sb.alloc(name, shape, dtype), Buf(name)

    def ring(self, name, n, shape, dtype):
        return Ring([self.tile("%s%d" % (name, i), shape, dtype) for i in range(n)])


def dram_ap(t, offset, pattern):
    return bass.AP(t.tensor if hasattr(t, 'tensor') else t, offset, [list(p) for p in pattern])


def build_program(phases=('p0', 'ffn1_0', 'mix0', 'ffn2_0', 'ffn1_1', 'mix1', 'ffn2_1'), debug=False):
    nc = bass.Bass("TRN2", target_bir_lowering=False)
    k = K(nc)
    P = k.P
    sb = k.sb

    def din(name, shape):
        return nc.dram_tensor(name, list(shape), F32, kind="ExternalInput").ap()

    def dout(name, shape):
        return nc.dram_tensor(name, list(shape), F32, kind="ExternalOutput").ap()

    I = {}
    I['xp'] = din('xp', [NSEQ * SEQ, D])
    I['xs'] = din('xs', [NSEQ * TS, D])
    I['cache_a_k'] = din('cache_a_k', [NSEQ, 512, 512])
    I['cache_a_v'] = din('cache_a_v', [NSEQ, 512, 512])
    I['cache_b_k'] = din('cache_b_k', [NSEQ, PAST, 512])
    I['cache_b_v'] = din('cache_b_v', [NSEQ, PAST, 512])
    I['cache_c_kv'] = din('cache_c_kv', [NSEQ, PAST, C_KVL])
    I['cache_c_kr'] = din('cache_c_kr', [NSEQ, PAST, C_ROPE])
    I['t5_bias'] = din('t5_bias', [32, 4])
    for nm in ('ffn1_norm', 'mix_norm', 'ffn2_norm'):
        I[nm] = din(nm, [2, D])
    for nm in ('ffn1_w_gu', 'ffn2_w_gu'):
        I[nm] = din(nm, [2, D, 2 * DFF])
    for nm in ('ffn1_w_down', 'ffn2_w_down'):
        I[nm] = din(nm, [2, DFF, D])
    I['e_w_in'] = din('e_w_in', [D, 3072])
    I['a_rel_bias'] = din('a_rel_bias', [8, 129])
    for nm in ('b_lambda_q1', 'b_lambda_k1', 'b_lambda_q2', 'b_lambda_k2'):
        I[nm] = din(nm, [1, 64])
    I['b_subln'] = din('b_subln', [1, 128])
    I['e_w_out'] = din('e_w_out', [D, D])
    I['c_w_in'] = din('c_w_in', [D, 672])
    I['c_q_norm'] = din('c_q_norm', [1, C_QL])
    I['c_kv_norm'] = din('c_kv_norm', [1, C_KVL])
    I['c_w_q_up'] = din('c_w_q_up', [C_QL, 1536])
    I['c_w_kv_up'] = din('c_w_kv_up', [C_KVL, 2048])
    I['c_w_out'] = din('c_w_out', [D, D])
    I['final_norm'] = din('final_norm', [1, D])
    I['c_ident'] = din('c_ident', [128, 128])
    I['c_rope_cos'] = din('c_rope_cos', [96, SEQ + 32])
    I['c_rope_sin'] = din('c_rope_sin', [96, SEQ + 32])
    I['c_t5_onehot'] = din('c_t5_onehot', [32, LT])

    O = {}
    O['y_prompt'] = dout('y_prompt', [NSEQ * SEQ, D])
    O['y_sample'] = dout('y_sample', [NSEQ * TS, D])
    O['p_a_k'] = dout('p_a_k', [NSEQ, 512, 512])
    O['p_a_v'] = dout('p_a_v', [NSEQ, 512, 512])
    O['p_b_k'] = dout('p_b_k', [NSEQ, SEQ, 512])
    O['p_b_v'] = dout('p_b_v', [NSEQ, SEQ, 512])
    O['p_c_kv'] = dout('p_c_kv', [NSEQ, SEQ, C_KVL])
    O['p_c_kr'] = dout('p_c_kr', [NSEQ, SEQ, C_ROPE])
    O['s_a_k'] = dout('s_a_k', [NSEQ * TS, 512])
    O['s_a_v'] = dout('s_a_v', [NSEQ * TS, 512])
    O['s_b_k'] = dout('s_b_k', [NSEQ * TS, 512])
    O['s_b_v'] = dout('s_b_v', [NSEQ * TS, 512])
    O['s_c_kv'] = dout('s_c_kv', [NSEQ * TS, C_KVL])
    O['s_c_kr'] = dout('s_c_kr', [NSEQ * TS, C_ROPE])
    if debug:
        O['dbg_xT'] = dout('dbg_xT', [8, 128, NTOK])

    xT = nc.dram_tensor("xT_scr", [8, 128, NTOK], F32).ap()
    xT_pc = xT.rearrange("c p t -> p c t")
    B_xT = Buf('xT')
    RES = {'pc': xT_pc, 'B': B_xT, 'flat': xT}

    ident, B_ident = k.tile('ident', [128, 128], F32)
    k.dma(ident[:, :], I['c_ident'], writes=[B_ident])
    ones_bf, B_ones = k.tile('ones', [128, 128], BF16)
    k.memset('pool', ones_bf[:, :], 1.0, [B_ones])
    gains, B_gains = k.tile('gains', [128, 6, 8], F32)
    gi = 0
    for nm in ('ffn1_norm', 'mix_norm', 'ffn2_norm'):
        for l in range(2):
            k.dma(gains[:, gi, :], I[nm][l].rearrange("(c p) -> p c", p=128), writes=[B_gains],
                  allow_slow_non_contiguous=True)
            gi += 1
    G_IDX = {('ffn1', 0): 0, ('ffn1', 1): 1, ('mix', 0): 2, ('mix', 1): 3, ('ffn2', 0): 4, ('ffn2', 1): 5}

    psum = [(nc.alloc_psum_tensor("ps%d" % i, [128, 512], F32), Buf("ps%d" % i, x=True)) for i in range(8)]

    def phase0():
        m = sb.mark()
        xin = k.ring('p0in', 2, [128, 4, D], F32)
        xtr = k.ring('p0tr', 2, [128, 8, 512], F32)
        pr = Ring(psum[0:4])
        xp_v = I['xp'].rearrange("(g s p) d -> g p s d", s=4, p=128)
        for g in range(NSEQ * SEQ // 512):
            ti, Bi = xin.next()
            k.dma(ti[:, :, :], xp_v[g], writes=[Bi])
            to, Bo = xtr.next()
            for c in range(8):
                ps, Bp = pr.next()
                k.trs([(ps[:, s * 128:(s + 1) * 128], ti[:, s, c * 128:(c + 1) * 128], ident[:, :]) for s in range(4)],
                      reads=[Bi, B_ident], wbuf=Bp)
                k.copy_rr(to[:, c, :], ps[:, :], reads=[Bp], writes=[Bo])
            k.dma(xT_pc[:, :, g * 512:(g + 1) * 512], to[:, :, :], reads=[Bo], writes=[B_xT], eng='pool')
        ti, Bi = xin.next()
        k.dma(ti[0:32, 0, :], I['xs'], writes=[Bi])
        to, Bo = xtr.next()
        ps, Bp = pr.next()
        k.trs([(ps[:, c * 32:(c + 1) * 32], ti[0:32, 0, c * 128:(c + 1) * 128], ident[0:32, 0:32]) for c in range(8)],
              reads=[Bi, B_ident], wbuf=Bp)
        k.copy_rr(to[:, :, 0:32], ps[:, 0:256].rearrange("p (c t) -> p c t", c=8), reads=[Bp], writes=[Bo])
        k.dma(xT_pc[:, :, SOFF:SOFF + 32], to[:, :, 0:32], reads=[Bo], writes=[B_xT], eng='pool')
        sb.release(m)

    def rms_fm(x_t, Bx, nch, T, g_ap_fn, xn_t, Bxn, sq_ring, ps_ss, rs_t, Brs, dim, gbuf=None):
        pss, Bpss = ps_ss
        gbuf = B_gains if gbuf is None else gbuf
        for c in range(nch):
            sq, Bsq = sq_ring.next()
            k.act(sq[:, :T], x_t[:, c, :T], AF.Square, reads=[Bx], writes=[Bsq])
            k.mm(pss[:, :T], [(ones_bf[:, :], sq[:, :T])], reads=[B_ones, Bsq], wbuf=Bpss, start=(c == 0), stop=(c == nch - 1))
        k.rstd(rs_t[:, :T], pss[:, :T], [Bpss, B_eps], [Brs], 1.0 / dim, eps_t[:, 0:1])
        for c in range(nch):
            dst = xn_t(c) if callable(xn_t) else xn_t[:, c, :T]
            k.stt(dst, x_t[:, c, :T], g_ap_fn(c), rs_t[:, :T], ALU.mult, ALU.mult,
                  reads=[Bx, Brs, gbuf], writes=[Bxn])

    eps_t, B_eps = k.tile('eps', [128, 1], F32)
    k.memset('pool', eps_t[:, :], EPS, [B_eps])

    FT = 512

    def ffn_phase(which, layer):
        m = sb.mark()
        k.act_every = 2
        wgu_d = I[which + '_w_gu'][layer]
        wd_d = I[which + '_w_down'][layer]
        gidx = G_IDX[(which, layer)]
        wg, Bwg = k.tile('wg', [128, 8, 2 * DFF], BF16)
        wd, Bwd = k.tile('wd', [128, NJ, D], BF16)
        wgu_v = wgu_d.rearrange("(c p) n -> p c n", p=128)
        Bwg_g = [None] * NJ
        Bwg_u = [None] * NJ
        for j0 in range(0, NJ, 4):
            j1 = min(NJ, j0 + 4)
            bg, bu = Buf('wg_g'), Buf('wg_u')
            for j in range(j0, j1):
                Bwg_g[j], Bwg_u[j] = bg, bu
            k.dma(wg[:, :, j0 * 128:j1 * 128], wgu_v[:, :, j0 * 128:j1 * 128], writes=[bg], eng='pool')
            k.dma(wg[:, :, DFF + j0 * 128:DFF + j1 * 128], wgu_v[:, :, DFF + j0 * 128:DFF + j1 * 128],
                  writes=[bu], eng='pool')
        wd_v = wd_d.rearrange("(j p) n -> p j n", p=128)
        Bwd_blk = [Buf('wdblk') for _ in range(2)]
        k.dma(wd[:, 0:11, :], wd_v[:, 0:11, :], writes=[Bwd_blk[0]], eng='pool')
        k.dma(wd[:, 11:22, :], wd_v[:, 11:22, :], writes=[Bwd_blk[1]], eng='pool')

        xr = k.ring('fx', 2, [128, 8, FT], F32)
        xn, Bxn = k.tile('fxn', [128, 8, FT], BF16)
        sqr = k.ring('fsq', 3, [128, FT], BF16)
        rs, Brs = k.tile('frs', [128, FT], F32)
        actt, Bact = k.tile('fact', [128, NJ, FT], BF16)
        Bact_j = [Buf('actj') for _ in range(NJ)]
        sgr = k.ring('fsg', 2, [128, FT], F32)
        ps_g = Ring(psum[0:2])
        ps_u = Ring(psum[2:4])
        ps_y = Ring(psum[4:6])
        ps_ss = psum[6]
        tiles = [(i * FT, FT) for i in range(NTOK // FT)]
        if NTOK % FT:
            tiles.append((NTOK - NTOK % FT, NTOK % FT))
        ntile = len(tiles)

        def load(t):
            t0, T = tiles[t]
            tx, Bx = xr.next()
            k.dma(tx[:, :, :T], RES['pc'][:, :, t0:t0 + T], reads=[RES['B']], writes=[Bx])
            return tx, Bx

        def norm(t, tx, Bx):
            rms_fm(tx, Bx, 8, tiles[t][1], lambda c: gains[:, gidx, c:c + 1], xn, Bxn, sqr, ps_ss, rs, Brs, D)

        cur = load(0)
        norm(0, *cur)
        for t in range(ntile):
            t0, T = tiles[t]
            tx, Bx = cur
            nxt = load(t + 1) if t + 1 < ntile else None
            for j in range(NJ):
                pg, Bpg = ps_g.next()
                pu, Bpu = ps_u.next()
                k.mm(pg[:, :T], [(wg[:, c, j * 128:(j + 1) * 128], xn[:, c, :T]) for c in range(8)],
                     reads=[Bwg_g[j], Bxn], wbuf=Bpg)
                k.mm(pu[:, :T], [(wg[:, c, DFF + j * 128:DFF + (j + 1) * 128], xn[:, c, :T]) for c in range(8)],
                     reads=[Bwg_u[j], Bxn], wbuf=Bpu)
                sg, Bsg = sgr.next()
                k.act(sg[:, :T], pg[:, :T], AF.Silu, reads=[Bpg], writes=[Bsg])
                k.tt(actt[:, j, :T], pu[:, :T], sg[:, :T], ALU.mult, reads=[Bpu, Bsg], writes=[Bact_j[j]])
            if nxt is not None:
                norm(t + 1, *nxt)
            for c in range(8):
                py, Bpy = ps_y.next()
                k.mm(py[:, :T], [(wd[:, j, c * 128:(c + 1) * 128], actt[:, j, :T]) for j in range(NJ)],
                     reads=Bwd_blk + Bact_j, wbuf=Bpy)
                k.stt(tx[:, c, :T], py[:, :T], 0.5, tx[:, c, :T], ALU.mult, ALU.add, reads=[Bpy, Bx], writes=[Bx])
            k.dma(RES['pc'][:, :, t0:t0 + T], tx[:, :, :T], reads=[Bx], writes=[RES['B']], eng='pool')
            cur = nxt
        sb.release(m)

    def phaseZ():
        m = sb.mark()
        gfin, Bgf = k.tile('gfin', [128, D], F32)
        k.dma(gfin[:, :], dram_ap(I['final_norm'], 0, [[0, 128], [1, D]]), writes=[Bgf])
        xin = k.ring('zin', 2, [128, 8, 512], F32)
        xo = k.ring('zo', 3, [128, D], F32)
        junk, Bjunk = k.tile('zjunk', [128, D], BF16)
        st = k.ring('zst', 4, [128, 2], F32)
        pr = Ring(psum[0:6])

        def do_sub(ti, Bi, col0, ntok, dst_ap):
            to, Bo = xo.next()
            for half in range(2):
                ps, Bp = pr.next()
                k.trs([(ps[0:ntok, cc * 128:(cc + 1) * 128], ti[:, half * 4 + cc, col0:col0 + ntok], ident[:, :])
                       for cc in range(4)], reads=[Bi, B_ident], wbuf=Bp)
                k.copy_rr(to[0:ntok, half * 512:(half + 1) * 512], ps[0:ntok, :], reads=[Bp], writes=[Bo])
            s, Bs = st.next()
            k.act(junk[0:ntok, :], to[0:ntok, :], AF.Square, reads=[Bo], writes=[Bjunk, Bs], accum_out=s[0:ntok, 0:1])
            k.act(s[0:ntok, 1:2], s[0:ntok, 0:1], AF.Sqrt, reads=[Bs, B_eps], writes=[Bs], scale=1.0 / D, bias=eps_t[0:ntok, 0:1])
            k.recip_dve(s[0:ntok, 1:2], s[0:ntok, 1:2], reads=[Bs], writes=[Bs])
            k.stt(to[0:ntok, :], to[0:ntok, :], s[0:ntok, 1:2], gfin[0:ntok, :], ALU.mult, ALU.mult,
                  reads=[Bo, Bs, Bgf], writes=[Bo])
            k.dma(dst_ap, to[0:ntok, :], reads=[Bo], eng='pool')

        for g in range(NSEQ * SEQ // 512):
            ti, Bi = xin.next()
            k.dma(ti[:, :, :], RES['pc'][:, :, g * 512:(g + 1) * 512], reads=[RES['B']], writes=[Bi])
            for s_ in range(4):
                r0 = g * 512 + s_ * 128
                do_sub(ti, Bi, s_ * 128, 128, O['y_prompt'][r0:r0 + 128, :])
        ti, Bi = xin.next()
        k.dma(ti[:, :, 0:32], RES['pc'][:, :, SOFF:SOFF + 32], reads=[RES['B']], writes=[Bi])
        do_sub(ti, Bi, 0, 32, O['y_sample'][:, :])
        sb.release(m)

    oTA_scr = nc.dram_tensor("oTA_scr", [128, 4, NTOK], BF16).ap()
    B_oTA = Buf('oTA')
    toe_scr = nc.dram_tensor("toe_scr", [12, 130 * LT], F32).ap()
    B_toe = Buf('toe')
    CAUSAL_PAT = [[64, 8], [0, 64]]
    BAND_PAT = [[-64, 8], [0, 64]]

    def toe_write(v_t, Bv, nh, row0):
        src = v_t[0:nh, :].unsqueeze(1).broadcast_to([nh, 130, LT])
        dst = toe_scr[row0:row0 + nh, :].rearrange("h (r l) -> h r l", l=LT)
        k.dma(dst, src, reads=[Bv], writes=[B_toe])

    def toe_read(nh, row0, masters):
        for h in range(nh):
            T_, BT = masters[h]
            k.dma(T_[:, :], dram_ap(toe_scr, (row0 + h) * 130 * LT, [[LT - 1, 128], [1, WM]]), reads=[B_toe], writes=[BT])

    def toe_prep():
        m = sb.mark()
        arel8, Ba8 = k.tile('arel8', [8, 129], F32)
        k.dma(arel8[:, :], I['a_rel_bias'], writes=[Ba8])
        va, Bva = k.tile('va', [8, LT], F32)
        k.memset('dve', va[:, :], 0.0, [Bva])
        k.ts(va[:, 0:320], va[:, 0:320], arel8[:, 0:1], None, ALU.add, ALU.bypass, reads=[Bva, Ba8], writes=[Bva])
        k.copy('dve', va[:, 320:449], arel8[:, :], reads=[Ba8], writes=[Bva])
        k.ts(va[:, 449:1024], va[:, 449:1024], arel8[:, 128:129], None, ALU.add, ALU.bypass, reads=[Bva, Ba8], writes=[Bva])
        k.ts(va[:, 1024:LT], va[:, 1024:LT], arel8[:, 0:1], None, ALU.add, ALU.bypass, reads=[Bva, Ba8], writes=[Bva])
        toe_write(va, Bva, 8, 0)
        t5s, Bt5s = k.tile('t5s', [32, 4], F32)
        k.dma(t5s[:, :], I['t5_bias'], writes=[Bt5s])
        oneh, Boneh = k.tile('oneh', [32, LT], F32)
        k.dma(oneh[:, :], I['c_t5_onehot'], writes=[Boneh])
        vb, Bvb = k.tile('vb', [4, LT], F32)
        for i in range(3):
            ps, Bp = psum[4 + i]
            k.mm(ps[0:4, 0:384], [(t5s[:, :], oneh[:, i * 384:(i + 1) * 384])], reads=[Bt5s, Boneh], wbuf=Bp)
            k.copy('dve', vb[:, i * 384:(i + 1) * 384], ps[0:4, 0:384], reads=[Bp], writes=[Bvb])
        toe_write(vb, Bvb, 4, 8)
        sb.release(m)

    def mixer_res(nS):
        r = {}
        r['x'] = k.ring('mx', 2, [128, 8, 512], F32)
        r['h'] = k.ring('mh', 2, [128, 8, 512], BF16)
        r['sq'] = k.ring('msq', 3, [128, 512], BF16)
        r['rs'] = k.tile('mrs', [128, 512], F32)
        r['ss'] = psum[7]
        r['proj'] = Ring([psum[7], psum[0], psum[1], psum[2], psum[3]])
        r['S'] = Ring(psum[0:4])
        r['E'] = k.ring('mE', 8, [128, 512], BF16)
        r['Sp'] = k.ring('mSp', 3, [128, 512], F32)
        r['hl'] = k.ring('mhl', 4, [128, 512], BF16)
        r['rd'] = k.ring('mrd', 2, [128, 512], F32)
        r['stg'] = k.ring('mstg', 3, [128, 512], F32)
        return r

    def load_norm(mx, t0, T, gidx):
        tx, Bx = mx['x'].next()
        k.dma(tx[:, :, :T], xT_pc[:, :, t0:t0 + T], reads=[B_xT], writes=[Bx])
        hT, BhT = mx['h'].next()
        rms_fm(tx, Bx, 8, T, lambda c: gains[:, gidx, c:c + 1], hT, BhT, mx['sq'], mx['ss'], mx['rs'][0], mx['rs'][1], D)
        return tx, Bx, hT, BhT

    def proj_fm(mx, w, Bw, col0, hT, BhT, T, dst_ap, Bdst, mrows=128):
        ps, Bp = mx['proj'].next()
        k.mm(ps[:mrows, :T], [(w[:, c, col0:col0 + mrows], hT[:, c, :T]) for c in range(8)], reads=[Bw, BhT], wbuf=Bp)
        k.copy_rr(dst_ap, ps[:mrows, :T], reads=[Bp], writes=[Bdst])

    def proj_tm(mx, w, Bw, col0, ncol, hT, BhT, tcol0, ntok):
        ps, Bp = mx['proj'].next()
        k.mm(ps[:ntok, :ncol], [(hT[:, c, tcol0:tcol0 + ntok], w[:, c, col0:col0 + ncol]) for c in range(8)],
             reads=[Bw, BhT], wbuf=Bp)
        return ps, Bp

    def out_rows(mx, ps, Bp, ntok, ncol, dst_ap):
        stg, Bst = mx['stg'].next()
        k.copy_rr(stg[:ntok, :ncol], ps[:ntok, :ncol], reads=[Bp], writes=[Bst])
        k.dma(dst_ap, stg[:ntok, :ncol], reads=[Bst], eng='pool')

    def attn_run(mx, jobs, nq, scale, la=2, pair=True):
        steps = []
        gsz = 2 if pair else 1
        for j0 in range(0, len(jobs), gsz):
            grp = jobs[j0:j0 + gsz]
            nb = len(grp[0]['blocks'])
            for g_ in grp:
                assert len(g_['blocks']) == nb
            for bi in range(nb):
                steps.append([(g_, bi, nb, g_['blocks'][bi]) for g_ in grp])
        n = len(steps)
        st = {}

        def stA(t):
            ents = steps[t]
            S = [mx['S'].next() for _ in ents]
            items = []
            reads = []
            nqk = max(len(e_[3]['qk']) for e_ in ents)
            for qi in range(nqk):
                for ei, e_ in enumerate(ents):
                    b = e_[3]
                    if qi < len(b['qk']):
                        l, r = b['qk'][qi]
                        c0, c1 = b.get('cols', (0, nq))
                        items.append((S[ei][0][:b['nk'], :c1 - c0], l, r[:, c0:c1], qi == 0, qi == len(b['qk']) - 1))
            for e_ in ents:
                reads += e_[3]['rd']
            k.mm_multi(items, reads, [s_[1] for s_ in S])
            st[t] = S

        def stB(t):
            ents = steps[t]
            S = st[t]
            Es = []
            for ei, e_ in enumerate(ents):
                b = e_[3]
                nk = b['nk']
                c0, c1 = b.get('cols', (0, nq))
                n_ = c1 - c0
                ps, Bs = S[ei]
                E, BE = mx['E'].next()
                if b['bias'][0] == 't':
                    sp, Bsp = mx['Sp'].next()
                    k.stt(sp[:nk, :n_], ps[:nk, :n_], scale, b['bias'][1], ALU.mult, ALU.add,
                          reads=[Bs, b['bias'][2]], writes=[Bsp])
                    k.act(E[:nk, :n_], sp[:nk, :n_], AF.Exp, reads=[Bsp], writes=[BE])
                else:
                    k.act(E[:nk, :n_], ps[:nk, :n_], AF.Exp, reads=[Bs] + b['bias'][2], writes=[BE],
                          scale=scale, bias=b['bias'][1])
                for (pat, op, base, cm) in b['masks']:
                    def fsel(e, E=E, pat=pat, op=op, base=base, cm=cm, nk=nk, n_=n_):
                        return e.affine_select(out=E[:nk, :n_], in_=E[:nk, :n_], pattern=pat, compare_op=op, fill=0.0,
                                               base=base, channel_multiplier=cm)
                    P.op('pool', fsel, reads=[BE], writes=[BE])
                Es.append((E, BE))
            st[t] = Es

        def stC(t):
            ents = steps[t]
            Es = st.pop(t)
            for ei, (j, bi, nb, b) in enumerate(ents):
                nk = b['nk']
                c0, c1 = b.get('cols', (0, nq))
                E, BE = Es[ei]
                O_, BO = j['O']
                M = j['M']
                k.mm(O_[:M, c0:c1], [(b['v'], E[:nk, :c1 - c0])], reads=[b['Bv'], BE], wbuf=BO, start=(bi == 0), stop=(bi == nb - 1))
                if j.get('D') is not None:
                    Dap, BD = j['D']
                    k.mm(Dap[:, c0:c1], [(ones_bf[:nk, :], E[:nk, :c1 - c0])], reads=[B_ones, BE], wbuf=BD,
                         start=(bi == 0), stop=(bi == nb - 1))
            for ei, (j, bi, nb, b) in enumerate(ents):
                if bi == nb - 1:
                    j['epi']()

        for t in range(n + la):
            if t < n:
                stA(t)
            if 0 <= t - 1 < n:
                stB(t - 1)
            if 0 <= t - la < n:
                stC(t - la)

    def den_recip(mx, bc_ps, den_ap, Bden, p0, Mout, nq):
        hi, Bhi = mx['hl'].next()
        lo, Blo = mx['hl'].next()
        k.copy('dve', hi[p0:p0 + 1, :nq], den_ap, reads=[Bden], writes=[Bhi])
        k.tt(lo[p0:p0 + 1, :nq], den_ap, hi[p0:p0 + 1, :nq], ALU.subtract, reads=[Bden, Bhi], writes=[Blo])
        bc, Bbc = bc_ps
        k.mm(bc[0:Mout, :nq], [(ones_bf[p0:p0 + 1, 0:Mout], hi[p0:p0 + 1, :nq]), (ones_bf[p0:p0 + 1, 0:Mout], lo[p0:p0 + 1, :nq])],
             reads=[B_ones, Bhi, Blo], wbuf=Bbc)
        rd, Brd = mx['rd'].next()
        k.recip(rd[0:Mout, :nq], bc[0:Mout, :nq], reads=[Bbc], writes=[Brd])
        return rd, Brd

    def epilogue_std(mx, bc_ps, O_, BO, nq, dst_ap, Bdst):
        rd, Brd = den_recip(mx, bc_ps, O_[64:65, :nq], BO, 64, 64, nq)
        k.tt(dst_ap, O_[0:64, :nq], rd[0:64, :nq], ALU.mult, reads=[BO, Brd], writes=[Bdst])

    def load_cache_T(mx, cache_ap, ntok, dstT, BdstT, nchunk=4, width=128):
        for tt in range(ntok // 128):
            stg, Bst = mx['stg'].next()
            k.dma(stg[:, :nchunk * width], cache_ap[tt * 128:(tt + 1) * 128, :], writes=[Bst])
            ps, Bp = mx['proj'].next()
            k.trs([(ps[:width, cc * 128:(cc + 1) * 128], stg[:, cc * width:(cc + 1) * width], ident[:, :]) for cc in range(nchunk)],
                  reads=[Bst, B_ident], wbuf=Bp)
            k.copy_rr(dstT[:width, 0:nchunk, tt * 128:(tt + 1) * 128],
                      ps[:width, 0:nchunk * 128].rearrange("p (c t) -> p c t", c=nchunk), reads=[Bp], writes=[BdstT])

    def mixA():
        m = sb.mark()
        k.act_every = 3
        gidx = G_IDX[('mix', 0)]
        scale = A_D ** -0.5
        win, Bwin = k.tile('winA', [128, 8, 1536], BF16)
        Bw3 = [Buf('winA%d' % i) for i in range(3)]
        wv = I['e_w_in'].rearrange("(c p) n -> p c n", p=128)
        for i in (1, 2, 0):
            k.dma(win[:, :, i * 512:(i + 1) * 512], wv[:, :, i * 512:(i + 1) * 512], writes=[Bw3[i]], eng='pool')
        Bq, Bk_, Bv_ = Bw3
        arel_bc, Barel = k.tile('arelbc', [128, 8 * 129], F32)
        k.dma(arel_bc[:, :], dram_ap(I['a_rel_bias'], 0, [[0, 128], [1, 8 * 129]]), writes=[Barel])
        TA = [k.tile('TA%d' % h, [128, WM], F32) for h in range(8)]
        toe_read(8, 0, TA)
        KT, BKT = k.tile('KTA', [128, 4, SEQ], BF16)
        QTr = k.ring('QTAp', 2, [128, 8, 512], BF16)
        for (QT_, BQT_) in QTr.items:
            k.memset('pool', QT_[:, :, :], 0.0, [BQT_])
        KTn, BKTn = k.tile('KTAn', [128, 4, 32], BF16)
        VA, BVA = k.tile('VA', [128, 16, 8, 65], BF16)
        k.memset('pool', VA[:, :, :, 64:65], 1.0, [BVA])
        oT, BoT = k.tile('oTA', [128, 4, 512], BF16)
        mx = mixer_res(3)
        mx['S'] = Ring(psum[0:3])
        Oring = Ring(psum[3:6])
        bc_ps = psum[6]

        def heads(QT, BQT, nq, qcol0, blocks_fn, ocol0):
            jobs = []
            for h in range(8):
                pb, oc = (h % 2) * 64, h // 2
                qap = QT[:, h, qcol0:qcol0 + nq]
                Oo = Oring.next()
                jobs.append(dict(blocks=blocks_fn(h, pb, oc, qap, BQT), O=Oo, M=65, D=None,
                                 epi=(lambda Oo=Oo, pb=pb, oc=oc: epilogue_std(mx, bc_ps, Oo[0], Oo[1], nq,
                                                                               oT[pb:pb + 64, oc, ocol0:ocol0 + nq], BoT))))
            attn_run(mx, jobs, nq, scale, la=4, pair=False)

        def q_projA(QT, BQT, hT, BhT, T):
            for oc in range(4):
                ps, Bp = mx['proj'].next()
                k.mm(ps[:, :T], [(win[:, c, oc * 128:(oc + 1) * 128], hT[:, c, :T]) for c in range(8)], reads=[Bq, BhT], wbuf=Bp)
                k.copy_rr(QT[0:64, 2 * oc, :T], ps[0:64, :T], reads=[Bp], writes=[BQT])
                k.copy_rr(QT[64:128, 2 * oc + 1, :T], ps[64:128, :T], reads=[Bp], writes=[BQT])

        def bias_for(h, d, nk, nq):
            if -384 <= d <= 128:
                return ('t', TA[h][0][:nk, d + 384:d + 384 + nq], TA[h][1])
            return ('c', arel_bc[:nk, h * 129 + 128:h * 129 + 129], [Barel])

        DBG = os.environ.get('MIXDBG', '')
        for s in range(NSEQ):
            if DBG == 'setup' or (DBG in ('pass1', 'pass2', 'pass2h1') and s > 0):
                break
            tb = s * SEQ
            nxt = load_norm(mx, tb, 512, gidx)
            for tt in range(4):
                tx, Bx, hT, BhT = nxt
                if tt < 3:
                    nxt = load_norm(mx, tb + (tt + 1) * 512, 512, gidx)
                for oc in range(4):
                    proj_fm(mx, win, Bk_, 512 + oc * 128, hT, BhT, 512, KT[:, oc, tt * 512:(tt + 1) * 512], BKT)
                for su in range(4):
                    ps, Bp = proj_tm(mx, win, Bv_, 1024, 512, hT, BhT, su * 128, 128)
                    k.copy_rr(VA[:, tt * 4 + su, :, 0:64], ps[:, 0:512].rearrange("p (h d) -> p h d", h=8), reads=[Bp], writes=[BVA])
                    if tt == 3:
                        out_rows(mx, ps, Bp, 128, 512, O['p_a_v'][s, su * 128:(su + 1) * 128, :])
                        ps, Bp = proj_tm(mx, win, Bk_, 512, 512, hT, BhT, su * 128, 128)
                        out_rows(mx, ps, Bp, 128, 512, O['p_a_k'][s, su * 128:(su + 1) * 128, :])
            def prologue(qt, tb=tb):
                tx, Bx, hT, BhT = load_norm(mx, tb + qt * 512, 512, gidx)
                QT, BQT = QTr.next()
                q_projA(QT, BQT, hT, BhT, 512)
                return QT, BQT
            nxtq = prologue(0)
            for qt in range(4):
                q0 = qt * 512
                QT, BQT = nxtq
                if qt < 3:
                    nxtq = prologue(qt + 1)

                def blocks_fn(h, pb, oc, qap, BQT, q0=q0):
                    bl = []
                    k0s = list(range(max(0, q0 - 512), q0 + 512, 128))
                    k0s.sort(key=lambda k0_: (k0_ != q0))
                    for k0 in k0s:
                        d = q0 - k0
                        c0, c1 = 0, 512
                        if d < 0:
                            c0 = -d
                        if d >= 256:
                            c1 = 640 - d
                        n_ = c1 - c0
                        masks = []
                        if d <= 0:
                            masks.append(([[64, n_ // 64], [0, 64]], ALU.is_gt, d + 64 + c0, -1))
                        if d >= 128:
                            masks.append(([[-64, n_ // 64], [0, 64]], ALU.is_ge, 512 - d, 1))
                        if -384 <= d <= 128:
                            bias = ('t', TA[h][0][:128, d + 384 + c0:d + 384 + c1], TA[h][1])
                        else:
                            bias = bias_for(h, d, 128, 512)
                        bl.append(dict(nk=128, qk=[(KT[:, oc, k0:k0 + 128], qap)], rd=[BKT, BQT], cols=(c0, c1),
                                       v=VA[:, k0 // 128, h, :], Bv=BVA, bias=bias, masks=masks))
                    return bl
                heads(QT, BQT, 512, 0, blocks_fn, 0)
                k.dma(oTA_scr[:, :, tb + q0:tb + q0 + 512], oT[:, :, :], reads=[BoT], writes=[B_oTA], eng='pool')
        tx, Bx, hT, BhT = load_norm(mx, SOFF, 32, gidx)
        QT, BQT = QTr.next()
        q_projA(QT, BQT, hT, BhT, 32)
        for oc in range(4):
            proj_fm(mx, win, Bk_, 512 + oc * 128, hT, BhT, 32, KTn[:, oc, :], BKTn)
        for s in range(NSEQ):
            if DBG in ('setup', 'pass1', 'pass2', 'pass2h1'):
                break
            ps, Bp = proj_tm(mx, win, Bk_, 512, 512, hT, BhT, s * TS, TS)
            out_rows(mx, ps, Bp, TS, 512, O['s_a_k'][s * TS:(s + 1) * TS, :])
            ps, Bp = proj_tm(mx, win, Bv_, 1024, 512, hT, BhT, s * TS, TS)
            out_rows(mx, ps, Bp, TS, 512, O['s_a_v'][s * TS:(s + 1) * TS, :])
            k.copy_rr(VA[0:TS, 4, :, 0:64], ps[0:TS, 0:512].rearrange("p (h d) -> p h d", h=8), reads=[Bp], writes=[BVA])
            load_cache_T(mx, I['cache_a_k'][s], 512, KT, BKT)
            k.copy('dve', KT[:, :, 512:512 + TS], KTn[:, :, s * TS:(s + 1) * TS], reads=[BKTn], writes=[BKT])
            for kt in range(4):
                stg, Bst = mx['stg'].next()
                k.dma(stg[:, :], I['cache_a_v'][s, kt * 128:(kt + 1) * 128, :], writes=[Bst])
                k.copy_rr(VA[:, kt, :, 0:64], stg[:, :].rearrange("p (h d) -> p h d", h=8), reads=[Bst], writes=[BVA])

            def blocks_fn(h, pb, oc, qap, BQT):
                bl = []
                for kt in range(4):
                    d = 512 - 128 * kt
                    bl.append(dict(nk=128, qk=[(KT[:, oc, kt * 128:(kt + 1) * 128], qap)], rd=[BKT, BQT],
                                   v=VA[:, kt, h, :], Bv=BVA, bias=bias_for(h, d, 128, TS), masks=[]))
                bl.append(dict(nk=TS, qk=[(KT[:, oc, 512:512 + TS], qap)], rd=[BKT, BQT],
                               v=VA[0:TS, 4, h, :], Bv=BVA, bias=bias_for(h, 0, TS, TS), masks=[]))
                return bl
            heads(QT, BQT, TS, s * TS, blocks_fn, s * TS)
        k.dma(oTA_scr[:, :, SOFF:SOFF + 32], oT[:, :, 0:32], reads=[BoT], writes=[B_oTA], eng='pool')
        sb.release(m)

    def mixB():
        m = sb.mark()
        k.act_every = 3
        gidx = G_IDX[('mix', 0)]
        scale = B_D ** -0.5
        lam_init = 0.8 - 0.6 * math.exp(-0.3 * 0)
        win, Bwin = k.tile('winB', [128, 8, 1536], BF16)
        Bw3 = [Buf('winB%d' % i) for i in range(3)]
        wv = I['e_w_in'].rearrange("(c p) n -> p c n", p=128)
        for i in (1, 2, 0):
            k.dma(win[:, :, i * 512:(i + 1) * 512], wv[:, :, 1536 + i * 512:1536 + (i + 1) * 512], writes=[Bw3[i]], eng='pool')
        Bq, Bk_, Bv_ = Bw3
        wout, Bwout = k.tile('woutE', [128, 8, D], BF16)
        k.dma(wout[:, :, :], I['e_w_out'].rearrange("(c p) n -> p c n", p=128), writes=[Bwout], eng='pool')
        lv, Blv = k.tile('lamv', [128, 4, 64], F32)
        for i, nm in enumerate(('b_lambda_q1', 'b_lambda_k1', 'b_lambda_q2', 'b_lambda_k2')):
            k.dma(lv[:, i, :], dram_ap(I[nm], 0, [[0, 128], [1, 64]]), writes=[Blv])
        lsc, Blsc = k.tile('lamsc', [128, 8], F32)
        ljunk, Bljunk = k.tile('lamjunk', [128, 64], F32)
        for i in range(2):
            P.op('dve', (lambda e, i=i: e.scalar_tensor_tensor(out=ljunk[:, :], in0=lv[:, 2 * i, :], scalar=1.0, in1=lv[:, 2 * i + 1, :],
                                                                op0=ALU.mult, op1=ALU.mult, accum_out=lsc[:, i:i + 1])),
                 reads=[Blv], writes=[Bljunk, Blsc])
        k.act(lsc[:, 2:4], lsc[:, 0:2], AF.Exp, reads=[Blsc], writes=[Blsc])
        k.tt(lsc[:, 4:5], lsc[:, 3:4], lsc[:, 2:3], ALU.subtract, reads=[Blsc], writes=[Blsc])
        k.ts(lsc[:, 5:6], lsc[:, 4:5], -lam_init, None, ALU.add, ALU.bypass, reads=[Blsc], writes=[Blsc])
        nlam = lsc[:, 5:6]
        gsub, Bgsub = k.tile('gsub', [128, 2], F32)
        k.dma(gsub[:, 0:1], I['b_subln'].rearrange("o (p u) -> (o p) u", u=1), writes=[Bgsub])
        k.ts(gsub[:, 1:2], gsub[:, 0:1], 1.0 - lam_init, None, ALU.mult, ALU.bypass, reads=[Bgsub], writes=[Bgsub])
        t5_bc, Bt5bc = k.tile('t5bc', [128, 128], F32)
        k.dma(t5_bc[:, :], dram_ap(I['t5_bias'], 0, [[0, 128], [1, 128]]), writes=[Bt5bc])
        TB = [k.tile('TB%d' % h, [128, WM], F32) for h in range(4)]
        toe_read(4, 8, TB)
        KT, BKT = k.tile('KTB', [128, 4, SEQ], BF16)
        QTr = k.ring('QTBp', 2, [128, 8, 512], BF16)
        for (QT_, BQT_) in QTr.items:
            k.memset('pool', QT_[:, :, :], 0.0, [BQT_])
        KTn, BKTn = k.tile('KTBn', [128, 4, 32], BF16)
        VB, BVB = k.tile('VB', [128, 16, 512], BF16)
        oT, BoT = k.tile('oTB', [128, 8, 512], BF16)
        on_r = k.ring('mOn', 2, [128, 512], F32)
        ob_t, Bob = k.tile('mob', [128, 512], F32)
        mx = mixer_res(2)
        mx['S'] = Ring(psum[0:3])
        mx['proj'] = Ring([psum[7], psum[0], psum[1], psum[2]])
        Ops = [psum[3], psum[4]]
        Dps = [psum[5], psum[6]]
        bc_ps = psum[7]

        def q_proj(hT, BhT, T):
            QT, BQT = QTr.next()
            q_proj2(QT, BQT, hT, BhT, T)
            return QT, BQT

        def q_proj2(QT, BQT, hT, BhT, T):
            for h in range(4):
                ps, Bp = mx['proj'].next()
                k.mm(ps[:, :T], [(win[:, c, h * 128:(h + 1) * 128], hT[:, c, :T]) for c in range(8)], reads=[Bq, BhT], wbuf=Bp)
                k.copy_rr(QT[0:64, 2 * h, :T], ps[0:64, :T], reads=[Bp], writes=[BQT])
                k.copy_rr(QT[64:128, 2 * h + 1, :T], ps[64:128, :T], reads=[Bp], writes=[BQT])

        def heads(QT, BQT, nq, qcol0, blocks_fn, ocol0):
            jobs = []
            for h in range(4):
                Ons = [on_r.next(), on_r.next()]

                def epi_map(mp_, Ons=Ons):
                    O_, BO = Ops[mp_]
                    rd, Brd = mx['rd'].next()
                    k.recip(rd[:, :nq], Dps[mp_][0][:, :nq], reads=[Dps[mp_][1]], writes=[Brd])
                    On, BOn = Ons[mp_]
                    k.tt(On[:, :nq], O_[:, :nq], rd[:, :nq], ALU.mult, reads=[BO, Brd], writes=[BOn])

                def epi1(h=h, Ons=Ons, epi_map=epi_map):
                    epi_map(1)
                    k.stt(ob_t[:, :nq], Ons[1][0][:, :nq], nlam, Ons[0][0][:, :nq], ALU.mult, ALU.add,
                          reads=[Ons[0][1], Ons[1][1], Blsc], writes=[Bob])
                    sq, Bsq = mx['sq'].next()
                    k.act(sq[:, :nq], ob_t[:, :nq], AF.Square, reads=[Bob], writes=[Bsq])
                    bc, Bbc = bc_ps
                    k.mm(bc[:, :nq], [(ones_bf[:, :], sq[:, :nq])], reads=[B_ones, Bsq], wbuf=Bbc)
                    rs, Brs = mx['rd'].next()
                    k.rstd(rs[:, :nq], bc[:, :nq], [Bbc, B_eps], [Brs], 1.0 / 128, eps_t[:, 0:1])
                    k.stt(oT[:, 4 + h, ocol0:ocol0 + nq], ob_t[:, :nq], gsub[:, 1:2], rs[:, :nq], ALU.mult, ALU.mult,
                          reads=[Bob, Brs, Bgsub], writes=[BoT])

                for mp_ in range(2):
                    qap = QT[:, 2 * h + mp_, qcol0:qcol0 + nq]
                    jobs.append(dict(blocks=blocks_fn(h, mp_, qap, BQT), O=Ops[mp_], M=128, D=Dps[mp_],
                                     epi=((lambda epi_map=epi_map: epi_map(0)) if mp_ == 0 else epi1)))
            attn_run(mx, jobs, nq, scale, la=4, pair=False)

        def bias_for(h, dk, nk, nq):
            if dk >= -128:
                return ('t', TB[h][0][:nk, 384 - dk:384 - dk + nq], TB[h][1])
            return ('c', t5_bc[:nk, 15 * 4 + h:15 * 4 + h + 1], [Bt5bc])

        def out_proj(tx, Bx, T, t0):
            for c in range(8):
                ps, Bp = mx['proj'].next()
                k.mm(ps[:, :T], [(wout[:, kc, c * 128:(c + 1) * 128], oT[:, kc, :T]) for kc in range(8)], reads=[Bwout, BoT], wbuf=Bp)
                k.stt(tx[:, c, :T], ps[:, :T], 1.0, tx[:, c, :T], ALU.mult, ALU.add, reads=[Bp, Bx], writes=[Bx])
            k.dma(xT_pc[:, :, t0:t0 + T], tx[:, :, :T], reads=[Bx], writes=[B_xT], eng='pool')

        for s in range(NSEQ):
            tb = s * SEQ
            nxt = load_norm(mx, tb, 512, gidx)
            for tt in range(4):
                tx, Bx, hT, BhT = nxt
                if tt < 3:
                    nxt = load_norm(mx, tb + (tt + 1) * 512, 512, gidx)
                for oc in range(4):
                    proj_fm(mx, win, Bk_, 512 + oc * 128, hT, BhT, 512, KT[:, oc, tt * 512:(tt + 1) * 512], BKT)
                for su in range(4):
                    r0 = tt * 512 + su * 128
                    ps, Bp = proj_tm(mx, win, Bv_, 1024, 512, hT, BhT, su * 128, 128)
                    k.copy_rr(VB[:, tt * 4 + su, :], ps[:, 0:512], reads=[Bp], writes=[BVB])
                    out_rows(mx, ps, Bp, 128, 512, O['p_b_v'][s, r0:r0 + 128, :])
                    ps, Bp = proj_tm(mx, win, Bk_, 512, 512, hT, BhT, su * 128, 128)
                    out_rows(mx, ps, Bp, 128, 512, O['p_b_k'][s, r0:r0 + 128, :])
            def prologue(qt, tb=tb):
                tx, Bx, hT, BhT = load_norm(mx, tb + qt * 512, 512, gidx)
                QT, BQT = q_proj(hT, BhT, 512)
                return tx, Bx, QT, BQT
            nxtq = prologue(0)
            for qt in range(4):
                q0 = qt * 512
                tx, Bx, QT, BQT = nxtq
                if qt < 3:
                    nxtq = prologue(qt + 1)
                k.dma(oT[:, 0:4, :], oTA_scr[:, :, tb + q0:tb + q0 + 512], reads=[B_oTA], writes=[BoT])

                def blocks_fn(h, mp_, qap, BQT, q0=q0):
                    bl = []
                    for k0 in range(0, q0 + 512, 128):
                        dk = k0 - q0
                        c0 = max(0, dk)
                        n_ = 512 - c0
                        masks = [([[64, n_ // 64], [0, 64]], ALU.is_gt, 64, -1)] if dk >= 0 else []
                        if dk >= -128:
                            bias = ('t', TB[h][0][:128, 384 - dk + c0:384 - dk + 512], TB[h][1])
                        else:
                            bias = bias_for(h, dk, 128, 512)
                        bl.append(dict(nk=128, qk=[(KT[:, h, k0:k0 + 128], qap)], rd=[BKT, BQT], cols=(c0, 512),
                                       v=VB[:, k0 // 128, h * 128:(h + 1) * 128], Bv=BVB, bias=bias, masks=masks))
                    return bl
                heads(QT, BQT, 512, 0, blocks_fn, 0)
                out_proj(tx, Bx, 512, tb + q0)
        tx, Bx, hT, BhT = load_norm(mx, SOFF, 32, gidx)
        QT, BQT = q_proj(hT, BhT, 32)
        for oc in range(4):
            proj_fm(mx, win, Bk_, 512 + oc * 128, hT, BhT, 32, KTn[:, oc, :], BKTn)
        k.dma(oT[:, 0:4, 0:32], oTA_scr[:, :, SOFF:SOFF + 32], reads=[B_oTA], writes=[BoT])
        for s in range(NSEQ):
            ps, Bp = proj_tm(mx, win, Bk_, 512, 512, hT, BhT, s * TS, TS)
            out_rows(mx, ps, Bp, TS, 512, O['s_b_k'][s * TS:(s + 1) * TS, :])
            ps, Bp = proj_tm(mx, win, Bv_, 1024, 512, hT, BhT, s * TS, TS)
            out_rows(mx, ps, Bp, TS, 512, O['s_b_v'][s * TS:(s + 1) * TS, :])
            k.copy_rr(VB[0:TS, 8, :], ps[0:TS, 0:512], reads=[Bp], writes=[BVB])
            load_cache_T(mx, I['cache_b_k'][s], PAST, KT, BKT)
            k.copy('dve', KT[:, :, PAST:PAST + TS], KTn[:, :, s * TS:(s + 1) * TS], reads=[BKTn], writes=[BKT])
            for kt in range(8):
                stg, Bst = mx['stg'].next()
                k.dma(stg[:, :], I['cache_b_v'][s, kt * 128:(kt + 1) * 128, :], writes=[Bst])
                k.copy_rr(VB[:, kt, :], stg[:, :], reads=[Bst], writes=[BVB])

            def blocks_fn(h, mp_, qap, BQT):
                bl = []
                for kt in range(8):
                    dk = 128 * kt - PAST
                    bl.append(dict(nk=128, qk=[(KT[:, h, kt * 128:(kt + 1) * 128], qap)], rd=[BKT, BQT],
                                   v=VB[:, kt, h * 128:(h + 1) * 128], Bv=BVB, bias=bias_for(h, dk, 128, TS), masks=[]))
                bl.append(dict(nk=TS, qk=[(KT[:, h, PAST:PAST + TS], qap)], rd=[BKT, BQT],
                               v=VB[0:TS, 8, h * 128:(h + 1) * 128], Bv=BVB, bias=bias_for(h, 0, TS, TS), masks=[]))
                return bl
            heads(QT, BQT, TS, s * TS, blocks_fn, s * TS)
        out_proj(tx, Bx, 32, SOFF)
        sb.release(m)

    def mixC():
        m = sb.mark()
        k.act_every = 3
        gidx = G_IDX[('mix', 1)]
        scale = (C_NOPE + C_ROPE) ** -0.5
        win, Bwin = k.tile('winC', [128, 8, 672], BF16)
        k.dma(win[:, :, :], I['c_w_in'].rearrange("(c p) n -> p c n", p=128), writes=[Bwin], eng='pool')
        wkr3, Bwkr3 = k.tile('wkr3', [128, 8, 96], BF16)
        wkr3r, Bwkr3r = k.tile('wkr3r', [128, 8, 96], BF16)
        for g in range(3):
            k.copy('dve', wkr3[:, :, g * 32:(g + 1) * 32], win[:, :, 640:672], reads=[Bwin], writes=[Bwkr3])
            k.ts(wkr3r[:, :, g * 32:g * 32 + 16], win[:, :, 656:672], -1.0, None, ALU.mult, ALU.bypass, reads=[Bwin], writes=[Bwkr3r])
            k.copy('dve', wkr3r[:, :, g * 32 + 16:g * 32 + 32], win[:, :, 640:656], reads=[Bwin], writes=[Bwkr3r])
        wq_v = I['c_w_q_up'].rearrange("(k p) (h e) -> p k h e", p=128, e=96)
        wqn, Bwqn = k.tile('wqn', [128, 3, 16, 64], BF16)
        wqr, Bwqr = k.tile('wqr', [128, 3, 16, 32], BF16)
        wqrr, Bwqrr = k.tile('wqrr', [128, 3, 16, 32], BF16)
        for k3 in range(3):
            k.dma(wqn[:, k3, :, :], wq_v[:, k3, :, 0:64], writes=[Bwqn], eng='pool')
            k.dma(wqr[:, k3, :, :], wq_v[:, k3, :, 64:96], writes=[Bwqr], eng='pool')
        for k3 in range(3):
            k.ts(wqrr[:, k3, :, 0:16], wqr[:, k3, :, 16:32], -1.0, None, ALU.mult, ALU.bypass, reads=[Bwqr], writes=[Bwqrr])
            k.copy('dve', wqrr[:, k3, :, 16:32], wqr[:, k3, :, 0:16], reads=[Bwqr], writes=[Bwqrr])
        wqn2 = wqn[:, :, :, :].rearrange("p k h e -> p k (h e)")
        wqr2 = wqr[:, :, :, :].rearrange("p k h e -> p k (h e)")
        wqrr2 = wqrr[:, :, :, :].rearrange("p k h e -> p k (h e)")
        wkv_v = I['c_w_kv_up'].rearrange("(k p) (h e) -> p k h e", p=128, e=128)
        wkn, Bwkn = k.tile('wkn', [128, 2, 16, 64], BF16)
        wvv, Bwvv = k.tile('wvv', [128, 2, 16, 64], BF16)
        for k2 in range(2):
            k.dma(wkn[:, k2, :, :], wkv_v[:, k2, :, 0:64], writes=[Bwkn], eng='pool')
            k.dma(wvv[:, k2, :, :], wkv_v[:, k2, :, 64:128], writes=[Bwvv], eng='pool')
        wkn2 = wkn[:, :, :, :].rearrange("p k h e -> p k (h e)")
        wvv2 = wvv[:, :, :, :].rearrange("p k h e -> p k (h e)")
        wout, Bwout = k.tile('woutC', [128, 8, D], BF16)
        k.dma(wout[:, :, :], I['c_w_out'].rearrange("(c p) n -> p c n", p=128), writes=[Bwout], eng='pool')
        gq, Bgq = k.tile('gq', [128, 5], F32)
        k.dma(gq[:, 0:3], I['c_q_norm'].rearrange("o (c p) -> (o p) c", p=128), writes=[Bgq], allow_slow_non_contiguous=True)
        k.dma(gq[:, 3:5], I['c_kv_norm'].rearrange("o (c p) -> (o p) c", p=128), writes=[Bgq], allow_slow_non_contiguous=True)
        knT, BknT = k.tile('knT', [128, 8, SEQ], BF16)
        krr3, Bkrr3 = k.tile('krrp', [128, 3, SEQ], BF16)
        k.memset('pool', krr3[:, :, :], 0.0, [Bkrr3])
        xT2 = nc.dram_tensor("xT2_scr", [8, 128, NTOK], F32).ap()
        xT2_pc = xT2.rearrange("c p t -> p c t")
        B_xT2 = Buf('xT2')

        def kr_write(c0, n, src, Bsrc):
            for g_ in range(3):
                k.copy_rr(krr3[g_ * 32:(g_ + 1) * 32, g_, c0:c0 + n], src[g_ * 32:(g_ + 1) * 32, :n], reads=[Bsrc], writes=[Bkrr3])
        VC, BVC = k.tile('VC', [128, 16, 16, 65], BF16)
        k.memset('pool', VC[:, :, :, 64:65], 1.0, [BVC])
        cqn, Bcqn = k.tile('cqn', [128, 3, SEQ], BF16)
        S_ring = Ring(psum[0:3])
        Oring = Ring(psum[3:6])
        bc_ps = psum[6]
        proj = Ring([psum[7], psum[0], psum[1], psum[2], psum[3]])
        ss_ps = psum[7]

        def work1(T, with_cache=False):
            w = {}
            w['x'] = k.tile('c1x', [128, 8, T], F32)
            w['h'] = k.ring('c1h', 2, [128, 8, T], BF16)
            w['sq'] = k.ring('c1sq', 3, [128, T], BF16)
            w['rs'] = k.tile('c1rs', [128, T], F32)
            w['cqf'] = k.tile('c1cqf', [128, 3, T], F32)
            w['ckvf'] = k.tile('c1ckvf', [128, 2, T], F32)
            w['ckvn'] = w['ckvf']
            w['ckvb'] = k.tile('c1ckvb', [128, 2, T], BF16)
            w['t1'] = k.tile('c1t1', [96, T], F32)
            w['krrf'] = k.tile('c1krrf', [96, T], F32)
            w['cos'] = k.ring('c1cos', 2, [96, T], F32)
            w['sin'] = k.ring('c1sin', 2, [96, T], F32)
            w['stg'] = k.ring('c1stg', 2, [128, 512], F32)
            return w

        def work2(T):
            w = {}
            w['xc'] = k.ring('c2xc', 3, [128, T], F32)
            w['cos'] = k.tile('c2cos', [96, T], F32)
            w['sin'] = k.tile('c2sin', [96, T], F32)
            w['qnT'] = k.tile('c2qnp', [128, 16, T], BF16)
            w['qrT'] = k.tile('c2qr', [128, 6, T], BF16)
            k.memset('pool', w['qnT'][0][:, :, :], 0.0, [w['qnT'][1]])
            k.memset('pool', w['qrT'][0][:, :, :], 0.0, [w['qrT'][1]])
            w['oT'] = k.tile('c2oT', [128, 8, T], BF16)
            w['t1'] = k.tile('c2t1', [96, T], F32)
            w['t2'] = k.tile('c2t2', [96, T], F32)
            w['E'] = k.ring('c2E', 6, [128, T], BF16)
            w['hl'] = k.ring('c2hl', 2, [128, T], BF16)
            w['rd'] = k.ring('c2rd', 2, [128, T], F32)
            w['S'] = S_ring
            return w

        def p1_load(w, t0, T, pcol):
            tx, Bx = w['x']
            k.dma(tx[:, :, :T], xT_pc[:, :, t0:t0 + T], reads=[B_xT], writes=[Bx])
            cs, Bcs = w['cos'].next()
            sn, Bsn = w['sin'].next()
            k.dma(cs[:, :T], I['c_rope_cos'][:, pcol:pcol + T], writes=[Bcs])
            k.dma(sn[:, :T], I['c_rope_sin'][:, pcol:pcol + T], writes=[Bsn])
            hT, BhT = w['h'].next()
            rms_fm(tx, Bx, 8, T, lambda c: gains[:, gidx, c:c + 1], hT, BhT, w['sq'], ss_ps, w['rs'][0], w['rs'][1], D)
            return hT, BhT, cs, Bcs, sn, Bsn

        def pass1_tile(w, ld, T, tcol, kr_dst, kv_out, kr_out):
            hT, BhT, cs, Bcs, sn, Bsn = ld
            cqf, Bcqf = w['cqf']
            ckvf, Bckvf = w['ckvf']
            for kq in range(3):
                ps, Bp = proj.next()
                k.mm(ps[:, :T], [(win[:, c, kq * 128:(kq + 1) * 128], hT[:, c, :T]) for c in range(8)], reads=[Bwin, BhT], wbuf=Bp)
                k.copy_rr(cqf[:, kq, :T], ps[:, :T], reads=[Bp], writes=[Bcqf])
            for k2 in range(2):
                ps, Bp = proj.next()
                k.mm(ps[:, :T], [(win[:, c, 384 + k2 * 128:384 + (k2 + 1) * 128], hT[:, c, :T]) for c in range(8)], reads=[Bwin, BhT], wbuf=Bp)
                k.copy_rr(ckvf[:, k2, :T], ps[:, :T], reads=[Bp], writes=[Bckvf])
            t1, Bt1 = w['t1']
            krrf, Bkrrf = w['krrf']
            ps, Bp = proj.next()
            k.mm(ps[0:96, :T], [(wkr3[:, c, :], hT[:, c, :T]) for c in range(8)], reads=[Bwkr3, BhT], wbuf=Bp)
            k.tt(t1[:, :T], ps[0:96, :T], cs[:, :T], ALU.mult, reads=[Bp, Bcs], writes=[Bt1])
            ps, Bp = proj.next()
            k.mm(ps[0:96, :T], [(wkr3r[:, c, :], hT[:, c, :T]) for c in range(8)], reads=[Bwkr3r, BhT], wbuf=Bp)
            k.tt(krrf[:, :T], ps[0:96, :T], sn[:, :T], ALU.mult, reads=[Bp, Bsn], writes=[Bkrrf])
            k.tt(krrf[:, :T], krrf[:, :T], t1[:, :T], ALU.add, reads=[Bkrrf, Bt1], writes=[Bkrrf])
            kr_dst(krrf, Bkrrf)
            rms_fm(cqf, Bcqf, 3, T, lambda c: gq[:, c:c + 1], (lambda c: cqn[:, c, tcol:tcol + T]), Bcqn, w['sq'], ss_ps,
                   w['rs'][0], w['rs'][1], C_QL, gbuf=Bgq)
            ckvn, Bckvn = w['ckvn']
            ckvb, Bckvb = w['ckvb']
            rms_fm(ckvf, Bckvf, 2, T, lambda c: gq[:, 3 + c:4 + c], ckvn, Bckvn, w['sq'], ss_ps, w['rs'][0], w['rs'][1], C_KVL, gbuf=Bgq)
            for k2 in range(2):
                k.copy_rr(ckvb[:, k2, :T], ckvn[:, k2, :T], reads=[Bckvn], writes=[Bckvb])
            for su in range((T + 127) // 128):
                nt = min(128, T - su * 128)
                ps, Bp = proj.next()
                items = [(ps[0:nt, k2 * 128:(k2 + 1) * 128], ckvn[:, k2, su * 128:su * 128 + nt], ident[:, :]) for k2 in range(2)]
                items.append((ps[0:nt, 256:288], krrf[0:32, su * 128:su * 128 + nt], ident[0:32, 0:32]))
                k.trs(items, reads=[Bckvn, Bkrrf, B_ident], wbuf=Bp)
                stg, Bst = w['stg'].next()
                k.copy_rr(stg[0:nt, 0:288], ps[0:nt, 0:288], reads=[Bp], writes=[Bst])
                k.dma(kv_out(su * 128, nt), stg[0:nt, 0:256], reads=[Bst], eng='pool')
                k.dma(kr_out(su * 128, nt), stg[0:nt, 256:288], reads=[Bst], eng='pool')
            return ckvb, Bckvb

        def expand_kv(src_fn, Bsrc, ncols, tcol):
            for c0 in range(0, ncols, 512):
                n = min(512, ncols - c0)
                for p in range(8):
                    ps, Bp = proj.next()
                    k.mm(ps[:, :n], [(wkn2[:, k2, p * 128:(p + 1) * 128], src_fn(k2, c0, n)) for k2 in range(2)], reads=[Bwkn, Bsrc], wbuf=Bp)
                    k.copy_rr(knT[:, p, tcol + c0:tcol + c0 + n], ps[:, :n], reads=[Bp], writes=[BknT])
            for c0 in range(0, ncols, 128):
                n = min(128, ncols - c0)
                kt = (tcol + c0) // 128
                for half in range(2):
                    ps, Bp = proj.next()
                    k.mm(ps[0:n, :], [(src_fn(k2, c0, n), wvv2[:, k2, half * 512:(half + 1) * 512]) for k2 in range(2)], reads=[Bwvv, Bsrc], wbuf=Bp)
                    k.copy_rr(VC[0:n, kt, half * 8:(half + 1) * 8, 0:64], ps[0:n, :].rearrange("p (h d) -> p h d", h=8), reads=[Bp], writes=[BVC])

        def q_proj(w, T, qcol, pcol):
            cs, Bcs = w['cos']
            sn, Bsn = w['sin']
            k.dma(cs[:, :T], I['c_rope_cos'][:, pcol:pcol + T], writes=[Bcs])
            k.dma(sn[:, :T], I['c_rope_sin'][:, pcol:pcol + T], writes=[Bsn])
            qnT, BqnT = w['qnT']
            qrT, BqrT = w['qrT']
            for p in range(8):
                ps, Bp = proj.next()
                k.mm(ps[:, :T], [(wqn2[:, k3, p * 128:(p + 1) * 128], cqn[:, k3, qcol:qcol + T]) for k3 in range(3)], reads=[Bwqn, Bcqn], wbuf=Bp)
                k.copy_rr(qnT[0:64, 2 * p, :T], ps[0:64, :T], reads=[Bp], writes=[BqnT])
                k.copy_rr(qnT[64:128, 2 * p + 1, :T], ps[64:128, :T], reads=[Bp], writes=[BqnT])
            t1, Bt1 = w['t1']
            t2, Bt2 = w['t2']
            for ch in range(6):
                nr = 96 if ch < 5 else 32
                ps, Bp = proj.next()
                k.mm(ps[0:nr, :T], [(wqr2[:, k3, ch * 96:ch * 96 + nr], cqn[:, k3, qcol:qcol + T]) for k3 in range(3)], reads=[Bwqr, Bcqn], wbuf=Bp)
                k.tt(t1[0:nr, :T], ps[0:nr, :T], cs[0:nr, :T], ALU.mult, reads=[Bp, Bcs], writes=[Bt1])
                ps, Bp = proj.next()
                k.mm(ps[0:nr, :T], [(wqrr2[:, k3, ch * 96:ch * 96 + nr], cqn[:, k3, qcol:qcol + T]) for k3 in range(3)], reads=[Bwqrr, Bcqn], wbuf=Bp)
                k.tt(t2[0:nr, :T], ps[0:nr, :T], sn[0:nr, :T], ALU.mult, reads=[Bp, Bsn], writes=[Bt2])
                k.tt(qrT[0:nr, ch, :T], t1[0:nr, :T], t2[0:nr, :T], ALU.add, reads=[Bt1, Bt2], writes=[BqrT])

        def heads(w, nq, qc0, blocks_fn):
            qnT, BqnT = w['qnT']
            qrT, BqrT = w['qrT']
            oT, BoT = w['oT']
            jobs = []
            for h in range(16):
                pb, p = (h % 2) * 64, h // 2
                g, ch = h % 3, h // 3
                qa = qnT[:, h, qc0:qc0 + nq]
                qb = qrT[:, ch, qc0:qc0 + nq]
                Oo = Oring.next()
                jobs.append(dict(blocks=blocks_fn(h, pb, p, g, qa, qb, [BknT, Bkrr3, BqnT, BqrT]), O=Oo, M=65, D=None,
                                 epi=(lambda Oo=Oo, pb=pb, p=p: epilogue_std(w, bc_ps, Oo[0], Oo[1], nq,
                                                                             oT[pb:pb + 64, p, qc0:qc0 + nq], BoT))))
            attn_run(w, jobs, nq, scale, la=4, pair=False)

        def out_proj(w, T, t0, load_x=True):
            oT, BoT = w['oT']
            for c in range(8):
                xc, Bxc = w['xc'].next()
                k.dma(xc[:, :T], xT_pc[:, c, t0:t0 + T], reads=[B_xT], writes=[Bxc])
                ps, Bp = proj.next()
                k.mm(ps[:, :T], [(wout[:, kc, c * 128:(c + 1) * 128], oT[:, kc, :T]) for kc in range(8)], reads=[Bwout, BoT], wbuf=Bp)
                k.stt(xc[:, :T], ps[:, :T], 1.0, xc[:, :T], ALU.mult, ALU.add, reads=[Bp, Bxc], writes=[Bxc])
                k.dma(xT2_pc[:, c, t0:t0 + T], xc[:, :T], reads=[Bxc], writes=[B_xT2], eng='pool')

        def mk_block(nk, kcols, vkt, h, pb, p, g, qa, qb, rd, masks):
            return dict(nk=nk, qk=[(knT[:, p, kcols[0]:kcols[1]], qa), (krr3[:, g, kcols[0]:kcols[1]], qb)],
                        rd=rd, v=VC[0:nk, vkt, h, :], Bv=BVC, bias=('c', 0.0, []), masks=masks)

        for s in range(NSEQ):
            tb = s * SEQ
            m1 = sb.mark()
            w = work1(512)
            nxt = p1_load(w, tb, 512, 0)
            for tt in range(4):
                ld = nxt
                if tt < 3:
                    nxt = p1_load(w, tb + (tt + 1) * 512, 512, (tt + 1) * 512)
                ckvb, Bckvb = pass1_tile(w, ld, 512, tt * 512, (lambda kf, Bkf, tt=tt: kr_write(tt * 512, 512, kf, Bkf)),
                                         lambda r0, n, tt=tt: O['p_c_kv'][s, tt * 512 + r0:tt * 512 + r0 + n, :],
                                         lambda r0, n, tt=tt: O['p_c_kr'][s, tt * 512 + r0:tt * 512 + r0 + n, :])
                expand_kv(lambda k2, c0, n: ckvb[:, k2, c0:c0 + n], Bckvb, 512, tt * 512)
            sb.release(m1)
            w = work2(512)
            for qt in range(4):
                q0 = qt * 512
                q_proj(w, 512, q0, q0)

                def blocks_fn(h, pb, p, g, qa, qb, rd, q0=q0):
                    bl = []
                    for k0 in range(0, q0 + 512, 128):
                        dk = k0 - q0
                        c0 = max(0, dk)
                        n_ = 512 - c0
                        masks = [([[64, n_ // 64], [0, 64]], ALU.is_gt, 64, -1)] if dk >= 0 else []
                        blk = mk_block(128, (k0, k0 + 128), k0 // 128, h, pb, p, g, qa, qb, rd, masks)
                        blk['cols'] = (c0, 512)
                        bl.append(blk)
                    return bl
                heads(w, 512, 0, blocks_fn)
                out_proj(w, 512, tb + q0)
            sb.release(m1)
        m1 = sb.mark()
        w = work1(32)
        krn, Bkrn = k.tile('krn', [96, 32], BF16)
        ckT, BckT = k.tile('ckT', [128, 2, PAST + TS], BF16)
        w2 = work2(32)
        ckvb, Bckvb = pass1_tile(w, p1_load(w, SOFF, 32, SEQ), 32, 0, (lambda kf, Bkf: k.copy('act', krn[:, :], kf[:, :32], reads=[Bkf], writes=[Bkrn])),
                                 lambda r0, n: O['s_c_kv'][r0:r0 + n, :], lambda r0, n: O['s_c_kr'][r0:r0 + n, :])
        q_proj(w2, 32, 0, SEQ)
        mxs = dict(stg=w['stg'], proj=proj)
        for s in range(NSEQ):
            load_cache_T(mxs, I['cache_c_kv'][s], PAST, ckT, BckT, nchunk=2, width=128)
            k.copy('dve', ckT[:, :, PAST:PAST + TS], ckvb[:, :, s * TS:(s + 1) * TS], reads=[Bckvb], writes=[BckT])
            expand_kv(lambda k2, c0, n: ckT[:, k2, c0:c0 + n], BckT, PAST + TS, 0)
            for tt in range(PAST // 128):
                stg, Bst = w['stg'].next()
                k.dma(stg[:, 0:96].rearrange("p (g e) -> p g e", g=3),
                      dram_ap(I['cache_c_kr'], (s * PAST + tt * 128) * C_ROPE, [[C_ROPE, 128], [0, 3], [1, C_ROPE]]), writes=[Bst])
                ps, Bp = proj.next()
                k.tr(ps[0:96, 0:128], stg[:, 0:96], ident[:, :], reads=[Bst, B_ident], wbuf=Bp)
                kr_write(tt * 128, 128, ps[0:96, 0:128], Bp)
            kr_write(PAST, TS, krn[:, s * TS:(s + 1) * TS], Bkrn)

            def blocks_fn(h, pb, p, g, qa, qb, rd):
                bl = [mk_block(128, (kt * 128, (kt + 1) * 128), kt, h, pb, p, g, qa, qb, rd, []) for kt in range(PAST // 128)]
                bl.append(mk_block(TS, (PAST, PAST + TS), PAST // 128, h, pb, p, g, qa, qb, rd, []))
                return bl
            heads(w2, TS, s * TS, blocks_fn)
        out_proj(w2, 32, SOFF, load_x=False)
        RES['pc'], RES['B'] = xT2_pc, B_xT2
        RES['flat'] = xT2
        sb.release(m1)
        sb.release(m)

    if 'p0' in phases:
        toe_prep()
        phase0()
    if 'ffn1_0' in phases:
        ffn_phase('ffn1', 0)
    if 'mixA' in phases or 'mix0' in phases:
        mixA()
    if 'mixB' in phases or 'mix0' in phases:
        mixB()
    if 'ffn2_0' in phases:
        ffn_phase('ffn2', 0)
    if 'ffn1_1' in phases:
        ffn_phase('ffn1', 1)
    if 'mix1' in phases:
        mixC()
    if 'ffn2_1' in phases:
        ffn_phase('ffn2', 1)
    if debug:
        k.dma(O['dbg_xT'], RES['flat'], reads=[RES['B']])
    phaseZ()
    P.final_wait_all('sp')
    P.emit()
    return nc, sorted(I.keys()), sorted(O.keys()), sb.peak


def _t5_bucket_host(rel):
    import jax
    import jax.numpy as jnp
    with jax.default_device(jax.devices('cpu')[0]):
        rel = jnp.asarray(np.asarray(rel, dtype=np.int32))
        nb = 16
        max_exact = nb // 2
        ret = jnp.where(rel > 0, nb, 0)
        n = jnp.abs(rel)
        n_f = jnp.maximum(n, 1).astype(jnp.float32)
        large = max_exact + (jnp.log(n_f / max_exact) / math.log(128 / max_exact) * (nb - max_exact)).astype(jnp.int32)
        large = jnp.minimum(large, nb - 1)
        return np.asarray(ret + jnp.where(n < max_exact, n, large))


def host_constants():
    c = {}
    c['c_ident'] = np.eye(128, dtype=np.float32)
    half = C_ROPE // 2
    inv = (10000.0 ** (-np.arange(half, dtype=np.float32) / half)).astype(np.float32)
    pos = np.concatenate([np.arange(SEQ), PAST + np.arange(TS), PAST + np.arange(TS)]).astype(np.float32)
    ang = pos[None, :] * inv[(np.arange(96) % 32) % half][:, None]
    c['c_rope_cos'] = np.cos(ang.astype(np.float32)).astype(np.float32)
    c['c_rope_sin'] = np.sin(ang.astype(np.float32)).astype(np.float32)
    u = np.arange(LT)
    rel = np.where(u < WM, 384 - u, 384 + LT - u)
    bidx = np.asarray(_t5_bucket_host(rel))
    oh = np.zeros((32, LT), dtype=np.float32)
    oh[bidx, u] = 1.0
    c['c_t5_onehot'] = oh
    return c


_CACHE = {}


def kernel(**inputs):
    n = 8
    if 'nc' not in _CACHE:
        _CACHE['nc'] = build_program()
    nc, in_names, out_names, _ = _CACHE['nc']
    consts = host_constants()
    f = lambda a: np.ascontiguousarray(np.asarray(a, dtype=np.float32))
    in_maps = []
    for i in range(n):
        b0, b1 = i * NSEQ, (i + 1) * NSEQ
        mp = {}
        mp['xp'] = f(inputs['x_prompt'][b0:b1]).reshape(NSEQ * SEQ, D)
        mp['xs'] = f(inputs['x_sample'][b0:b1]).reshape(NSEQ * TS, D)
        mp['cache_a_k'] = f(inputs['cache_a_k'][0, b0:b1]).reshape(NSEQ, 512, 512)
        mp['cache_a_v'] = f(inputs['cache_a_v'][0, b0:b1]).reshape(NSEQ, 512, 512)
        mp['cache_b_k'] = f(inputs['cache_b_k'][0, b0:b1]).reshape(NSEQ, PAST, 512)
        mp['cache_b_v'] = f(inputs['cache_b_v'][0, b0:b1]).reshape(NSEQ, PAST, 512)
        mp['cache_c_kv'] = f(inputs['cache_c_kv'][0, b0:b1])
        mp['cache_c_kr'] = f(inputs['cache_c_kr'][0, b0:b1])
        for nm in ('t5_bias', 'ffn1_norm', 'mix_norm', 'ffn2_norm', 'ffn1_w_gu', 'ffn2_w_gu', 'ffn1_w_down', 'ffn2_w_down'):
            mp[nm] = f(inputs[nm])
        for nm in ('e_w_in', 'a_rel_bias', 'e_w_out', 'c_w_in', 'c_w_q_up', 'c_w_kv_up', 'c_w_out'):
            mp[nm] = f(inputs[nm][0])
        for nm in ('b_lambda_q1', 'b_lambda_k1', 'b_lambda_q2', 'b_lambda_k2', 'b_subln', 'c_q_norm', 'c_kv_norm'):
            mp[nm] = f(inputs[nm]).reshape(1, -1)
        mp['final_norm'] = f(inputs['final_norm']).reshape(1, D)
        mp.update(consts)
        in_maps.append({kk: mp[kk] for kk in in_names})
    res = run_bass_kernel_spmd(nc, in_maps, core_ids=list(range(n)))
    R = res.results
    B = 16

    def cat(name, shape):
        return np.concatenate([np.asarray(R[i][name], dtype=np.float32) for i in range(n)], axis=0).reshape(shape)
    y_prompt = cat('y_prompt', (B, SEQ, D))
    y_sample = cat('y_sample', (B, TS, D))
    p_a_k = cat('p_a_k', (1, B, 512, A_H, A_D))
    p_a_v = cat('p_a_v', (1, B, 512, A_H, A_D))
    p_b_k = cat('p_b_k', (1, B, SEQ, B_H, 2 * B_D))
    p_b_v = cat('p_b_v', (1, B, SEQ, B_H, 2 * B_D))
    p_c_kv = cat('p_c_kv', (1, B, SEQ, C_KVL))
    p_c_kr = cat('p_c_kr', (1, B, SEQ, C_ROPE))
    s_a_k = cat('s_a_k', (1, B, TS, A_H, A_D))
    s_a_v = cat('s_a_v', (1, B, TS, A_H, A_D))
    s_b_k = cat('s_b_k', (1, B, TS, B_H, 2 * B_D))
    s_b_v = cat('s_b_v', (1, B, TS, B_H, 2 * B_D))
    s_c_kv = cat('s_c_kv', (1, B, TS, C_KVL))
    s_c_kr = cat('s_c_kr', (1, B, TS, C_ROPE))
    return (y_prompt, y_sample, p_a_k, p_a_v, p_b_k, p_b_v, p_c_kv, p_c_kr,
            s_a_k, s_a_v, s_b_k, s_b_v, s_c_kv, s_c_kr)
```

```python
import math
import os
import numpy as np
import concourse.bass as bass
import concourse.mybir as mybir
from concourse.bass_utils import run_bass_kernel_spmd

F32 = mybir.dt.float32
BF16 = mybir.dt.bfloat16
AF = mybir.ActivationFunctionType
ALU = mybir.AluOpType

ENGS = ('pe', 'act', 'dve', 'pool', 'sp')

D = 1024
SEQ = 2048
NSEQ = 2
TS = 16
PAST = 1024
NTOK = NSEQ * SEQ + NSEQ * TS
SOFF = NSEQ * SEQ
DFF = 2816
NJ = DFF // 128
EPS = 1e-6
CHUNK = 64
A_H, A_D = 8, 64
B_H, B_D = 4, 64
C_H, C_NOPE, C_ROPE, C_V = 16, 64, 32, 64
C_QL, C_KVL = 384, 256
LT = 1152
WM = 1024


class Buf:
    __slots__ = ('name', 'w', 'r', 'x')

    def __init__(self, name='', x=False):
        self.name = name
        self.w = None
        self.r = {}
        self.x = x


class Prog:
    def __init__(self, nc, n_dma_sems=(('sp', 48), ('pool', 24))):
        self.nc = nc
        self.q = {e: [] for e in ENGS}
        self.cnt = {e: 0 for e in ENGS}
        self.sem = {e: nc.alloc_semaphore("s_" + e) for e in ENGS}
        self.seen = {e: {} for e in ENGS}
        self.dsem, self.dcnt, self.dpool, self.drr = {}, {}, {}, {}
        for e, n in n_dma_sems:
            ids = []
            for i in range(n):
                k = (e, i)
                self.dsem[k] = nc.alloc_semaphore("d_%s_%d" % (e, i))
                self.dcnt[k] = 0
                ids.append(k)
            self.dpool[e] = ids
            self.drr[e] = 0
        self.nops = 0
        self.hist = {}

    def _need(self, eng, ev, waits):
        if ev is None:
            return
        kind, key, val = ev
        if kind == 'e' and key == eng:
            if eng == 'pe':
                return
            if val < self.cnt[eng] - 1:
                return
        k = (kind, key)
        if self.seen[eng].get(k, 0) >= val:
            return
        self.seen[eng][k] = val
        waits.append(ev)
        snap = self.hist.get(ev)
        if snap is not None:
            mine = self.seen[eng]
            for k2, v2 in snap.items():
                if mine.get(k2, 0) < v2:
                    mine[k2] = v2

    def _deps(self, eng, reads, writes):
        waits = []
        for b in reads:
            self._need(eng, b.w, waits)
            if b.x:
                for ev in b.r.values():
                    self._need(eng, ev, waits)
        for b in writes:
            self._need(eng, b.w, waits)
            for ev in b.r.values():
                self._need(eng, ev, waits)
        return waits

    def _mark(self, ev, rkey, reads, writes):
        for b in reads:
            b.r[rkey] = ev
        for b in writes:
            b.w = ev
            b.r = {}

    def op(self, eng, fn, reads=(), writes=()):
        waits = self._deps(eng, reads, writes)
        self.cnt[eng] += 1
        ev = ('e', eng, self.cnt[eng])
        snap = dict(self.seen[eng])
        snap[('e', eng)] = self.cnt[eng] - 1
        self.hist[ev] = snap
        self.q[eng].append((waits, fn, (self.sem[eng], 1)))
        self._mark(ev, ('e', eng), reads, writes)
        self.nops += 1
        return ev

    def dma(self, eng, out, in_, reads=(), writes=(), **kw):
        pool = self.dpool[eng]
        k = pool[self.drr[eng] % len(pool)]
        self.drr[eng] += 1
        waits = self._deps(eng, reads, writes)
        if self.dcnt[k] > 0:
            self._need(eng, ('d', k, self.dcnt[k]), waits)
        self.dcnt[k] += 16
        ev = ('d', k, self.dcnt[k])
        snap = dict(self.seen[eng])
        snap.pop(('e', eng), None)
        self.hist[ev] = snap

        def fn(e, out=out, in_=in_, kw=kw):
            return e.dma_start(out=out, in_=in_, **kw)
        self.q[eng].append((waits, fn, (self.dsem[k], 16)))
        self._mark(ev, ('d', k), reads, writes)
        self.nops += 1
        return ev

    def barrier(self):
        evs = [('e', e, self.cnt[e]) for e in ENGS if self.cnt[e] > 0]
        evs += [('d', kk, v) for kk, v in self.dcnt.items() if v > 0]
        for eng in ENGS:
            waits = []
            for ev in evs:
                if ev[0] == 'e' and ev[1] == eng:
                    continue
                self._need(eng, ev, waits)
            if waits:
                self.q[eng].append((waits, None, None))

    def _sem_of(self, ev):
        kind, key, val = ev
        return (self.sem[key] if kind == 'e' else self.dsem[key]), val

    def final_wait_all(self, eng='sp'):
        waits = []
        for e in ENGS:
            if e != eng and self.cnt[e] > 0:
                waits.append(('e', e, self.cnt[e]))
        for k, v in self.dcnt.items():
            if v > 0:
                waits.append(('d', k, v))
        self.q[eng].append((waits, None, None))

    def emit(self):
        nc = self.nc
        prog = self

        def run(e, name):
            for waits, fn, inc in prog.q[name]:
                for ev in waits:
                    s, v = prog._sem_of(ev)
                    e.wait_ge(s, v)
                if fn is not None:
                    ins = fn(e)
                    ins.then_inc(inc[0], inc[1])

        with nc.Block() as block:
            @block.tensor
            def _(e):
                run(e, 'pe')

            @block.scalar
            def _(e):
                run(e, 'act')

            @block.vector
            def _(e):
                run(e, 'dve')

            @block.gpsimd
            def _(e):
                run(e, 'pool')

            @block.sync
            def _(e):
                run(e, 'sp')


class SB:
    def __init__(self, nc, base=16512, limit=229344):
        self.nc = nc
        self.off = base
        self.limit = limit
        self.n = 0
        self.peak = 0
        self.on_release = None

    def alloc(self, name, shape, dtype):
        esz = 2 if dtype == BF16 else 4
        free = int(np.prod(shape[1:])) * esz
        self.off = (self.off + 63) // 64 * 64
        self.n += 1
        t = self.nc.alloc_sbuf_tensor_at("%s_%d" % (name, self.n), list(shape), dtype, offset=self.off)
        self.off += free
        self.peak = max(self.peak, self.off)
        assert self.off <= self.limit, ("SBUF overflow", name, self.off, self.limit)
        return t

    def mark(self):
        return self.off

    def release(self, m):
        self.off = m
        if self.on_release is not None:
            self.on_release()


class Ring:
    def __init__(self, items):
        self.items = items
        self.i = 0

    def next(self):
        it = self.items[self.i % len(self.items)]
        self.i += 1
        return it


class K:
    def __init__(self, nc):
        self.nc = nc
        self.P = Prog(nc)
        self.sb = SB(nc)
        self.sb.on_release = self.P.barrier
        self.cp_i = 0
        self.act_every = 2

    def mm(self, out, pairs, reads, wbuf, start=True, stop=True):
        def fn(e, pairs=list(pairs), out=out):
            n = len(pairs)
            ins = None
            for i, (l, r) in enumerate(pairs):
                ins = e.matmul(out, l, r, start=(start and i == 0), stop=(stop and i == n - 1))
            return ins
        return self.P.op('pe', fn, reads=reads, writes=[wbuf])

    def mm_multi(self, items, reads, wbufs):
        def fn(e, items=list(items)):
            ins = None
            for (o, l, r, st_, sp_) in items:
                ins = e.matmul(o, l, r, start=st_, stop=sp_)
            return ins
        return self.P.op('pe', fn, reads=reads, writes=wbufs)

    def tr(self, out, in_, ident, reads, wbuf):
        return self.P.op('pe', lambda e: e.transpose(out, in_, ident), reads=reads, writes=[wbuf])

    def trs(self, items, reads, wbuf):
        def fn(e, items=list(items)):
            ins = None
            for (o, i, idn) in items:
                ins = e.transpose(o, i, idn)
            return ins
        return self.P.op('pe', fn, reads=reads, writes=[wbuf])

    def act(self, out, in_, func, reads, writes, scale=1.0, bias=0.0, accum_out=None):
        def fn(e):
            if accum_out is not None:
                return e.activation(out=out, in_=in_, func=func, bias=bias, scale=scale, accum_out=accum_out)
            return e.activation(out=out, in_=in_, func=func, bias=bias, scale=scale)
        return self.P.op('act', fn, reads=reads, writes=writes)

    def stt(self, out, in0, scalar, in1, op0, op1, reads, writes):
        return self.P.op('dve', lambda e: e.scalar_tensor_tensor(out=out, in0=in0, scalar=scalar, in1=in1, op0=op0, op1=op1),
                         reads=reads, writes=writes)

    def ts(self, out, in0, s1, s2, op0, op1, reads, writes, eng='dve'):
        return self.P.op(eng, lambda e: e.tensor_scalar(out=out, in0=in0, scalar1=s1, scalar2=s2, op0=op0, op1=op1),
                         reads=reads, writes=writes)

    def tt(self, out, in0, in1, op, reads, writes, eng='dve'):
        return self.P.op(eng, lambda e: e.tensor_tensor(out=out, in0=in0, in1=in1, op=op), reads=reads, writes=writes)

    def recip(self, out, in_, reads, writes):
        self.act(out, in_, AF.Ln, reads, writes)
        return self.act(out, out, AF.Exp, list(writes), writes, scale=-1.0)

    def rstd(self, out, in_, reads, writes, scale, eps_ap):
        self.act(out, in_, AF.Ln, reads, writes, scale=scale, bias=eps_ap)
        return self.act(out, out, AF.Exp, list(writes), writes, scale=-0.5)

    def recip_dve(self, out, in_, reads, writes):
        return self.P.op('dve', lambda e: e.reciprocal(out=out, in_=in_), reads=reads, writes=writes)

    def copy(self, eng, out, in_, reads, writes):
        if eng == 'act':
            return self.act(out, in_, AF.Copy, reads, writes)
        return self.P.op(eng, lambda e: e.tensor_copy(out=out, in_=in_), reads=reads, writes=writes)

    def copy_rr(self, out, in_, reads, writes):
        self.cp_i += 1
        n = self.act_every
        return self.copy('act' if (n > 0 and self.cp_i % n == 0) else 'dve', out, in_, reads, writes)

    def memset(self, eng, ap, val, writes):
        return self.P.op(eng, lambda e: e.memset(ap, val), writes=writes)

    def dma(self, out, in_, reads=(), writes=(), eng='sp', **kw):
        return self.P.dma(eng, out, in_, reads=reads, writes=writes, **kw)

    def tile(self, name, shape, dtype):
        return self.sb.alloc(name, shape, dtype), Buf(name)

    def ring(self, name, n, shape, dtype):
        return Ring([self.tile("%s%d" % (name, i), shape, dtype) for i in range(n)])


def dram_ap(t, offset, pattern):
    return bass.AP(t.tensor if hasattr(t, 'tensor') else t, offset, [list(p) for p in pattern])


def build_program(phases=('p0', 'ffn1_0', 'mix0', 'ffn2_0', 'ffn1_1', 'mix1', 'ffn2_1'), debug=False):
    nc = bass.Bass("TRN2", target_bir_lowering=False)
    k = K(nc)
    P = k.P
    sb = k.sb

    def din(name, shape):
        return nc.dram_tensor(name, list(shape), F32, kind="ExternalInput").ap()

    def dout(name, shape):
        return nc.dram_tensor(name, list(shape), F32, kind="ExternalOutput").ap()

    I = {}
    I['xp'] = din('xp', [NSEQ * SEQ, D])
    I['xs'] = din('xs', [NSEQ * TS, D])
    I['cache_a_k'] = din('cache_a_k', [NSEQ, 512, 512])
    I['cache_a_v'] = din('cache_a_v', [NSEQ, 512, 512])
    I['cache_b_k'] = din('cache_b_k', [NSEQ, PAST, 512])
    I['cache_b_v'] = din('cache_b_v', [NSEQ, PAST, 512])
    I['cache_c_kv'] = din('cache_c_kv', [NSEQ, PAST, C_KVL])
    I['cache_c_kr'] = din('cache_c_kr', [NSEQ, PAST, C_ROPE])
    I['t5_bias'] = din('t5_bias', [32, 4])
    for nm in ('ffn1_norm', 'mix_norm', 'ffn2_norm'):
        I[nm] = din(nm, [2, D])
    for nm in ('ffn1_w_gu', 'ffn2_w_gu'):
        I[nm] = din(nm, [2, D, 2 * DFF])
    for nm in ('ffn1_w_down', 'ffn2_w_down'):
        I[nm] = din(nm, [2, DFF, D])
    I['e_w_in'] = din('e_w_in', [D, 3072])
    I['a_rel_bias'] = din('a_rel_bias', [8, 129])
    for nm in ('b_lambda_q1', 'b_lambda_k1', 'b_lambda_q2', 'b_lambda_k2'):
        I[nm] = din(nm, [1, 64])
    I['b_subln'] = din('b_subln', [1, 128])
    I['e_w_out'] = din('e_w_out', [D, D])
    I['c_w_in'] = din('c_w_in', [D, 672])
    I['c_q_norm'] = din('c_q_norm', [1, C_QL])
    I['c_kv_norm'] = din('c_kv_norm', [1, C_KVL])
    I['c_w_q_up'] = din('c_w_q_up', [C_QL, 1536])
    I['c_w_kv_up'] = din('c_w_kv_up', [C_KVL, 2048])
    I['c_w_out'] = din('c_w_out', [D, D])
    I['final_norm'] = din('final_norm', [1, D])
    I['c_ident'] = din('c_ident', [128, 128])
    I['c_rope_cos'] = din('c_rope_cos', [96, SEQ + 32])
    I['c_rope_sin'] = din('c_rope_sin', [96, SEQ + 32])
    I['c_t5_onehot'] = din('c_t5_onehot', [32, LT])

    O = {}
    O['y_prompt'] = dout('y_prompt', [NSEQ * SEQ, D])
    O['y_sample'] = dout('y_sample', [NSEQ * TS, D])
    O['p_a_k'] = dout('p_a_k', [NSEQ, 512, 512])
    O['p_a_v'] = dout('p_a_v', [NSEQ, 512, 512])
    O['p_b_k'] = dout('p_b_k', [NSEQ, SEQ, 512])
    O['p_b_v'] = dout('p_b_v', [NSEQ, SEQ, 512])
    O['p_c_kv'] = dout('p_c_kv', [NSEQ, SEQ, C_KVL])
    O['p_c_kr'] = dout('p_c_kr', [NSEQ, SEQ, C_ROPE])
    O['s_a_k'] = dout('s_a_k', [NSEQ * TS, 512])
    O['s_a_v'] = dout('s_a_v', [NSEQ * TS, 512])
    O['s_b_k'] = dout('s_b_k', [NSEQ * TS, 512])
    O['s_b_v'] = dout('s_b_v', [NSEQ * TS, 512])
    O['s_c_kv'] = dout('s_c_kv', [NSEQ * TS, C_KVL])
    O['s_c_kr'] = dout('s_c_kr', [NSEQ * TS, C_ROPE])
    if debug:
        O['dbg_xT'] = dout('dbg_xT', [8, 128, NTOK])

    xT = nc.dram_tensor("xT_scr", [8, 128, NTOK], F32).ap()
    xT_pc = xT.rearrange("c p t -> p c t")
    B_xT = Buf('xT')
    RES = {'pc': xT_pc, 'B': B_xT, 'flat': xT}

    ident, B_ident = k.tile('ident', [128, 128], F32)
    k.dma(ident[:, :], I['c_ident'], writes=[B_ident])
    ones_bf, B_ones = k.tile('ones', [128, 128], BF16)
    k.memset('pool', ones_bf[:, :], 1.0, [B_ones])
    gains, B_gains = k.tile('gains', [128, 6, 8], F32)
    gi = 0
    for nm in ('ffn1_norm', 'mix_norm', 'ffn2_norm'):
        for l in range(2):
            k.dma(gains[:, gi, :], I[nm][l].rearrange("(c p) -> p c", p=128), writes=[B_gains],
                  allow_slow_non_contiguous=True)
            gi += 1
    G_IDX = {('ffn1', 0): 0, ('ffn1', 1): 1, ('mix', 0): 2, ('mix', 1): 3, ('ffn2', 0): 4, ('ffn2', 1): 5}

    psum = [(nc.alloc_psum_tensor("ps%d" % i, [128, 512], F32), Buf("ps%d" % i, x=True)) for i in range(8)]

    def phase0():
        m = sb.mark()
        xin = k.ring('p0in', 2, [128, 4, D], F32)
        xtr = k.ring('p0tr', 2, [128, 8, 512], F32)
        pr = Ring(psum[0:4])
        xp_v = I['xp'].rearrange("(g s p) d -> g p s d", s=4, p=128)
        for g in range(NSEQ * SEQ // 512):
            ti, Bi = xin.next()
            k.dma(ti[:, :, :], xp_v[g], writes=[Bi])
            to, Bo = xtr.next()
            for c in range(8):
                ps, Bp = pr.next()
                k.trs([(ps[:, s * 128:(s + 1) * 128], ti[:, s, c * 128:(c + 1) * 128], ident[:, :]) for s in range(4)],
                      reads=[Bi, B_ident], wbuf=Bp)
                k.copy_rr(to[:, c, :], ps[:, :], reads=[Bp], writes=[Bo])
            k.dma(xT_pc[:, :, g * 512:(g + 1) * 512], to[:, :, :], reads=[Bo], writes=[B_xT], eng='pool')
        ti, Bi = xin.next()
        k.dma(ti[0:32, 0, :], I['xs'], writes=[Bi])
        to, Bo = xtr.next()
        ps, Bp = pr.next()
        k.trs([(ps[:, c * 32:(c + 1) * 32], ti[0:32, 0, c * 128:(c + 1) * 128], ident[0:32, 0:32]) for c in range(8)],
              reads=[Bi, B_ident], wbuf=Bp)
        k.copy_rr(to[:, :, 0:32], ps[:, 0:256].rearrange("p (c t) -> p c t", c=8), reads=[Bp], writes=[Bo])
        k.dma(xT_pc[:, :, SOFF:SOFF + 32], to[:, :, 0:32], reads=[Bo], writes=[B_xT], eng='pool')
        sb.release(m)

    def rms_fm(x_t, Bx, nch, T, g_ap_fn, xn_t, Bxn, sq_ring, ps_ss, rs_t, Brs, dim, gbuf=None):
        pss, Bpss = ps_ss
        gbuf = B_gains if gbuf is None else gbuf
        for c in range(nch):
            sq, Bsq = sq_ring.next()
            k.act(sq[:, :T], x_t[:, c, :T], AF.Square, reads=[Bx], writes=[Bsq])
            k.mm(pss[:, :T], [(ones_bf[:, :], sq[:, :T])], reads=[B_ones, Bsq], wbuf=Bpss, start=(c == 0), stop=(c == nch - 1))
        k.rstd(rs_t[:, :T], pss[:, :T], [Bpss, B_eps], [Brs], 1.0 / dim, eps_t[:, 0:1])
        for c in range(nch):
            dst = xn_t(c) if callable(xn_t) else xn_t[:, c, :T]
            k.stt(dst, x_t[:, c, :T], g_ap_fn(c), rs_t[:, :T], ALU.mult, ALU.mult,
                  reads=[Bx, Brs, gbuf], writes=[Bxn])

    eps_t, B_eps = k.tile('eps', [128, 1], F32)
    k.memset('pool', eps_t[:, :], EPS, [B_eps])

    FT = 512

    def ffn_phase(which, layer):
        m = sb.mark()
        k.act_every = 2
        wgu_d = I[which + '_w_gu'][layer]
        wd_d = I[which + '_w_down'][layer]
        gidx = G_IDX[(which, layer)]
        wg, Bwg = k.tile('wg', [128, 8, 2 * DFF], BF16)
        wd, Bwd = k.tile('wd', [128, NJ, D], BF16)
        wgu_v = wgu_d.rearrange("(c p) n -> p c n", p=128)
        Bwg_g = [None] * NJ
        Bwg_u = [None] * NJ
        for j0 in range(0, NJ, 4):
            j1 = min(NJ, j0 + 4)
            bg, bu = Buf('wg_g'), Buf('wg_u')
            for j in range(j0, j1):
                Bwg_g[j], Bwg_u[j] = bg, bu
            k.dma(wg[:, :, j0 * 128:j1 * 128], wgu_v[:, :, j0 * 128:j1 * 128], writes=[bg], eng='pool')
            k.dma(wg[:, :, DFF + j0 * 128:DFF + j1 * 128], wgu_v[:, :, DFF + j0 * 128:DFF + j1 * 128],
                  writes=[bu], eng='pool')
        wd_v = wd_d.rearrange("(j p) n -> p j n", p=128)
        Bwd_blk = [Buf('wdblk') for _ in range(2)]
        k.dma(wd[:, 0:11, :], wd_v[:, 0:11, :], writes=[Bwd_blk[0]], eng='pool')
        k.dma(wd[:, 11:22, :], wd_v[:, 11:22, :], writes=[Bwd_blk[1]], eng='pool')

        xr = k.ring('fx', 2, [128, 8, FT], F32)
        xn, Bxn = k.tile('fxn', [128, 8, FT], BF16)
        sqr = k.ring('fsq', 3, [128, FT], BF16)
        rs, Brs = k.tile('frs', [128, FT], F32)
        actt, Bact = k.tile('fact', [128, NJ, FT], BF16)
        Bact_j = [Buf('actj') for _ in range(NJ)]
        sgr = k.ring('fsg', 2, [128, FT], F32)
        ps_g = Ring(psum[0:2])
        ps_u = Ring(psum[2:4])
        ps_y = Ring(psum[4:6])
        ps_ss = psum[6]
        tiles = [(i * FT, FT) for i in range(NTOK // FT)]
        if NTOK % FT:
            tiles.append((NTOK - NTOK % FT, NTOK % FT))
        ntile = len(tiles)

        def load(t):
            t0, T = tiles[t]
            tx, Bx = xr.next()
            k.dma(tx[:, :, :T], RES['pc'][:, :, t0:t0 + T], reads=[RES['B']], writes=[Bx])
            return tx, Bx

        def norm(t, tx, Bx):
            rms_fm(tx, Bx, 8, tiles[t][1], lambda c: gains[:, gidx, c:c + 1], xn, Bxn, sqr, ps_ss, rs, Brs, D)

        cur = load(0)
        norm(0, *cur)
        for t in range(ntile):
            t0, T = tiles[t]
            tx, Bx = cur
            nxt = load(t + 1) if t + 1 < ntile else None
            for j in range(NJ):
                pg, Bpg = ps_g.next()
                pu, Bpu = ps_u.next()
                k.mm(pg[:, :T], [(wg[:, c, j * 128:(j + 1) * 128], xn[:, c, :T]) for c in range(8)],
                     reads=[Bwg_g[j], Bxn], wbuf=Bpg)
                k.mm(pu[:, :T], [(wg[:, c, DFF + j * 128:DFF + (j + 1) * 128], xn[:, c, :T]) for c in range(8)],
                     reads=[Bwg_u[j], Bxn], wbuf=Bpu)
                sg, Bsg = sgr.next()
                k.act(sg[:, :T], pg[:, :T], AF.Silu, reads=[Bpg], writes=[Bsg])
                k.tt(actt[:, j, :T], pu[:, :T], sg[:, :T], ALU.mult, reads=[Bpu, Bsg], writes=[Bact_j[j]])
            if nxt is not None:
                norm(t + 1, *nxt)
            for c in range(8):
                py, Bpy = ps_y.next()
                k.mm(py[:, :T], [(wd[:, j, c * 128:(c + 1) * 128], actt[:, j, :T]) for j in range(NJ)],
                     reads=Bwd_blk + Bact_j, wbuf=Bpy)
                k.stt(tx[:, c, :T], py[:, :T], 0.5, tx[:, c, :T], ALU.mult, ALU.add, reads=[Bpy, Bx], writes=[Bx])
            k.dma(RES['pc'][:, :, t0:t0 + T], tx[:, :, :T], reads=[Bx], writes=[RES['B']], eng='pool')
            cur = nxt
        sb.release(m)

    def phaseZ():
        m = sb.mark()
        gfin, Bgf = k.tile('gfin', [128, D], F32)
        k.dma(gfin[:, :], dram_ap(I['final_norm'], 0, [[0, 128], [1, D]]), writes=[Bgf])
        xin = k.ring('zin', 2, [128, 8, 512], F32)
        xo = k.ring('zo', 3, [128, D], F32)
        junk, Bjunk = k.tile('zjunk', [128, D], BF16)
        st = k.ring('zst', 4, [128, 2], F32)
        pr = Ring(psum[0:6])

        def do_sub(ti, Bi, col0, ntok, dst_ap):
            to, Bo = xo.next()
            for half in range(2):
                ps, Bp = pr.next()
                k.trs([(ps[0:ntok, cc * 128:(cc + 1) * 128], ti[:, half * 4 + cc, col0:col0 + ntok], ident[:, :])
                       for cc in range(4)], reads=[Bi, B_ident], wbuf=Bp)
                k.copy_rr(to[0:ntok, half * 512:(half + 1) * 512], ps[0:ntok, :], reads=[Bp], writes=[Bo])
            s, Bs = st.next()
            k.act(junk[0:ntok, :], to[0:ntok, :], AF.Square, reads=[Bo], writes=[Bjunk, Bs], accum_out=s[0:ntok, 0:1])
            k.act(s[0:ntok, 1:2], s[0:ntok, 0:1], AF.Sqrt, reads=[Bs, B_eps], writes=[Bs], scale=1.0 / D, bias=eps_t[0:ntok, 0:1])
            k.recip_dve(s[0:ntok, 1:2], s[0:ntok, 1:2], reads=[Bs], writes=[Bs])
            k.stt(to[0:ntok, :], to[0:ntok, :], s[0:ntok, 1:2], gfin[0:ntok, :], ALU.mult, ALU.mult,
                  reads=[Bo, Bs, Bgf], writes=[Bo])
            k.dma(dst_ap, to[0:ntok, :], reads=[Bo], eng='pool')

        for g in range(NSEQ * SEQ // 512):
            ti, Bi = xin.next()
            k.dma(ti[:, :, :], RES['pc'][:, :, g * 512:(g + 1) * 512], reads=[RES['B']], writes=[Bi])
            for s_ in range(4):
                r0 = g * 512 + s_ * 128
                do_sub(ti, Bi, s_ * 128, 128, O['y_prompt'][r0:r0 + 128, :])
        ti, Bi = xin.next()
        k.dma(ti[:, :, 0:32], RES['pc'][:, :, SOFF:SOFF + 32], reads=[RES['B']], writes=[Bi])
        do_sub(ti, Bi, 0, 32, O['y_sample'][:, :])
        sb.release(m)

    oTA_scr = nc.dram_tensor("oTA_scr", [128, 4, NTOK], BF16).ap()
    B_oTA = Buf('oTA')
    toe_scr = nc.dram_tensor("toe_scr", [12, 130 * LT], F32).ap()
    B_toe = Buf('toe')
    CAUSAL_PAT = [[64, 8], [0, 64]]
    BAND_PAT = [[-64, 8], [0, 64]]

    def toe_write(v_t, Bv, nh, row0):
        src = v_t[0:nh, :].unsqueeze(1).broadcast_to([nh, 130, LT])
        dst = toe_scr[row0:row0 + nh, :].rearrange("h (r l) -> h r l", l=LT)
        k.dma(dst, src, reads=[Bv], writes=[B_toe])

    def toe_read(nh, row0, masters):
        for h in range(nh):
            T_, BT = masters[h]
            k.dma(T_[:, :], dram_ap(toe_scr, (row0 + h) * 130 * LT, [[LT - 1, 128], [1, WM]]), reads=[B_toe], writes=[BT])

    def toe_prep():
        m = sb.mark()
        arel8, Ba8 = k.tile('arel8', [8, 129], F32)
        k.dma(arel8[:, :], I['a_rel_bias'], writes=[Ba8])
        va, Bva = k.tile('va', [8, LT], F32)
        k.memset('dve', va[:, :], 0.0, [Bva])
        k.ts(va[:, 0:320], va[:, 0:320], arel8[:, 0:1], None, ALU.add, ALU.bypass, reads=[Bva, Ba8], writes=[Bva])
        k.copy('dve', va[:, 320:449], arel8[:, :], reads=[Ba8], writes=[Bva])
        k.ts(va[:, 449:1024], va[:, 449:1024], arel8[:, 128:129], None, ALU.add, ALU.bypass, reads=[Bva, Ba8], writes=[Bva])
        k.ts(va[:, 1024:LT], va[:, 1024:LT], arel8[:, 0:1], None, ALU.add, ALU.bypass, reads=[Bva, Ba8], writes=[Bva])
        toe_write(va, Bva, 8, 0)
        t5s, Bt5s = k.tile('t5s', [32, 4], F32)
        k.dma(t5s[:, :], I['t5_bias'], writes=[Bt5s])
        oneh, Boneh = k.tile('oneh', [32, LT], F32)
        k.dma(oneh[:, :], I['c_t5_onehot'], writes=[Boneh])
        vb, Bvb = k.tile('vb', [4, LT], F32)
        for i in range(3):
            ps, Bp = psum[4 + i]
            k.mm(ps[0:4, 0:384], [(t5s[:, :], oneh[:, i * 384:(i + 1) * 384])], reads=[Bt5s, Boneh], wbuf=Bp)
            k.copy('dve', vb[:, i * 384:(i + 1) * 384], ps[0:4, 0:384], reads=[Bp], writes=[Bvb])
        toe_write(vb, Bvb, 4, 8)
        sb.release(m)

    def mixer_res(nS):
        r = {}
        r['x'] = k.ring('mx', 2, [128, 8, 512], F32)
        r['h'] = k.ring('mh', 2, [128, 8, 512], BF16)
        r['sq'] = k.ring('msq', 3, [128, 512], BF16)
        r['rs'] = k.tile('mrs', [128, 512], F32)
        r['ss'] = psum[7]
        r['proj'] = Ring([psum[7], psum[0], psum[1], psum[2], psum[3]])
        r['S'] = Ring(psum[0:4])
        r['E'] = k.ring('mE', 8, [128, 512], BF16)
        r['Sp'] = k.ring('mSp', 3, [128, 512], F32)
        r['hl'] = k.ring('mhl', 4, [128, 512], BF16)
        r['rd'] = k.ring('mrd', 2, [128, 512], F32)
        r['stg'] = k.ring('mstg', 3, [128, 512], F32)
        return r

    def load_norm(mx, t0, T, gidx):
        tx, Bx = mx['x'].next()
        k.dma(tx[:, :, :T], xT_pc[:, :, t0:t0 + T], reads=[B_xT], writes=[Bx])
        hT, BhT = mx['h'].next()
        rms_fm(tx, Bx, 8, T, lambda c: gains[:, gidx, c:c + 1], hT, BhT, mx['sq'], mx['ss'], mx['rs'][0], mx['rs'][1], D)
        return tx, Bx, hT, BhT

    def proj_fm(mx, w, Bw, col0, hT, BhT, T, dst_ap, Bdst, mrows=128):
        ps, Bp = mx['proj'].next()
        k.mm(ps[:mrows, :T], [(w[:, c, col0:col0 + mrows], hT[:, c, :T]) for c in range(8)], reads=[Bw, BhT], wbuf=Bp)
        k.copy_rr(dst_ap, ps[:mrows, :T], reads=[Bp], writes=[Bdst])

    def proj_tm(mx, w, Bw, col0, ncol, hT, BhT, tcol0, ntok):
        ps, Bp = mx['proj'].next()
        k.mm(ps[:ntok, :ncol], [(hT[:, c, tcol0:tcol0 + ntok], w[:, c, col0:col0 + ncol]) for c in range(8)],
             reads=[Bw, BhT], wbuf=Bp)
        return ps, Bp

    def out_rows(mx, ps, Bp, ntok, ncol, dst_ap):
        stg, Bst = mx['stg'].next()
        k.copy_rr(stg[:ntok, :ncol], ps[:ntok, :ncol], reads=[Bp], writes=[Bst])
        k.dma(dst_ap, stg[:ntok, :ncol], reads=[Bst], eng='pool')

    def attn_run(mx, jobs, nq, scale, la=2, pair=True):
        steps = []
        gsz = 2 if pair else 1
        for j0 in range(0, len(jobs), gsz):
            grp = jobs[j0:j0 + gsz]
            nb = len(grp[0]['blocks'])
            for g_ in grp:
                assert len(g_['blocks']) == nb
            for bi in range(nb):
                steps.append([(g_, bi, nb, g_['blocks'][bi]) for g_ in grp])
        n = len(steps)
        st = {}

        def stA(t):
            ents = steps[t]
            S = [mx['S'].next() for _ in ents]
            items = []
            reads = []
            nqk = max(len(e_[3]['qk']) for e_ in ents)
            for qi in range(nqk):
                for ei, e_ in enumerate(ents):
                    b = e_[3]
                    if qi < len(b['qk']):
                        l, r = b['qk'][qi]
                        c0, c1 = b.get('cols', (0, nq))
                        items.append((S[ei][0][:b['nk'], :c1 - c0], l, r[:, c0:c1], qi == 0, qi == len(b['qk']) - 1))
            for e_ in ents:
                reads += e_[3]['rd']
            k.mm_multi(items, reads, [s_[1] for s_ in S])
            st[t] = S

        def stB(t):
            ents = steps[t]
            S = st[t]
            Es = []
            for ei, e_ in enumerate(ents):
                b = e_[3]
                nk = b['nk']
                c0, c1 = b.get('cols', (0, nq))
                n_ = c1 - c0
                ps, Bs = S[ei]
                E, BE = mx['E'].next()
                if b['bias'][0] == 't':
                    sp, Bsp = mx['Sp'].next()
                    k.stt(sp[:nk, :n_], ps[:nk, :n_], scale, b['bias'][1], ALU.mult, ALU.add,
                          reads=[Bs, b['bias'][2]], writes=[Bsp])
                    k.act(E[:nk, :n_], sp[:nk, :n_], AF.Exp, reads=[Bsp], writes=[BE])
                else:
                    k.act(E[:nk, :n_], ps[:nk, :n_], AF.Exp, reads=[Bs] + b['bias'][2], writes=[BE],
                          scale=scale, bias=b['bias'][1])
                for (pat, op, base, cm) in b['masks']:
                    def fsel(e, E=E, pat=pat, op=op, base=base, cm=cm, nk=nk, n_=n_):
                        return e.affine_select(out=E[:nk, :n_], in_=E[:nk, :n_], pattern=pat, compare_op=op, fill=0.0,
                                               base=base, channel_multiplier=cm)
                    P.op('pool', fsel, reads=[BE], writes=[BE])
                Es.append((E, BE))
            st[t] = Es

        def stC(t):
            ents = steps[t]
            Es = st.pop(t)
            for ei, (j, bi, nb, b) in enumerate(ents):
                nk = b['nk']
                c0, c1 = b.get('cols', (0, nq))
                E, BE = Es[ei]
                O_, BO = j['O']
                M = j['M']
                k.mm(O_[:M, c0:c1], [(b['v'], E[:nk, :c1 - c0])], reads=[b['Bv'], BE], wbuf=BO, start=(bi == 0), stop=(bi == nb - 1))
                if j.get('D') is not None:
                    Dap, BD = j['D']
                    k.mm(Dap[:, c0:c1], [(ones_bf[:nk, :], E[:nk, :c1 - c0])], reads=[B_ones, BE], wbuf=BD,
                         start=(bi == 0), stop=(bi == nb - 1))
            for ei, (j, bi, nb, b) in enumerate(ents):
                if bi == nb - 1:
                    r_ = j['epi']()
                    if callable(r_):
                        pending.append((cur_t[0] + 2, r_))

        pending = []
        cur_t = [0]
        for t in range(n + la):
            cur_t[0] = t
            if t < n:
                stA(t)
            if 0 <= t - 1 < n:
                stB(t - 1)
            if 0 <= t - la < n:
                stC(t - la)
            while pending and pending[0][0] <= t:
                pending.pop(0)[1]()
        while pending:
            pending.pop(0)[1]()

    def epilogue_std(mx, bc_ps, O_, BO, nq, dst_ap, Bdst):
        p0, Mout = 64, 64
        den_ap = O_[64:65, :nq]
        hi, Bhi = mx['hl'].next()
        lo, Blo = mx['hl'].next()
        k.copy('dve', hi[p0:p0 + 1, :nq], den_ap, reads=[BO], writes=[Bhi])
        k.tt(lo[p0:p0 + 1, :nq], den_ap, hi[p0:p0 + 1, :nq], ALU.subtract, reads=[BO, Bhi], writes=[Blo])

        def part2():
            bc, Bbc = bc_ps
            k.mm(bc[0:Mout, :nq], [(ones_bf[p0:p0 + 1, 0:Mout], hi[p0:p0 + 1, :nq]), (ones_bf[p0:p0 + 1, 0:Mout], lo[p0:p0 + 1, :nq])],
                 reads=[B_ones, Bhi, Blo], wbuf=Bbc)
            rd, Brd = mx['rd'].next()
            k.recip(rd[0:Mout, :nq], bc[0:Mout, :nq], reads=[Bbc], writes=[Brd])
            k.tt(dst_ap, O_[0:64, :nq], rd[0:64, :nq], ALU.mult, reads=[BO, Brd], writes=[Bdst])
        return part2

    def load_cache_T(mx, cache_ap, ntok, dstT, BdstT, nchunk=4, width=128):
        for tt in range(ntok // 128):
            stg, Bst = mx['stg'].next()
            k.dma(stg[:, :nchunk * width], cache_ap[tt * 128:(tt + 1) * 128, :], writes=[Bst])
            ps, Bp = mx['proj'].next()
            k.trs([(ps[:width, cc * 128:(cc + 1) * 128], stg[:, cc * width:(cc + 1) * width], ident[:, :]) for cc in range(nchunk)],
                  reads=[Bst, B_ident], wbuf=Bp)
            k.copy_rr(dstT[:width, 0:nchunk, tt * 128:(tt + 1) * 128],
                      ps[:width, 0:nchunk * 128].rearrange("p (c t) -> p c t", c=nchunk), reads=[Bp], writes=[BdstT])

    def mixA():
        m = sb.mark()
        k.act_every = 3
        gidx = G_IDX[('mix', 0)]
        scale = A_D ** -0.5
        win, Bwin = k.tile('winA', [128, 8, 1536], BF16)
        Bw3 = [Buf('winA%d' % i) for i in range(3)]
        wv = I['e_w_in'].rearrange("(c p) n -> p c n", p=128)
        for i in (1, 2, 0):
            k.dma(win[:, :, i * 512:(i + 1) * 512], wv[:, :, i * 512:(i + 1) * 512], writes=[Bw3[i]], eng='pool')
        Bq, Bk_, Bv_ = Bw3
        arel_bc, Barel = k.tile('arelbc', [128, 8 * 129], F32)
        k.dma(arel_bc[:, :], dram_ap(I['a_rel_bias'], 0, [[0, 128], [1, 8 * 129]]), writes=[Barel])
        TA = [k.tile('TA%d' % h, [128, WM], F32) for h in range(8)]
        toe_read(8, 0, TA)
        KT, BKT = k.tile('KTA', [128, 4, SEQ], BF16)
        QTr = k.ring('QTAp', 2, [128, 8, 512], BF16)
        for (QT_, BQT_) in QTr.items:
            k.memset('pool', QT_[:, :, :], 0.0, [BQT_])
        KTn, BKTn = k.tile('KTAn', [128, 4, 32], BF16)
        VA, BVA = k.tile('VA', [128, 16, 8, 65], BF16)
        k.memset('pool', VA[:, :, :, 64:65], 1.0, [BVA])
        oT, BoT = k.tile('oTA', [128, 4, 512], BF16)
        mx = mixer_res(3)
        mx['S'] = Ring(psum[0:3])
        Oring = Ring(psum[3:6])
        bc_ps = psum[6]

        def heads(QT, BQT, nq, qcol0, blocks_fn, ocol0):
            jobs = []
            for h in range(8):
                pb, oc = (h % 2) * 64, h // 2
                qap = QT[:, h, qcol0:qcol0 + nq]
                Oo = Oring.next()
                jobs.append(dict(blocks=blocks_fn(h, pb, oc, qap, BQT), O=Oo, M=65, D=None,
                                 epi=(lambda Oo=Oo, pb=pb, oc=oc: epilogue_std(mx, bc_ps, Oo[0], Oo[1], nq,
                                                                               oT[pb:pb + 64, oc, ocol0:ocol0 + nq], BoT))))
            attn_run(mx, jobs, nq, scale, la=4, pair=False)

        def q_projA(QT, BQT, hT, BhT, T):
            for oc in range(4):
                ps, Bp = mx['proj'].next()
                k.mm(ps[:, :T], [(win[:, c, oc * 128:(oc + 1) * 128], hT[:, c, :T]) for c in range(8)], reads=[Bq, BhT], wbuf=Bp)
                k.copy_rr(QT[0:64, 2 * oc, :T], ps[0:64, :T], reads=[Bp], writes=[BQT])
                k.copy_rr(QT[64:128, 2 * oc + 1, :T], ps[64:128, :T], reads=[Bp], writes=[BQT])

        def bias_for(h, d, nk, nq):
            if -384 <= d <= 128:
                return ('t', TA[h][0][:nk, d + 384:d + 384 + nq], TA[h][1])
            return ('c', arel_bc[:nk, h * 129 + 128:h * 129 + 129], [Barel])

        DBG = os.environ.get('MIXDBG', '')
        for s in range(NSEQ):
            if DBG == 'setup' or (DBG in ('pass1', 'pass2', 'pass2h1') and s > 0):
                break
            tb = s * SEQ
            nxt = load_norm(mx, tb, 512, gidx)
            for tt in range(4):
                tx, Bx, hT, BhT = nxt
                if tt < 3:
                    nxt = load_norm(mx, tb + (tt + 1) * 512, 512, gidx)
                for oc in range(4):
                    proj_fm(mx, win, Bk_, 512 + oc * 128, hT, BhT, 512, KT[:, oc, tt * 512:(tt + 1) * 512], BKT)
                for su in range(4):
                    ps, Bp = proj_tm(mx, win, Bv_, 1024, 512, hT, BhT, su * 128, 128)
                    k.copy_rr(VA[:, tt * 4 + su, :, 0:64], ps[:, 0:512].rearrange("p (h d) -> p h d", h=8), reads=[Bp], writes=[BVA])
                    if tt == 3:
                        out_rows(mx, ps, Bp, 128, 512, O['p_a_v'][s, su * 128:(su + 1) * 128, :])
                        ps, Bp = proj_tm(mx, win, Bk_, 512, 512, hT, BhT, su * 128, 128)
                        out_rows(mx, ps, Bp, 128, 512, O['p_a_k'][s, su * 128:(su + 1) * 128, :])
            def prologue(qt, tb=tb):
                tx, Bx, hT, BhT = load_norm(mx, tb + qt * 512, 512, gidx)
                QT, BQT = QTr.next()
                q_projA(QT, BQT, hT, BhT, 512)
                return QT, BQT
            nxtq = prologue(0)
            for qt in range(4):
                q0 = qt * 512
                QT, BQT = nxtq
                if qt < 3:
                    nxtq = prologue(qt + 1)

                def blocks_fn(h, pb, oc, qap, BQT, q0=q0):
                    bl = []
                    k0s = list(range(max(0, q0 - 512), q0 + 512, 128))
                    k0s.sort(key=lambda k0_: (k0_ != q0))
                    for k0 in k0s:
                        d = q0 - k0
                        c0, c1 = 0, 512
                        if d < 0:
                            c0 = -d
                        if d >= 256:
                            c1 = 640 - d
                        n_ = c1 - c0
                        masks = []
                        if d <= 0:
                            masks.append(([[64, n_ // 64], [0, 64]], ALU.is_gt, d + 64 + c0, -1))
                        if d >= 128:
                            masks.append(([[-64, n_ // 64], [0, 64]], ALU.is_ge, 512 - d, 1))
                        if -384 <= d <= 128:
                            bias = ('t', TA[h][0][:128, d + 384 + c0:d + 384 + c1], TA[h][1])
                        else:
                            bias = bias_for(h, d, 128, 512)
                        bl.append(dict(nk=128, qk=[(KT[:, oc, k0:k0 + 128], qap)], rd=[BKT, BQT], cols=(c0, c1),
                                       v=VA[:, k0 // 128, h, :], Bv=BVA, bias=bias, masks=masks))
                    return bl
                heads(QT, BQT, 512, 0, blocks_fn, 0)
                k.dma(oTA_scr[:, :, tb + q0:tb + q0 + 512], oT[:, :, :], reads=[BoT], writes=[B_oTA], eng='pool')
        tx, Bx, hT, BhT = load_norm(mx, SOFF, 32, gidx)
        QT, BQT = QTr.next()
        q_projA(QT, BQT, hT, BhT, 32)
        for oc in range(4):
            proj_fm(mx, win, Bk_, 512 + oc * 128, hT, BhT, 32, KTn[:, oc, :], BKTn)
        for s in range(NSEQ):
            if DBG in ('setup', 'pass1', 'pass2', 'pass2h1'):
                break
            ps, Bp = proj_tm(mx, win, Bk_, 512, 512, hT, BhT, s * TS, TS)
            out_rows(mx, ps, Bp, TS, 512, O['s_a_k'][s * TS:(s + 1) * TS, :])
            ps, Bp = proj_tm(mx, win, Bv_, 1024, 512, hT, BhT, s * TS, TS)
            out_rows(mx, ps, Bp, TS, 512, O['s_a_v'][s * TS:(s + 1) * TS, :])
            k.copy_rr(VA[0:TS, 4, :, 0:64], ps[0:TS, 0:512].rearrange("p (h d) -> p h d", h=8), reads=[Bp], writes=[BVA])
            load_cache_T(mx, I['cache_a_k'][s], 512, KT, BKT)
            k.copy('dve', KT[:, :, 512:512 + TS], KTn[:, :, s * TS:(s + 1) * TS], reads=[BKTn], writes=[BKT])
            for kt in range(4):
                stg, Bst = mx['stg'].next()
                k.dma(stg[:, :], I['cache_a_v'][s, kt * 128:(kt + 1) * 128, :], writes=[Bst])
                k.copy_rr(VA[:, kt, :, 0:64], stg[:, :].rearrange("p (h d) -> p h d", h=8), reads=[Bst], writes=[BVA])

            def blocks_fn(h, pb, oc, qap, BQT):
                bl = []
                for kt in range(4):
                    d = 512 - 128 * kt
                    bl.append(dict(nk=128, qk=[(KT[:, oc, kt * 128:(kt + 1) * 128], qap)], rd=[BKT, BQT],
                                   v=VA[:, kt, h, :], Bv=BVA, bias=bias_for(h, d, 128, TS), masks=[]))
                bl.append(dict(nk=TS, qk=[(KT[:, oc, 512:512 + TS], qap)], rd=[BKT, BQT],
                               v=VA[0:TS, 4, h, :], Bv=BVA, bias=bias_for(h, 0, TS, TS), masks=[]))
                return bl
            heads(QT, BQT, TS, s * TS, blocks_fn, s * TS)
        k.dma(oTA_scr[:, :, SOFF:SOFF + 32], oT[:, :, 0:32], reads=[BoT], writes=[B_oTA], eng='pool')
        sb.release(m)

    def mixB():
        m = sb.mark()
        k.act_every = 3
        gidx = G_IDX[('mix', 0)]
        scale = B_D ** -0.5
        lam_init = 0.8 - 0.6 * math.exp(-0.3 * 0)
        win, Bwin = k.tile('winB', [128, 8, 1536], BF16)
        Bw3 = [Buf('winB%d' % i) for i in range(3)]
        wv = I['e_w_in'].rearrange("(c p) n -> p c n", p=128)
        for i in (1, 2, 0):
            k.dma(win[:, :, i * 512:(i + 1) * 512], wv[:, :, 1536 + i * 512:1536 + (i + 1) * 512], writes=[Bw3[i]], eng='pool')
        Bq, Bk_, Bv_ = Bw3
        wout, Bwout = k.tile('woutE', [128, 8, D], BF16)
        k.dma(wout[:, :, :], I['e_w_out'].rearrange("(c p) n -> p c n", p=128), writes=[Bwout], eng='pool')
        lv, Blv = k.tile('lamv', [128, 4, 64], F32)
        for i, nm in enumerate(('b_lambda_q1', 'b_lambda_k1', 'b_lambda_q2', 'b_lambda_k2')):
            k.dma(lv[:, i, :], dram_ap(I[nm], 0, [[0, 128], [1, 64]]), writes=[Blv])
        lsc, Blsc = k.tile('lamsc', [128, 8], F32)
        ljunk, Bljunk = k.tile('lamjunk', [128, 64], F32)
        for i in range(2):
            P.op('dve', (lambda e, i=i: e.scalar_tensor_tensor(out=ljunk[:, :], in0=lv[:, 2 * i, :], scalar=1.0, in1=lv[:, 2 * i + 1, :],
                                                                op0=ALU.mult, op1=ALU.mult, accum_out=lsc[:, i:i + 1])),
                 reads=[Blv], writes=[Bljunk, Blsc])
        k.act(lsc[:, 2:4], lsc[:, 0:2], AF.Exp, reads=[Blsc], writes=[Blsc])
        k.tt(lsc[:, 4:5], lsc[:, 3:4], lsc[:, 2:3], ALU.subtract, reads=[Blsc], writes=[Blsc])
        k.ts(lsc[:, 5:6], lsc[:, 4:5], -lam_init, None, ALU.add, ALU.bypass, reads=[Blsc], writes=[Blsc])
        nlam = lsc[:, 5:6]
        gsub, Bgsub = k.tile('gsub', [128, 2], F32)
        k.dma(gsub[:, 0:1], I['b_subln'].rearrange("o (p u) -> (o p) u", u=1), writes=[Bgsub])
        k.ts(gsub[:, 1:2], gsub[:, 0:1], 1.0 - lam_init, None, ALU.mult, ALU.bypass, reads=[Bgsub], writes=[Bgsub])
        t5_bc, Bt5bc = k.tile('t5bc', [128, 128], F32)
        k.dma(t5_bc[:, :], dram_ap(I['t5_bias'], 0, [[0, 128], [1, 128]]), writes=[Bt5bc])
        TB = [k.tile('TB%d' % h, [128, WM], F32) for h in range(4)]
        toe_read(4, 8, TB)
        KT, BKT = k.tile('KTB', [128, 4, SEQ], BF16)
        QTr = k.ring('QTBp', 2, [128, 8, 512], BF16)
        for (QT_, BQT_) in QTr.items:
            k.memset('pool', QT_[:, :, :], 0.0, [BQT_])
        KTn, BKTn = k.tile('KTBn', [128, 4, 32], BF16)
        VB, BVB = k.tile('VB', [128, 16, 512], BF16)
        oT, BoT = k.tile('oTB', [128, 8, 512], BF16)
        on_r = k.ring('mOn', 2, [128, 512], F32)
        ob_t, Bob = k.tile('mob', [128, 512], F32)
        mx = mixer_res(2)
        mx['S'] = Ring(psum[0:3])
        mx['proj'] = Ring([psum[7], psum[0], psum[1], psum[2]])
        Ops = [psum[3], psum[4]]
        Dps = [psum[5], psum[6]]
        bc_ps = psum[7]

        def q_proj(hT, BhT, T):
            QT, BQT = QTr.next()
            q_proj2(QT, BQT, hT, BhT, T)
            return QT, BQT

        def q_proj2(QT, BQT, hT, BhT, T):
            for h in range(4):
                ps, Bp = mx['proj'].next()
                k.mm(ps[:, :T], [(win[:, c, h * 128:(h + 1) * 128], hT[:, c, :T]) for c in range(8)], reads=[Bq, BhT], wbuf=Bp)
                k.copy_rr(QT[0:64, 2 * h, :T], ps[0:64, :T], reads=[Bp], writes=[BQT])
                k.copy_rr(QT[64:128, 2 * h + 1, :T], ps[64:128, :T], reads=[Bp], writes=[BQT])

        def heads(QT, BQT, nq, qcol0, blocks_fn, ocol0):
            jobs = []
            for h in range(4):
                Ons = [on_r.next(), on_r.next()]

                def epi_map(mp_, Ons=Ons):
                    O_, BO = Ops[mp_]
                    rd, Brd = mx['rd'].next()
                    k.recip(rd[:, :nq], Dps[mp_][0][:, :nq], reads=[Dps[mp_][1]], writes=[Brd])
                    On, BOn = Ons[mp_]
                    k.tt(On[:, :nq], O_[:, :nq], rd[:, :nq], ALU.mult, reads=[BO, Brd], writes=[BOn])

                def epi1(h=h, Ons=Ons, epi_map=epi_map):
                    epi_map(1)
                    k.stt(ob_t[:, :nq], Ons[1][0][:, :nq], nlam, Ons[0][0][:, :nq], ALU.mult, ALU.add,
                          reads=[Ons[0][1], Ons[1][1], Blsc], writes=[Bob])
                    sq, Bsq = mx['sq'].next()
                    k.act(sq[:, :nq], ob_t[:, :nq], AF.Square, reads=[Bob], writes=[Bsq])
                    bc, Bbc = bc_ps
                    k.mm(bc[:, :nq], [(ones_bf[:, :], sq[:, :nq])], reads=[B_ones, Bsq], wbuf=Bbc)
                    rs, Brs = mx['rd'].next()
                    k.rstd(rs[:, :nq], bc[:, :nq], [Bbc, B_eps], [Brs], 1.0 / 128, eps_t[:, 0:1])
                    k.stt(oT[:, 4 + h, ocol0:ocol0 + nq], ob_t[:, :nq], gsub[:, 1:2], rs[:, :nq], ALU.mult, ALU.mult,
                          reads=[Bob, Brs, Bgsub], writes=[BoT])

                for mp_ in range(2):
                    qap = QT[:, 2 * h + mp_, qcol0:qcol0 + nq]
                    jobs.append(dict(blocks=blocks_fn(h, mp_, qap, BQT), O=Ops[mp_], M=128, D=Dps[mp_],
                                     epi=((lambda epi_map=epi_map: epi_map(0)) if mp_ == 0 else epi1)))
            attn_run(mx, jobs, nq, scale, la=4, pair=False)

        def bias_for(h, dk, nk, nq):
            if dk >= -128:
                return ('t', TB[h][0][:nk, 384 - dk:384 - dk + nq], TB[h][1])
            return ('c', t5_bc[:nk, 15 * 4 + h:15 * 4 + h + 1], [Bt5bc])

        def out_proj(tx, Bx, T, t0):
            for c in range(8):
                ps, Bp = mx['proj'].next()
                k.mm(ps[:, :T], [(wout[:, kc, c * 128:(c + 1) * 128], oT[:, kc, :T]) for kc in range(8)], reads=[Bwout, BoT], wbuf=Bp)
                k.stt(tx[:, c, :T], ps[:, :T], 1.0, tx[:, c, :T], ALU.mult, ALU.add, reads=[Bp, Bx], writes=[Bx])
            k.dma(xT_pc[:, :, t0:t0 + T], tx[:, :, :T], reads=[Bx], writes=[B_xT], eng='pool')

        for s in range(NSEQ):
            tb = s * SEQ
            nxt = load_norm(mx, tb, 512, gidx)
            for tt in range(4):
                tx, Bx, hT, BhT = nxt
                if tt < 3:
                    nxt = load_norm(mx, tb + (tt + 1) * 512, 512, gidx)
                for oc in range(4):
                    proj_fm(mx, win, Bk_, 512 + oc * 128, hT, BhT, 512, KT[:, oc, tt * 512:(tt + 1) * 512], BKT)
                for su in range(4):
                    r0 = tt * 512 + su * 128
                    ps, Bp = proj_tm(mx, win, Bv_, 1024, 512, hT, BhT, su * 128, 128)
                    k.copy_rr(VB[:, tt * 4 + su, :], ps[:, 0:512], reads=[Bp], writes=[BVB])
                    out_rows(mx, ps, Bp, 128, 512, O['p_b_v'][s, r0:r0 + 128, :])
                    ps, Bp = proj_tm(mx, win, Bk_, 512, 512, hT, BhT, su * 128, 128)
                    out_rows(mx, ps, Bp, 128, 512, O['p_b_k'][s, r0:r0 + 128, :])
            def prologue(qt, tb=tb):
                tx, Bx, hT, BhT = load_norm(mx, tb + qt * 512, 512, gidx)
                QT, BQT = q_proj(hT, BhT, 512)
                return tx, Bx, QT, BQT
            nxtq = prologue(0)
            for qt in range(4):
                q0 = qt * 512
                tx, Bx, QT, BQT = nxtq
                if qt < 3:
                    nxtq = prologue(qt + 1)
                k.dma(oT[:, 0:4, :], oTA_scr[:, :, tb + q0:tb + q0 + 512], reads=[B_oTA], writes=[BoT])

                def blocks_fn(h, mp_, qap, BQT, q0=q0):
                    bl = []
                    for k0 in range(0, q0 + 512, 128):
                        dk = k0 - q0
                        c0 = max(0, dk)
                        n_ = 512 - c0
                        masks = [([[64, n_ // 64], [0, 64]], ALU.is_gt, 64, -1)] if dk >= 0 else []
                        if dk >= -128:
                            bias = ('t', TB[h][0][:128, 384 - dk + c0:384 - dk + 512], TB[h][1])
                        else:
                            bias = bias_for(h, dk, 128, 512)
                        bl.append(dict(nk=128, qk=[(KT[:, h, k0:k0 + 128], qap)], rd=[BKT, BQT], cols=(c0, 512),
                                       v=VB[:, k0 // 128, h * 128:(h + 1) * 128], Bv=BVB, bias=bias, masks=masks))
                    return bl
                heads(QT, BQT, 512, 0, blocks_fn, 0)
                out_proj(tx, Bx, 512, tb + q0)
        tx, Bx, hT, BhT = load_norm(mx, SOFF, 32, gidx)
        QT, BQT = q_proj(hT, BhT, 32)
        for oc in range(4):
            proj_fm(mx, win, Bk_, 512 + oc * 128, hT, BhT, 32, KTn[:, oc, :], BKTn)
        k.dma(oT[:, 0:4, 0:32], oTA_scr[:, :, SOFF:SOFF + 32], reads=[B_oTA], writes=[BoT])
        for s in range(NSEQ):
            ps, Bp = proj_tm(mx, win, Bk_, 512, 512, hT, BhT, s * TS, TS)
            out_rows(mx, ps, Bp, TS, 512, O['s_b_k'][s * TS:(s + 1) * TS, :])
            ps, Bp = proj_tm(mx, win, Bv_, 1024, 512, hT, BhT, s * TS, TS)
            out_rows(mx, ps, Bp, TS, 512, O['s_b_v'][s * TS:(s + 1) * TS, :])
            k.copy_rr(VB[0:TS, 8, :], ps[0:TS, 0:512], reads=[Bp], writes=[BVB])
            load_cache_T(mx, I['cache_b_k'][s], PAST, KT, BKT)
            k.copy('dve', KT[:, :, PAST:PAST + TS], KTn[:, :, s * TS:(s + 1) * TS], reads=[BKTn], writes=[BKT])
            for kt in range(8):
                stg, Bst = mx['stg'].next()
                k.dma(stg[:, :], I['cache_b_v'][s, kt * 128:(kt + 1) * 128, :], writes=[Bst])
                k.copy_rr(VB[:, kt, :], stg[:, :], reads=[Bst], writes=[BVB])

            def blocks_fn(h, mp_, qap, BQT):
                bl = []
                for kt in range(8):
                    dk = 128 * kt - PAST
                    bl.append(dict(nk=128, qk=[(KT[:, h, kt * 128:(kt + 1) * 128], qap)], rd=[BKT, BQT],
                                   v=VB[:, kt, h * 128:(h + 1) * 128], Bv=BVB, bias=bias_for(h, dk, 128, TS), masks=[]))
                bl.append(dict(nk=TS, qk=[(KT[:, h, PAST:PAST + TS], qap)], rd=[BKT, BQT],
                               v=VB[0:TS, 8, h * 128:(h + 1) * 128], Bv=BVB, bias=bias_for(h, 0, TS, TS), masks=[]))
                return bl
            heads(QT, BQT, TS, s * TS, blocks_fn, s * TS)
        out_proj(tx, Bx, 32, SOFF)
        sb.release(m)

    def mixC():
        m = sb.mark()
        k.act_every = 3
        gidx = G_IDX[('mix', 1)]
        scale = (C_NOPE + C_ROPE) ** -0.5
        win, Bwin = k.tile('winC', [128, 8, 672], BF16)
        k.dma(win[:, :, :], I['c_w_in'].rearrange("(c p) n -> p c n", p=128), writes=[Bwin], eng='pool')
        wkr3, Bwkr3 = k.tile('wkr3', [128, 8, 96], BF16)
        wkr3r, Bwkr3r = k.tile('wkr3r', [128, 8, 96], BF16)
        for g in range(3):
            k.copy('dve', wkr3[:, :, g * 32:(g + 1) * 32], win[:, :, 640:672], reads=[Bwin], writes=[Bwkr3])
            k.ts(wkr3r[:, :, g * 32:g * 32 + 16], win[:, :, 656:672], -1.0, None, ALU.mult, ALU.bypass, reads=[Bwin], writes=[Bwkr3r])
            k.copy('dve', wkr3r[:, :, g * 32 + 16:g * 32 + 32], win[:, :, 640:656], reads=[Bwin], writes=[Bwkr3r])
        wq_v = I['c_w_q_up'].rearrange("(k p) (h e) -> p k h e", p=128, e=96)
        wqn, Bwqn = k.tile('wqn', [128, 3, 16, 64], BF16)
        wqr, Bwqr = k.tile('wqr', [128, 3, 16, 32], BF16)
        wqrr, Bwqrr = k.tile('wqrr', [128, 3, 16, 32], BF16)
        for k3 in range(3):
            k.dma(wqn[:, k3, :, :], wq_v[:, k3, :, 0:64], writes=[Bwqn], eng='pool')
            k.dma(wqr[:, k3, :, :], wq_v[:, k3, :, 64:96], writes=[Bwqr], eng='pool')
        for k3 in range(3):
            k.ts(wqrr[:, k3, :, 0:16], wqr[:, k3, :, 16:32], -1.0, None, ALU.mult, ALU.bypass, reads=[Bwqr], writes=[Bwqrr])
            k.copy('dve', wqrr[:, k3, :, 16:32], wqr[:, k3, :, 0:16], reads=[Bwqr], writes=[Bwqrr])
        wqn2 = wqn[:, :, :, :].rearrange("p k h e -> p k (h e)")
        wqr2 = wqr[:, :, :, :].rearrange("p k h e -> p k (h e)")
        wqrr2 = wqrr[:, :, :, :].rearrange("p k h e -> p k (h e)")
        wkv_v = I['c_w_kv_up'].rearrange("(k p) (h e) -> p k h e", p=128, e=128)
        wkn, Bwkn = k.tile('wkn', [128, 2, 16, 64], BF16)
        wvv, Bwvv = k.tile('wvv', [128, 2, 16, 64], BF16)
        for k2 in range(2):
            k.dma(wkn[:, k2, :, :], wkv_v[:, k2, :, 0:64], writes=[Bwkn], eng='pool')
            k.dma(wvv[:, k2, :, :], wkv_v[:, k2, :, 64:128], writes=[Bwvv], eng='pool')
        wkn2 = wkn[:, :, :, :].rearrange("p k h e -> p k (h e)")
        wvv2 = wvv[:, :, :, :].rearrange("p k h e -> p k (h e)")
        wout, Bwout = k.tile('woutC', [128, 8, D], BF16)
        k.dma(wout[:, :, :], I['c_w_out'].rearrange("(c p) n -> p c n", p=128), writes=[Bwout], eng='pool')
        gq, Bgq = k.tile('gq', [128, 5], F32)
        k.dma(gq[:, 0:3], I['c_q_norm'].rearrange("o (c p) -> (o p) c", p=128), writes=[Bgq], allow_slow_non_contiguous=True)
        k.dma(gq[:, 3:5], I['c_kv_norm'].rearrange("o (c p) -> (o p) c", p=128), writes=[Bgq], allow_slow_non_contiguous=True)
        knT, BknT = k.tile('knT', [128, 8, SEQ], BF16)
        krr3, Bkrr3 = k.tile('krrp', [128, 3, SEQ], BF16)
        k.memset('pool', krr3[:, :, :], 0.0, [Bkrr3])
        xT2 = nc.dram_tensor("xT2_scr", [8, 128, NTOK], F32).ap()
        xT2_pc = xT2.rearrange("c p t -> p c t")
        B_xT2 = Buf('xT2')

        def kr_write(c0, n, src, Bsrc):
            for g_ in range(3):
                k.copy_rr(krr3[g_ * 32:(g_ + 1) * 32, g_, c0:c0 + n], src[g_ * 32:(g_ + 1) * 32, :n], reads=[Bsrc], writes=[Bkrr3])
        VC, BVC = k.tile('VC', [128, 16, 16, 65], BF16)
        k.memset('pool', VC[:, :, :, 64:65], 1.0, [BVC])
        cqn, Bcqn = k.tile('cqn', [128, 3, SEQ], BF16)
        S_ring = Ring(psum[0:3])
        Oring = Ring(psum[3:6])
        bc_ps = psum[6]
        proj = Ring([psum[7], psum[0], psum[1], psum[2], psum[3]])
        ss_ps = psum[7]

        def work1(T, with_cache=False):
            w = {}
            w['x'] = k.tile('c1x', [128, 8, T], F32)
            w['h'] = k.ring('c1h', 2, [128, 8, T], BF16)
            w['sq'] = k.ring('c1sq', 3, [128, T], BF16)
            w['rs'] = k.tile('c1rs', [128, T], F32)
            w['cqf'] = k.tile('c1cqf', [128, 3, T], F32)
            w['ckvf'] = k.tile('c1ckvf', [128, 2, T], F32)
            w['ckvn'] = w['ckvf']
            w['ckvb'] = k.tile('c1ckvb', [128, 2, T], BF16)
            w['t1'] = k.tile('c1t1', [96, T], F32)
            w['krrf'] = k.tile('c1krrf', [96, T], F32)
            w['cos'] = k.ring('c1cos', 2, [96, T], F32)
            w['sin'] = k.ring('c1sin', 2, [96, T], F32)
            w['stg'] = k.ring('c1stg', 2, [128, 512], F32)
            return w

        def work2(T):
            w = {}
            w['xc'] = k.ring('c2xc', 3, [128, T], F32)
            w['cos'] = k.tile('c2cos', [96, T], F32)
            w['sin'] = k.tile('c2sin', [96, T], F32)
            w['qnT'] = k.tile('c2qnp', [128, 16, T], BF16)
            w['qrT'] = k.tile('c2qr', [128, 6, T], BF16)
            k.memset('pool', w['qnT'][0][:, :, :], 0.0, [w['qnT'][1]])
            k.memset('pool', w['qrT'][0][:, :, :], 0.0, [w['qrT'][1]])
            w['oT'] = k.tile('c2oT', [128, 8, T], BF16)
            w['t1'] = k.tile('c2t1', [96, T], F32)
            w['t2'] = k.tile('c2t2', [96, T], F32)
            w['E'] = k.ring('c2E', 6, [128, T], BF16)
            w['hl'] = k.ring('c2hl', 2, [128, T], BF16)
            w['rd'] = k.ring('c2rd', 2, [128, T], F32)
            w['S'] = S_ring
            return w

        def p1_load(w, t0, T, pcol):
            tx, Bx = w['x']
            k.dma(tx[:, :, :T], xT_pc[:, :, t0:t0 + T], reads=[B_xT], writes=[Bx])
            cs, Bcs = w['cos'].next()
            sn, Bsn = w['sin'].next()
            k.dma(cs[:, :T], I['c_rope_cos'][:, pcol:pcol + T], writes=[Bcs])
            k.dma(sn[:, :T], I['c_rope_sin'][:, pcol:pcol + T], writes=[Bsn])
            hT, BhT = w['h'].next()
            rms_fm(tx, Bx, 8, T, lambda c: gains[:, gidx, c:c + 1], hT, BhT, w['sq'], ss_ps, w['rs'][0], w['rs'][1], D)
            return hT, BhT, cs, Bcs, sn, Bsn

        def pass1_tile(w, ld, T, tcol, kr_dst, kv_out, kr_out):
            hT, BhT, cs, Bcs, sn, Bsn = ld
            cqf, Bcqf = w['cqf']
            ckvf, Bckvf = w['ckvf']
            for kq in range(3):
                ps, Bp = proj.next()
                k.mm(ps[:, :T], [(win[:, c, kq * 128:(kq + 1) * 128], hT[:, c, :T]) for c in range(8)], reads=[Bwin, BhT], wbuf=Bp)
                k.copy_rr(cqf[:, kq, :T], ps[:, :T], reads=[Bp], writes=[Bcqf])
            for k2 in range(2):
                ps, Bp = proj.next()
                k.mm(ps[:, :T], [(win[:, c, 384 + k2 * 128:384 + (k2 + 1) * 128], hT[:, c, :T]) for c in range(8)], reads=[Bwin, BhT], wbuf=Bp)
                k.copy_rr(ckvf[:, k2, :T], ps[:, :T], reads=[Bp], writes=[Bckvf])
            t1, Bt1 = w['t1']
            krrf, Bkrrf = w['krrf']
            ps, Bp = proj.next()
            k.mm(ps[0:96, :T], [(wkr3[:, c, :], hT[:, c, :T]) for c in range(8)], reads=[Bwkr3, BhT], wbuf=Bp)
            k.tt(t1[:, :T], ps[0:96, :T], cs[:, :T], ALU.mult, reads=[Bp, Bcs], writes=[Bt1])
            ps, Bp = proj.next()
            k.mm(ps[0:96, :T], [(wkr3r[:, c, :], hT[:, c, :T]) for c in range(8)], reads=[Bwkr3r, BhT], wbuf=Bp)
            k.tt(krrf[:, :T], ps[0:96, :T], sn[:, :T], ALU.mult, reads=[Bp, Bsn], writes=[Bkrrf])
            k.tt(krrf[:, :T], krrf[:, :T], t1[:, :T], ALU.add, reads=[Bkrrf, Bt1], writes=[Bkrrf])
            kr_dst(krrf, Bkrrf)
            rms_fm(cqf, Bcqf, 3, T, lambda c: gq[:, c:c + 1], (lambda c: cqn[:, c, tcol:tcol + T]), Bcqn, w['sq'], ss_ps,
                   w['rs'][0], w['rs'][1], C_QL, gbuf=Bgq)
            ckvn, Bckvn = w['ckvn']
            ckvb, Bckvb = w['ckvb']
            rms_fm(ckvf, Bckvf, 2, T, lambda c: gq[:, 3 + c:4 + c], ckvn, Bckvn, w['sq'], ss_ps, w['rs'][0], w['rs'][1], C_KVL, gbuf=Bgq)
            for k2 in range(2):
                k.copy_rr(ckvb[:, k2, :T], ckvn[:, k2, :T], reads=[Bckvn], writes=[Bckvb])
            for su in range((T + 127) // 128):
                nt = min(128, T - su * 128)
                ps, Bp = proj.next()
                items = [(ps[0:nt, k2 * 128:(k2 + 1) * 128], ckvn[:, k2, su * 128:su * 128 + nt], ident[:, :]) for k2 in range(2)]
                items.append((ps[0:nt, 256:288], krrf[0:32, su * 128:su * 128 + nt], ident[0:32, 0:32]))
                k.trs(items, reads=[Bckvn, Bkrrf, B_ident], wbuf=Bp)
                stg, Bst = w['stg'].next()
                k.copy_rr(stg[0:nt, 0:288], ps[0:nt, 0:288], reads=[Bp], writes=[Bst])
                k.dma(kv_out(su * 128, nt), stg[0:nt, 0:256], reads=[Bst], eng='pool')
                k.dma(kr_out(su * 128, nt), stg[0:nt, 256:288], reads=[Bst], eng='pool')
            return ckvb, Bckvb

        def expand_kv(src_fn, Bsrc, ncols, tcol):
            for c0 in range(0, ncols, 512):
                n = min(512, ncols - c0)
                for p in range(8):
                    ps, Bp = proj.next()
                    k.mm(ps[:, :n], [(wkn2[:, k2, p * 128:(p + 1) * 128], src_fn(k2, c0, n)) for k2 in range(2)], reads=[Bwkn, Bsrc], wbuf=Bp)
                    k.copy_rr(knT[:, p, tcol + c0:tcol + c0 + n], ps[:, :n], reads=[Bp], writes=[BknT])
            for c0 in range(0, ncols, 128):
                n = min(128, ncols - c0)
                kt = (tcol + c0) // 128
                for half in range(2):
                    ps, Bp = proj.next()
                    k.mm(ps[0:n, :], [(src_fn(k2, c0, n), wvv2[:, k2, half * 512:(half + 1) * 512]) for k2 in range(2)], reads=[Bwvv, Bsrc], wbuf=Bp)
                    k.copy_rr(VC[0:n, kt, half * 8:(half + 1) * 8, 0:64], ps[0:n, :].rearrange("p (h d) -> p h d", h=8), reads=[Bp], writes=[BVC])

        def q_proj(w, T, qcol, pcol):
            cs, Bcs = w['cos']
            sn, Bsn = w['sin']
            k.dma(cs[:, :T], I['c_rope_cos'][:, pcol:pcol + T], writes=[Bcs])
            k.dma(sn[:, :T], I['c_rope_sin'][:, pcol:pcol + T], writes=[Bsn])
            qnT, BqnT = w['qnT']
            qrT, BqrT = w['qrT']
            for p in range(8):
                ps, Bp = proj.next()
                k.mm(ps[:, :T], [(wqn2[:, k3, p * 128:(p + 1) * 128], cqn[:, k3, qcol:qcol + T]) for k3 in range(3)], reads=[Bwqn, Bcqn], wbuf=Bp)
                k.copy_rr(qnT[0:64, 2 * p, :T], ps[0:64, :T], reads=[Bp], writes=[BqnT])
                k.copy_rr(qnT[64:128, 2 * p + 1, :T], ps[64:128, :T], reads=[Bp], writes=[BqnT])
            t1, Bt1 = w['t1']
            t2, Bt2 = w['t2']
            for ch in range(6):
                nr = 96 if ch < 5 else 32
                ps, Bp = proj.next()
                k.mm(ps[0:nr, :T], [(wqr2[:, k3, ch * 96:ch * 96 + nr], cqn[:, k3, qcol:qcol + T]) for k3 in range(3)], reads=[Bwqr, Bcqn], wbuf=Bp)
                k.tt(t1[0:nr, :T], ps[0:nr, :T], cs[0:nr, :T], ALU.mult, reads=[Bp, Bcs], writes=[Bt1])
                ps, Bp = proj.next()
                k.mm(ps[0:nr, :T], [(wqrr2[:, k3, ch * 96:ch * 96 + nr], cqn[:, k3, qcol:qcol + T]) for k3 in range(3)], reads=[Bwqrr, Bcqn], wbuf=Bp)
                k.tt(t2[0:nr, :T], ps[0:nr, :T], sn[0:nr, :T], ALU.mult, reads=[Bp, Bsn], writes=[Bt2])
                k.tt(qrT[0:nr, ch, :T], t1[0:nr, :T], t2[0:nr, :T], ALU.add, reads=[Bt1, Bt2], writes=[BqrT])

        def heads(w, nq, qc0, blocks_fn):
            qnT, BqnT = w['qnT']
            qrT, BqrT = w['qrT']
            oT, BoT = w['oT']
            jobs = []
            for h in range(16):
                pb, p = (h % 2) * 64, h // 2
                g, ch = h % 3, h // 3
                qa = qnT[:, h, qc0:qc0 + nq]
                qb = qrT[:, ch, qc0:qc0 + nq]
                Oo = Oring.next()
                jobs.append(dict(blocks=blocks_fn(h, pb, p, g, qa, qb, [BknT, Bkrr3, BqnT, BqrT]), O=Oo, M=65, D=None,
                                 epi=(lambda Oo=Oo, pb=pb, p=p: epilogue_std(w, bc_ps, Oo[0], Oo[1], nq,
                                                                             oT[pb:pb + 64, p, qc0:qc0 + nq], BoT))))
            attn_run(w, jobs, nq, scale, la=4, pair=False)

        def out_proj(w, T, t0, load_x=True):
            oT, BoT = w['oT']
            for c in range(8):
                xc, Bxc = w['xc'].next()
                k.dma(xc[:, :T], xT_pc[:, c, t0:t0 + T], reads=[B_xT], writes=[Bxc])
                ps, Bp = proj.next()
                k.mm(ps[:, :T], [(wout[:, kc, c * 128:(c + 1) * 128], oT[:, kc, :T]) for kc in range(8)], reads=[Bwout, BoT], wbuf=Bp)
                k.stt(xc[:, :T], ps[:, :T], 1.0, xc[:, :T], ALU.mult, ALU.add, reads=[Bp, Bxc], writes=[Bxc])
                k.dma(xT2_pc[:, c, t0:t0 + T], xc[:, :T], reads=[Bxc], writes=[B_xT2], eng='pool')

        def mk_block(nk, kcols, vkt, h, pb, p, g, qa, qb, rd, masks):
            return dict(nk=nk, qk=[(knT[:, p, kcols[0]:kcols[1]], qa), (krr3[:, g, kcols[0]:kcols[1]], qb)],
                        rd=rd, v=VC[0:nk, vkt, h, :], Bv=BVC, bias=('c', 0.0, []), masks=masks)

        for s in range(NSEQ):
            tb = s * SEQ
            m1 = sb.mark()
            w = work1(512)
            nxt = p1_load(w, tb, 512, 0)
            for tt in range(4):
                ld = nxt
                if tt < 3:
                    nxt = p1_load(w, tb + (tt + 1) * 512, 512, (tt + 1) * 512)
                ckvb, Bckvb = pass1_tile(w, ld, 512, tt * 512, (lambda kf, Bkf, tt=tt: kr_write(tt * 512, 512, kf, Bkf)),
                                         lambda r0, n, tt=tt: O['p_c_kv'][s, tt * 512 + r0:tt * 512 + r0 + n, :],
                                         lambda r0, n, tt=tt: O['p_c_kr'][s, tt * 512 + r0:tt * 512 + r0 + n, :])
                expand_kv(lambda k2, c0, n: ckvb[:, k2, c0:c0 + n], Bckvb, 512, tt * 512)
            sb.release(m1)
            w = work2(512)
            for qt in range(4):
                q0 = qt * 512
                q_proj(w, 512, q0, q0)

                def blocks_fn(h, pb, p, g, qa, qb, rd, q0=q0):
                    bl = []
                    for k0 in range(0, q0 + 512, 128):
                        dk = k0 - q0
                        c0 = max(0, dk)
                        n_ = 512 - c0
                        masks = [([[64, n_ // 64], [0, 64]], ALU.is_gt, 64, -1)] if dk >= 0 else []
                        blk = mk_block(128, (k0, k0 + 128), k0 // 128, h, pb, p, g, qa, qb, rd, masks)
                        blk['cols'] = (c0, 512)
                        bl.append(blk)
                    return bl
                heads(w, 512, 0, blocks_fn)
                out_proj(w, 512, tb + q0)
            sb.release(m1)
        m1 = sb.mark()
        w = work1(32)
        krn, Bkrn = k.tile('krn', [96, 32], BF16)
        ckT, BckT = k.tile('ckT', [128, 2, PAST + TS], BF16)
        w2 = work2(32)
        ckvb, Bckvb = pass1_tile(w, p1_load(w, SOFF, 32, SEQ), 32, 0, (lambda kf, Bkf: k.copy('act', krn[:, :], kf[:, :32], reads=[Bkf], writes=[Bkrn])),
                                 lambda r0, n: O['s_c_kv'][r0:r0 + n, :], lambda r0, n: O['s_c_kr'][r0:r0 + n, :])
        q_proj(w2, 32, 0, SEQ)
        mxs = dict(stg=w['stg'], proj=proj)
        for s in range(NSEQ):
            load_cache_T(mxs, I['cache_c_kv'][s], PAST, ckT, BckT, nchunk=2, width=128)
            k.copy('dve', ckT[:, :, PAST:PAST + TS], ckvb[:, :, s * TS:(s + 1) * TS], reads=[Bckvb], writes=[BckT])
            expand_kv(lambda k2, c0, n: ckT[:, k2, c0:c0 + n], BckT, PAST + TS, 0)
            for tt in range(PAST // 128):
                stg, Bst = w['stg'].next()
                k.dma(stg[:, 0:96].rearrange("p (g e) -> p g e", g=3),
                      dram_ap(I['cache_c_kr'], (s * PAST + tt * 128) * C_ROPE, [[C_ROPE, 128], [0, 3], [1, C_ROPE]]), writes=[Bst])
                ps, Bp = proj.next()
                k.tr(ps[0:96, 0:128], stg[:, 0:96], ident[:, :], reads=[Bst, B_ident], wbuf=Bp)
                kr_write(tt * 128, 128, ps[0:96, 0:128], Bp)
            kr_write(PAST, TS, krn[:, s * TS:(s + 1) * TS], Bkrn)

            def blocks_fn(h, pb, p, g, qa, qb, rd):
                bl = [mk_block(128, (kt * 128, (kt + 1) * 128), kt, h, pb, p, g, qa, qb, rd, []) for kt in range(PAST // 128)]
                bl.append(mk_block(TS, (PAST, PAST + TS), PAST // 128, h, pb, p, g, qa, qb, rd, []))
                return bl
            heads(w2, TS, s * TS, blocks_fn)
        out_proj(w2, 32, SOFF, load_x=False)
        RES['pc'], RES['B'] = xT2_pc, B_xT2
        RES['flat'] = xT2
        sb.release(m1)
        sb.release(m)

    if 'p0' in phases:
        toe_prep()
        phase0()
    if 'ffn1_0' in phases:
        ffn_phase('ffn1', 0)
    if 'mixA' in phases or 'mix0' in phases:
        mixA()
    if 'mixB' in phases or 'mix0' in phases:
        mixB()
    if 'ffn2_0' in phases:
        ffn_phase('ffn2', 0)
    if 'ffn1_1' in phases:
        ffn_phase('ffn1', 1)
    if 'mix1' in phases:
        mixC()
    if 'ffn2_1' in phases:
        ffn_phase('ffn2', 1)
    if debug:
        k.dma(O['dbg_xT'], RES['flat'], reads=[RES['B']])
    phaseZ()
    P.final_wait_all('sp')
    P.emit()
    return nc, sorted(I.keys()), sorted(O.keys()), sb.peak


def _t5_bucket_host(rel):
    import jax
    import jax.numpy as jnp
    with jax.default_device(jax.devices('cpu')[0]):
        rel = jnp.asarray(np.asarray(rel, dtype=np.int32))
        nb = 16
        max_exact = nb // 2
        ret = jnp.where(rel > 0, nb, 0)
        n = jnp.abs(rel)
        n_f = jnp.maximum(n, 1).astype(jnp.float32)
        large = max_exact + (jnp.log(n_f / max_exact) / math.log(128 / max_exact) * (nb - max_exact)).astype(jnp.int32)
        large = jnp.minimum(large, nb - 1)
        return np.asarray(ret + jnp.where(n < max_exact, n, large))


def host_constants():
    c = {}
    c['c_ident'] = np.eye(128, dtype=np.float32)
    half = C_ROPE // 2
    inv = (10000.0 ** (-np.arange(half, dtype=np.float32) / half)).astype(np.float32)
    pos = np.concatenate([np.arange(SEQ), PAST + np.arange(TS), PAST + np.arange(TS)]).astype(np.float32)
    ang = pos[None, :] * inv[(np.arange(96) % 32) % half][:, None]
    c['c_rope_cos'] = np.cos(ang.astype(np.float32)).astype(np.float32)
    c['c_rope_sin'] = np.sin(ang.astype(np.float32)).astype(np.float32)
    u = np.arange(LT)
    rel = np.where(u < WM, 384 - u, 384 + LT - u)
    bidx = np.asarray(_t5_bucket_host(rel))
    oh = np.zeros((32, LT), dtype=np.float32)
    oh[bidx, u] = 1.0
    c['c_t5_onehot'] = oh
    return c


_CACHE = {}


def kernel(**inputs):
    n = 8
    if 'nc' not in _CACHE:
        _CACHE['nc'] = build_program()
    nc, in_names, out_names, _ = _CACHE['nc']
    consts = host_constants()
    f = lambda a: np.ascontiguousarray(np.asarray(a, dtype=np.float32))
    in_maps = []
    for i in range(n):
        b0, b1 = i * NSEQ, (i + 1) * NSEQ
        mp = {}
        mp['xp'] = f(inputs['x_prompt'][b0:b1]).reshape(NSEQ * SEQ, D)
        mp['xs'] = f(inputs['x_sample'][b0:b1]).reshape(NSEQ * TS, D)
        mp['cache_a_k'] = f(inputs['cache_a_k'][0, b0:b1]).reshape(NSEQ, 512, 512)
        mp['cache_a_v'] = f(inputs['cache_a_v'][0, b0:b1]).reshape(NSEQ, 512, 512)
        mp['cache_b_k'] = f(inputs['cache_b_k'][0, b0:b1]).reshape(NSEQ, PAST, 512)
        mp['cache_b_v'] = f(inputs['cache_b_v'][0, b0:b1]).reshape(NSEQ, PAST, 512)
        mp['cache_c_kv'] = f(inputs['cache_c_kv'][0, b0:b1])
        mp['cache_c_kr'] = f(inputs['cache_c_kr'][0, b0:b1])
        for nm in ('t5_bias', 'ffn1_norm', 'mix_norm', 'ffn2_norm', 'ffn1_w_gu', 'ffn2_w_gu', 'ffn1_w_down', 'ffn2_w_down'):
            mp[nm] = f(inputs[nm])
        for nm in ('e_w_in', 'a_rel_bias', 'e_w_out', 'c_w_in', 'c_w_q_up', 'c_w_kv_up', 'c_w_out'):
            mp[nm] = f(inputs[nm][0])
        for nm in ('b_lambda_q1', 'b_lambda_k1', 'b_lambda_q2', 'b_lambda_k2', 'b_subln', 'c_q_norm', 'c_kv_norm'):
            mp[nm] = f(inputs[nm]).reshape(1, -1)
        mp['final_norm'] = f(inputs['final_norm']).reshape(1, D)
        mp.update(consts)
        in_maps.append({kk: mp[kk] for kk in in_names})
    res = run_bass_kernel_spmd(nc, in_maps, core_ids=list(range(n)))
    R = res.results
    B = 16

    def cat(name, shape):
        return np.concatenate([np.asarray(R[i][name], dtype=np.float32) for i in range(n)], axis=0).reshape(shape)
    y_prompt = cat('y_prompt', (B, SEQ, D))
    y_sample = cat('y_sample', (B, TS, D))
    p_a_k = cat('p_a_k', (1, B, 512, A_H, A_D))
    p_a_v = cat('p_a_v', (1, B, 512, A_H, A_D))
    p_b_k = cat('p_b_k', (1, B, SEQ, B_H, 2 * B_D))
    p_b_v = cat('p_b_v', (1, B, SEQ, B_H, 2 * B_D))
    p_c_kv = cat('p_c_kv', (1, B, SEQ, C_KVL))
    p_c_kr = cat('p_c_kr', (1, B, SEQ, C_ROPE))
    s_a_k = cat('s_a_k', (1, B, TS, A_H, A_D))
    s_a_v = cat('s_a_v', (1, B, TS, A_H, A_D))
    s_b_k = cat('s_b_k', (1, B, TS, B_H, 2 * B_D))
    s_b_v = cat('s_b_v', (1, B, TS, B_H, 2 * B_D))
    s_c_kv = cat('s_c_kv', (1, B, TS, C_KVL))
    s_c_kr = cat('s_c_kr', (1, B, TS, C_ROPE))
    return (y_prompt, y_sample, p_a_k, p_a_v, p_b_k, p_b_v, p_c_kv, p_c_kr,
            s_a_k, s_a_v, s_b_k, s_b_v, s_c_kv, s_c_kr)
```

```python
import math
import os
import numpy as np
import concourse.bass as bass
import concourse.mybir as mybir
from concourse.bass_utils import run_bass_kernel_spmd

F32 = mybir.dt.float32
BF16 = mybir.dt.bfloat16
AF = mybir.ActivationFunctionType
ALU = mybir.AluOpType

ENGS = ('pe', 'act', 'dve', 'pool', 'sp')

D = 1024
SEQ = 2048
NSEQ = 2
TS = 16
PAST = 1024
NTOK = NSEQ * SEQ + NSEQ * TS
SOFF = NSEQ * SEQ
DFF = 2816
NJ = DFF // 128
EPS = 1e-6
CHUNK = 64
A_H, A_D = 8, 64
B_H, B_D = 4, 64
C_H, C_NOPE, C_ROPE, C_V = 16, 64, 32, 64
C_QL, C_KVL = 384, 256
LT = 1152
WM = 1024


class Buf:
    __slots__ = ('name', 'w', 'r', 'x')

    def __init__(self, name='', x=False):
        self.name = name
        self.w = None
        self.r = {}
        self.x = x


class Prog:
    def __init__(self, nc, n_dma_sems=(('sp', 48), ('pool', 24))):
        self.nc = nc
        self.q = {e: [] for e in ENGS}
        self.cnt = {e: 0 for e in ENGS}
        self.sem = {e: nc.alloc_semaphore("s_" + e) for e in ENGS}
        self.seen = {e: {} for e in ENGS}
        self.dsem, self.dcnt, self.dpool, self.drr = {}, {}, {}, {}
        for e, n in n_dma_sems:
            ids = []
            for i in range(n):
                k = (e, i)
                self.dsem[k] = nc.alloc_semaphore("d_%s_%d" % (e, i))
                self.dcnt[k] = 0
                ids.append(k)
            self.dpool[e] = ids
            self.drr[e] = 0
        self.nops = 0
        self.hist = {}

    def _need(self, eng, ev, waits):
        if ev is None:
            return
        kind, key, val = ev
        if kind == 'e' and key == eng:
            if eng == 'pe':
                return
            if val < self.cnt[eng] - 1:
                return
        k = (kind, key)
        if self.seen[eng].get(k, 0) >= val:
            return
        self.seen[eng][k] = val
        waits.append(ev)
        snap = self.hist.get(ev)
        if snap is not None:
            mine = self.seen[eng]
            for k2, v2 in snap.items():
                if mine.get(k2, 0) < v2:
                    mine[k2] = v2

    def _deps(self, eng, reads, writes):
        waits = []
        for b in reads:
            self._need(eng, b.w, waits)
            if b.x:
                for ev in b.r.values():
                    self._need(eng, ev, waits)
        for b in writes:
            self._need(eng, b.w, waits)
            for ev in b.r.values():
                self._need(eng, ev, waits)
        return waits

    def _mark(self, ev, rkey, reads, writes):
        for b in reads:
            b.r[rkey] = ev
        for b in writes:
            b.w = ev
            b.r = {}

    def op(self, eng, fn, reads=(), writes=()):
        waits = self._deps(eng, reads, writes)
        self.cnt[eng] += 1
        ev = ('e', eng, self.cnt[eng])
        snap = dict(self.seen[eng])
        snap[('e', eng)] = self.cnt[eng] - 1
        self.hist[ev] = snap
        self.q[eng].append((waits, fn, (self.sem[eng], 1)))
        self._mark(ev, ('e', eng), reads, writes)
        self.nops += 1
        return ev

    def dma(self, eng, out, in_, reads=(), writes=(), **kw):
        pool = self.dpool[eng]
        k = pool[self.drr[eng] % len(pool)]
        self.drr[eng] += 1
        waits = self._deps(eng, reads, writes)
        if self.dcnt[k] > 0:
            self._need(eng, ('d', k, self.dcnt[k]), waits)
        self.dcnt[k] += 16
        ev = ('d', k, self.dcnt[k])
        snap = dict(self.seen[eng])
        snap.pop(('e', eng), None)
        self.hist[ev] = snap

        def fn(e, out=out, in_=in_, kw=kw):
            return e.dma_start(out=out, in_=in_, **kw)
        self.q[eng].append((waits, fn, (self.dsem[k], 16)))
        self._mark(ev, ('d', k), reads, writes)
        self.nops += 1
        return ev

    def barrier(self):
        evs = [('e', e, self.cnt[e]) for e in ENGS if self.cnt[e] > 0]
        evs += [('d', kk, v) for kk, v in self.dcnt.items() if v > 0]
        for eng in ENGS:
            waits = []
            for ev in evs:
                if ev[0] == 'e' and ev[1] == eng:
                    continue
                self._need(eng, ev, waits)
            if waits:
                self.q[eng].append((waits, None, None))

    def _sem_of(self, ev):
        kind, key, val = ev
        return (self.sem[key] if kind == 'e' else self.dsem[key]), val

    def final_wait_all(self, eng='sp'):
        waits = []
        for e in ENGS:
            if e != eng and self.cnt[e] > 0:
                waits.append(('e', e, self.cnt[e]))
        for k, v in self.dcnt.items():
            if v > 0:
                waits.append(('d', k, v))
        self.q[eng].append((waits, None, None))

    def emit(self):
        nc = self.nc
        prog = self

        def run(e, name):
            for waits, fn, inc in prog.q[name]:
                for ev in waits:
                    s, v = prog._sem_of(ev)
                    e.wait_ge(s, v)
                if fn is not None:
                    ins = fn(e)
                    ins.then_inc(inc[0], inc[1])

        with nc.Block() as block:
            @block.tensor
            def _(e):
                run(e, 'pe')

            @block.scalar
            def _(e):
                run(e, 'act')

            @block.vector
            def _(e):
                run(e, 'dve')

            @block.gpsimd
            def _(e):
                run(e, 'pool')

            @block.sync
            def _(e):
                run(e, 'sp')


class SB:
    def __init__(self, nc, base=16512, limit=229344):
        self.nc = nc
        self.off = base
        self.limit = limit
        self.n = 0
        self.peak = 0
        self.on_release = None

    def alloc(self, name, shape, dtype):
        esz = 2 if dtype == BF16 else 4
        free = int(np.prod(shape[1:])) * esz
        self.off = (self.off + 63) // 64 * 64
        self.n += 1
        t = self.nc.alloc_sbuf_tensor_at("%s_%d" % (name, self.n), list(shape), dtype, offset=self.off)
        self.off += free
        self.peak = max(self.peak, self.off)
        assert self.off <= self.limit, ("SBUF overflow", name, self.off, self.limit)
        return t

    def mark(self):
        return self.off

    def release(self, m):
        self.off = m
        if self.on_release is not None:
            self.on_release()


class Ring:
    def __init__(self, items):
        self.items = items
        self.i = 0

    def next(self):
        it = self.items[self.i % len(self.items)]
        self.i += 1
        return it


class K:
    def __init__(self, nc):
        self.nc = nc
        self.P = Prog(nc)
        self.sb = SB(nc)
        self.sb.on_release = self.P.barrier
        self.cp_i = 0
        self.act_every = 2

    def mm(self, out, pairs, reads, wbuf, start=True, stop=True):
        def fn(e, pairs=list(pairs), out=out):
            n = len(pairs)
            ins = None
            for i, (l, r) in enumerate(pairs):
                ins = e.matmul(out, l, r, start=(start and i == 0), stop=(stop and i == n - 1))
            return ins
        return self.P.op('pe', fn, reads=reads, writes=[wbuf])

    def mm_multi(self, items, reads, wbufs):
        def fn(e, items=list(items)):
            ins = None
            for (o, l, r, st_, sp_) in items:
                ins = e.matmul(o, l, r, start=st_, stop=sp_)
            return ins
        return self.P.op('pe', fn, reads=reads, writes=wbufs)

    def tr(self, out, in_, ident, reads, wbuf):
        return self.P.op('pe', lambda e: e.transpose(out, in_, ident), reads=reads, writes=[wbuf])

    def trs(self, items, reads, wbuf):
        def fn(e, items=list(items)):
            ins = None
            for (o, i, idn) in items:
                ins = e.transpose(o, i, idn)
            return ins
        return self.P.op('pe', fn, reads=reads, writes=[wbuf])

    def act(self, out, in_, func, reads, writes, scale=1.0, bias=0.0, accum_out=None):
        def fn(e):
            if accum_out is not None:
                return e.activation(out=out, in_=in_, func=func, bias=bias, scale=scale, accum_out=accum_out)
            return e.activation(out=out, in_=in_, func=func, bias=bias, scale=scale)
        return self.P.op('act', fn, reads=reads, writes=writes)

    def stt(self, out, in0, scalar, in1, op0, op1, reads, writes):
        return self.P.op('dve', lambda e: e.scalar_tensor_tensor(out=out, in0=in0, scalar=scalar, in1=in1, op0=op0, op1=op1),
                         reads=reads, writes=writes)

    def ts(self, out, in0, s1, s2, op0, op1, reads, writes, eng='dve'):
        return self.P.op(eng, lambda e: e.tensor_scalar(out=out, in0=in0, scalar1=s1, scalar2=s2, op0=op0, op1=op1),
                         reads=reads, writes=writes)

    def tt(self, out, in0, in1, op, reads, writes, eng='dve'):
        return self.P.op(eng, lambda e: e.tensor_tensor(out=out, in0=in0, in1=in1, op=op), reads=reads, writes=writes)

    def recip(self, out, in_, reads, writes):
        self.act(out, in_, AF.Ln, reads, writes)
        return self.act(out, out, AF.Exp, list(writes), writes, scale=-1.0)

    def rstd(self, out, in_, reads, writes, scale, eps_ap):
        self.act(out, in_, AF.Ln, reads, writes, scale=scale, bias=eps_ap)
        return self.act(out, out, AF.Exp, list(writes), writes, scale=-0.5)

    def recip_dve(self, out, in_, reads, writes):
        return self.P.op('dve', lambda e: e.reciprocal(out=out, in_=in_), reads=reads, writes=writes)

    def copy(self, eng, out, in_, reads, writes):
        if eng == 'act':
            return self.act(out, in_, AF.Copy, reads, writes)
        return self.P.op(eng, lambda e: e.tensor_copy(out=out, in_=in_), reads=reads, writes=writes)

    def copy_rr(self, out, in_, reads, writes):
        self.cp_i += 1
        n = self.act_every
        return self.copy('act' if (n > 0 and self.cp_i % n == 0) else 'dve', out, in_, reads, writes)

    def memset(self, eng, ap, val, writes):
        return self.P.op(eng, lambda e: e.memset(ap, val), writes=writes)

    def dma(self, out, in_, reads=(), writes=(), eng='sp', **kw):
        return self.P.dma(eng, out, in_, reads=reads, writes=writes, **kw)

    def tile(self, name, shape, dtype):
        return self.sb.alloc(name, shape, dtype), Buf(name)

    def ring(self, name, n, shape, dtype):
        return Ring([self.tile("%s%d" % (name, i), shape, dtype) for i in range(n)])


def dram_ap(t, offset, pattern):
    return bass.AP(t.tensor if hasattr(t, 'tensor') else t, offset, [list(p) for p in pattern])


def build_program(phases=('p0', 'ffn1_0', 'mix0', 'ffn2_0', 'ffn1_1', 'mix1', 'ffn2_1'), debug=False):
    nc = bass.Bass("TRN2", target_bir_lowering=False)
    k = K(nc)
    P = k.P
    sb = k.sb

    def din(name, shape):
        return nc.dram_tensor(name, list(shape), F32, kind="ExternalInput").ap()

    def dout(name, shape):
        return nc.dram_tensor(name, list(shape), F32, kind="ExternalOutput").ap()

    I = {}
    I['xp'] = din('xp', [NSEQ * SEQ, D])
    I['xs'] = din('xs', [NSEQ * TS, D])
    I['cache_a_k'] = din('cache_a_k', [NSEQ, 512, 512])
    I['cache_a_v'] = din('cache_a_v', [NSEQ, 512, 512])
    I['cache_b_k'] = din('cache_b_k', [NSEQ, PAST, 512])
    I['cache_b_v'] = din('cache_b_v', [NSEQ, PAST, 512])
    I['cache_c_kv'] = din('cache_c_kv', [NSEQ, PAST, C_KVL])
    I['cache_c_kr'] = din('cache_c_kr', [NSEQ, PAST, C_ROPE])
    I['t5_bias'] = din('t5_bias', [32, 4])
    for nm in ('ffn1_norm', 'mix_norm', 'ffn2_norm'):
        I[nm] = din(nm, [2, D])
    for nm in ('ffn1_w_gu', 'ffn2_w_gu'):
        I[nm] = din(nm, [2, D, 2 * DFF])
    for nm in ('ffn1_w_down', 'ffn2_w_down'):
        I[nm] = din(nm, [2, DFF, D])
    I['e_w_in'] = din('e_w_in', [D, 3072])
    I['a_rel_bias'] = din('a_rel_bias', [8, 129])
    for nm in ('b_lambda_q1', 'b_lambda_k1', 'b_lambda_q2', 'b_lambda_k2'):
        I[nm] = din(nm, [1, 64])
    I['b_subln'] = din('b_subln', [1, 128])
    I['e_w_out'] = din('e_w_out', [D, D])
    I['c_w_in'] = din('c_w_in', [D, 672])
    I['c_q_norm'] = din('c_q_norm', [1, C_QL])
    I['c_kv_norm'] = din('c_kv_norm', [1, C_KVL])
    I['c_w_q_up'] = din('c_w_q_up', [C_QL, 1536])
    I['c_w_kv_up'] = din('c_w_kv_up', [C_KVL, 2048])
    I['c_w_out'] = din('c_w_out', [D, D])
    I['final_norm'] = din('final_norm', [1, D])
    I['c_ident'] = din('c_ident', [128, 128])
    I['c_rope_cos'] = din('c_rope_cos', [96, SEQ + 32])
    I['c_rope_sin'] = din('c_rope_sin', [96, SEQ + 32])
    I['c_t5_onehot'] = din('c_t5_onehot', [32, LT])

    O = {}
    O['y_prompt'] = dout('y_prompt', [NSEQ * SEQ, D])
    O['y_sample'] = dout('y_sample', [NSEQ * TS, D])
    O['p_a_k'] = dout('p_a_k', [NSEQ, 512, 512])
    O['p_a_v'] = dout('p_a_v', [NSEQ, 512, 512])
    O['p_b_k'] = dout('p_b_k', [NSEQ, SEQ, 512])
    O['p_b_v'] = dout('p_b_v', [NSEQ, SEQ, 512])
    O['p_c_kv'] = dout('p_c_kv', [NSEQ, SEQ, C_KVL])
    O['p_c_kr'] = dout('p_c_kr', [NSEQ, SEQ, C_ROPE])
    O['s_a_k'] = dout('s_a_k', [NSEQ * TS, 512])
    O['s_a_v'] = dout('s_a_v', [NSEQ * TS, 512])
    O['s_b_k'] = dout('s_b_k', [NSEQ * TS, 512])
    O['s_b_v'] = dout('s_b_v', [NSEQ * TS, 512])
    O['s_c_kv'] = dout('s_c_kv', [NSEQ * TS, C_KVL])
    O['s_c_kr'] = dout('s_c_kr', [NSEQ * TS, C_ROPE])
    if debug:
        O['dbg_xT'] = dout('dbg_xT', [8, 128, NTOK])

    xT = nc.dram_tensor("xT_scr", [8, 128, NTOK], F32).ap()
    xT_pc = xT.rearrange("c p t -> p c t")
    B_xT = Buf('xT')
    RES = {'pc': xT_pc, 'B': B_xT, 'flat': xT}

    ident, B_ident = k.tile('ident', [128, 128], F32)
    k.dma(ident[:, :], I['c_ident'], writes=[B_ident])
    ones_bf, B_ones = k.tile('ones', [128, 128], BF16)
    k.memset('pool', ones_bf[:, :], 1.0, [B_ones])
    gains, B_gains = k.tile('gains', [128, 6, 8], F32)
    gi = 0
    for nm in ('ffn1_norm', 'mix_norm', 'ffn2_norm'):
        for l in range(2):
            k.dma(gains[:, gi, :], I[nm][l].rearrange("(c p) -> p c", p=128), writes=[B_gains],
                  allow_slow_non_contiguous=True)
            gi += 1
    G_IDX = {('ffn1', 0): 0, ('ffn1', 1): 1, ('mix', 0): 2, ('mix', 1): 3, ('ffn2', 0): 4, ('ffn2', 1): 5}

    psum = [(nc.alloc_psum_tensor("ps%d" % i, [128, 512], F32), Buf("ps%d" % i, x=True)) for i in range(8)]

    def phase0():
        m = sb.mark()
        xin = k.ring('p0in', 2, [128, 4, D], F32)
        xtr = k.ring('p0tr', 2, [128, 8, 512], F32)
        pr = Ring(psum[0:4])
        xp_v = I['xp'].rearrange("(g s p) d -> g p s d", s=4, p=128)
        for g in range(NSEQ * SEQ // 512):
            ti, Bi = xin.next()
            k.dma(ti[:, :, :], xp_v[g], writes=[Bi])
            to, Bo = xtr.next()
            for c in range(8):
                ps, Bp = pr.next()
                k.trs([(ps[:, s * 128:(s + 1) * 128], ti[:, s, c * 128:(c + 1) * 128], ident[:, :]) for s in range(4)],
                      reads=[Bi, B_ident], wbuf=Bp)
                k.copy_rr(to[:, c, :], ps[:, :], reads=[Bp], writes=[Bo])
            k.dma(xT_pc[:, :, g * 512:(g + 1) * 512], to[:, :, :], reads=[Bo], writes=[B_xT], eng='pool')
        ti, Bi = xin.next()
        k.dma(ti[0:32, 0, :], I['xs'], writes=[Bi])
        to, Bo = xtr.next()
        ps, Bp = pr.next()
        k.trs([(ps[:, c * 32:(c + 1) * 32], ti[0:32, 0, c * 128:(c + 1) * 128], ident[0:32, 0:32]) for c in range(8)],
              reads=[Bi, B_ident], wbuf=Bp)
        k.copy_rr(to[:, :, 0:32], ps[:, 0:256].rearrange("p (c t) -> p c t", c=8), reads=[Bp], writes=[Bo])
        k.dma(xT_pc[:, :, SOFF:SOFF + 32], to[:, :, 0:32], reads=[Bo], writes=[B_xT], eng='pool')
        sb.release(m)

    def rms_fm(x_t, Bx, nch, T, g_ap_fn, xn_t, Bxn, sq_ring, ps_ss, rs_t, Brs, dim, gbuf=None):
        pss, Bpss = ps_ss
        gbuf = B_gains if gbuf is None else gbuf
        for c in range(nch):
            sq, Bsq = sq_ring.next()
            k.act(sq[:, :T], x_t[:, c, :T], AF.Square, reads=[Bx], writes=[Bsq])
            k.mm(pss[:, :T], [(ones_bf[:, :], sq[:, :T])], reads=[B_ones, Bsq], wbuf=Bpss, start=(c == 0), stop=(c == nch - 1))
        k.rstd(rs_t[:, :T], pss[:, :T], [Bpss, B_eps], [Brs], 1.0 / dim, eps_t[:, 0:1])
        for c in range(nch):
            dst = xn_t(c) if callable(xn_t) else xn_t[:, c, :T]
            k.stt(dst, x_t[:, c, :T], g_ap_fn(c), rs_t[:, :T], ALU.mult, ALU.mult,
                  reads=[Bx, Brs, gbuf], writes=[Bxn])

    eps_t, B_eps = k.tile('eps', [128, 1], F32)
    k.memset('pool', eps_t[:, :], EPS, [B_eps])

    FT = 512

    def ffn_phase(which, layer):
        m = sb.mark()
        k.act_every = 2
        wgu_d = I[which + '_w_gu'][layer]
        wd_d = I[which + '_w_down'][layer]
        gidx = G_IDX[(which, layer)]
        wg, Bwg = k.tile('wg', [128, 8, 2 * DFF], BF16)
        wd, Bwd = k.tile('wd', [128, NJ, D], BF16)
        wgu_v = wgu_d.rearrange("(c p) n -> p c n", p=128)
        Bwg_g = [None] * NJ
        Bwg_u = [None] * NJ
        for j0 in range(0, NJ, 4):
            j1 = min(NJ, j0 + 4)
            bg, bu = Buf('wg_g'), Buf('wg_u')
            for j in range(j0, j1):
                Bwg_g[j], Bwg_u[j] = bg, bu
            k.dma(wg[:, :, j0 * 128:j1 * 128], wgu_v[:, :, j0 * 128:j1 * 128], writes=[bg], eng='pool')
            k.dma(wg[:, :, DFF + j0 * 128:DFF + j1 * 128], wgu_v[:, :, DFF + j0 * 128:DFF + j1 * 128],
                  writes=[bu], eng='pool')
        wd_v = wd_d.rearrange("(j p) n -> p j n", p=128)
        Bwd_blk = [Buf('wdblk') for _ in range(2)]
        k.dma(wd[:, 0:11, :], wd_v[:, 0:11, :], writes=[Bwd_blk[0]], eng='pool')
        k.dma(wd[:, 11:22, :], wd_v[:, 11:22, :], writes=[Bwd_blk[1]], eng='pool')

        xr = k.ring('fx', 2, [128, 8, FT], F32)
        xn, Bxn = k.tile('fxn', [128, 8, FT], BF16)
        sqr = k.ring('fsq', 3, [128, FT], BF16)
        rs, Brs = k.tile('frs', [128, FT], F32)
        actt, Bact = k.tile('fact', [128, NJ, FT], BF16)
        Bact_j = [Buf('actj') for _ in range(NJ)]
        sgr = k.ring('fsg', 2, [128, FT], F32)
        ps_g = Ring(psum[0:2])
        ps_u = Ring(psum[2:4])
        ps_y = Ring(psum[4:6])
        ps_ss = psum[6]
        tiles = [(i * FT, FT) for i in range(NTOK // FT)]
        if NTOK % FT:
            tiles.append((NTOK - NTOK % FT, NTOK % FT))
        ntile = len(tiles)

        def load(t):
            t0, T = tiles[t]
            tx, Bx = xr.next()
            k.dma(tx[:, :, :T], RES['pc'][:, :, t0:t0 + T], reads=[RES['B']], writes=[Bx])
            return tx, Bx

        def norm(t, tx, Bx):
            rms_fm(tx, Bx, 8, tiles[t][1], lambda c: gains[:, gidx, c:c + 1], xn, Bxn, sqr, ps_ss, rs, Brs, D)

        cur = load(0)
        norm(0, *cur)
        for t in range(ntile):
            t0, T = tiles[t]
            tx, Bx = cur
            nxt = load(t + 1) if t + 1 < ntile else None
            for j in range(NJ):
                pg, Bpg = ps_g.next()
                pu, Bpu = ps_u.next()
                k.mm(pg[:, :T], [(wg[:, c, j * 128:(j + 1) * 128], xn[:, c, :T]) for c in range(8)],
                     reads=[Bwg_g[j], Bxn], wbuf=Bpg)
                k.mm(pu[:, :T], [(wg[:, c, DFF + j * 128:DFF + (j + 1) * 128], xn[:, c, :T]) for c in range(8)],
                     reads=[Bwg_u[j], Bxn], wbuf=Bpu)
                sg, Bsg = sgr.next()
                k.act(sg[:, :T], pg[:, :T], AF.Silu, reads=[Bpg], writes=[Bsg])
                k.tt(actt[:, j, :T], pu[:, :T], sg[:, :T], ALU.mult, reads=[Bpu, Bsg], writes=[Bact_j[j]])
            if nxt is not None:
                norm(t + 1, *nxt)
            for c in range(8):
                py, Bpy = ps_y.next()
                k.mm(py[:, :T], [(wd[:, j, c * 128:(c + 1) * 128], actt[:, j, :T]) for j in range(NJ)],
                     reads=Bwd_blk + Bact_j, wbuf=Bpy)
                k.stt(tx[:, c, :T], py[:, :T], 0.5, tx[:, c, :T], ALU.mult, ALU.add, reads=[Bpy, Bx], writes=[Bx])
            k.dma(RES['pc'][:, :, t0:t0 + T], tx[:, :, :T], reads=[Bx], writes=[RES['B']], eng='pool')
            cur = nxt
        sb.release(m)

    def phaseZ():
        m = sb.mark()
        gfin, Bgf = k.tile('gfin', [128, D], F32)
        k.dma(gfin[:, :], dram_ap(I['final_norm'], 0, [[0, 128], [1, D]]), writes=[Bgf])
        xin = k.ring('zin', 2, [128, 8, 512], F32)
        xo = k.ring('zo', 3, [128, D], F32)
        junk, Bjunk = k.tile('zjunk', [128, D], BF16)
        st = k.ring('zst', 4, [128, 2], F32)
        pr = Ring(psum[0:6])

        def do_sub(ti, Bi, col0, ntok, dst_ap):
            to, Bo = xo.next()
            for half in range(2):
                ps, Bp = pr.next()
                k.trs([(ps[0:ntok, cc * 128:(cc + 1) * 128], ti[:, half * 4 + cc, col0:col0 + ntok], ident[:, :])
                       for cc in range(4)], reads=[Bi, B_ident], wbuf=Bp)
                k.copy_rr(to[0:ntok, half * 512:(half + 1) * 512], ps[0:ntok, :], reads=[Bp], writes=[Bo])
            s, Bs = st.next()
            k.act(junk[0:ntok, :], to[0:ntok, :], AF.Square, reads=[Bo], writes=[Bjunk, Bs], accum_out=s[0:ntok, 0:1])
            k.act(s[0:ntok, 1:2], s[0:ntok, 0:1], AF.Sqrt, reads=[Bs, B_eps], writes=[Bs], scale=1.0 / D, bias=eps_t[0:ntok, 0:1])
            k.recip_dve(s[0:ntok, 1:2], s[0:ntok, 1:2], reads=[Bs], writes=[Bs])
            k.stt(to[0:ntok, :], to[0:ntok, :], s[0:ntok, 1:2], gfin[0:ntok, :], ALU.mult, ALU.mult,
                  reads=[Bo, Bs, Bgf], writes=[Bo])
            k.dma(dst_ap, to[0:ntok, :], reads=[Bo], eng='pool')

        for g in range(NSEQ * SEQ // 512):
            ti, Bi = xin.next()
            k.dma(ti[:, :, :], RES['pc'][:, :, g * 512:(g + 1) * 512], reads=[RES['B']], writes=[Bi])
            for s_ in range(4):
                r0 = g * 512 + s_ * 128
                do_sub(ti, Bi, s_ * 128, 128, O['y_prompt'][r0:r0 + 128, :])
        ti, Bi = xin.next()
        k.dma(ti[:, :, 0:32], RES['pc'][:, :, SOFF:SOFF + 32], reads=[RES['B']], writes=[Bi])
        do_sub(ti, Bi, 0, 32, O['y_sample'][:, :])
        sb.release(m)

    oTA_scr = nc.dram_tensor("oTA_scr", [128, 4, NTOK], BF16).ap()
    B_oTA = Buf('oTA')
    toe_scr = nc.dram_tensor("toe_scr", [12, 130 * LT], F32).ap()
    B_toe = Buf('toe')
    CAUSAL_PAT = [[64, 8], [0, 64]]
    BAND_PAT = [[-64, 8], [0, 64]]

    def toe_write(v_t, Bv, nh, row0):
        src = v_t[0:nh, :].unsqueeze(1).broadcast_to([nh, 130, LT])
        dst = toe_scr[row0:row0 + nh, :].rearrange("h (r l) -> h r l", l=LT)
        k.dma(dst, src, reads=[Bv], writes=[B_toe])

    def toe_read(nh, row0, masters):
        for h in range(nh):
            T_, BT = masters[h]
            k.dma(T_[:, :], dram_ap(toe_scr, (row0 + h) * 130 * LT, [[LT - 1, 128], [1, WM]]), reads=[B_toe], writes=[BT])

    def toe_prep():
        m = sb.mark()
        arel8, Ba8 = k.tile('arel8', [8, 129], F32)
        k.dma(arel8[:, :], I['a_rel_bias'], writes=[Ba8])
        va, Bva = k.tile('va', [8, LT], F32)
        k.memset('dve', va[:, :], 0.0, [Bva])
        k.ts(va[:, 0:320], va[:, 0:320], arel8[:, 0:1], None, ALU.add, ALU.bypass, reads=[Bva, Ba8], writes=[Bva])
        k.copy('dve', va[:, 320:449], arel8[:, :], reads=[Ba8], writes=[Bva])
        k.ts(va[:, 449:1024], va[:, 449:1024], arel8[:, 128:129], None, ALU.add, ALU.bypass, reads=[Bva, Ba8], writes=[Bva])
        k.ts(va[:, 1024:LT], va[:, 1024:LT], arel8[:, 0:1], None, ALU.add, ALU.bypass, reads=[Bva, Ba8], writes=[Bva])
        toe_write(va, Bva, 8, 0)
        t5s, Bt5s = k.tile('t5s', [32, 4], F32)
        k.dma(t5s[:, :], I['t5_bias'], writes=[Bt5s])
        oneh, Boneh = k.tile('oneh', [32, LT], F32)
        k.dma(oneh[:, :], I['c_t5_onehot'], writes=[Boneh])
        vb, Bvb = k.tile('vb', [4, LT], F32)
        for i in range(3):
            ps, Bp = psum[4 + i]
            k.mm(ps[0:4, 0:384], [(t5s[:, :], oneh[:, i * 384:(i + 1) * 384])], reads=[Bt5s, Boneh], wbuf=Bp)
            k.copy('dve', vb[:, i * 384:(i + 1) * 384], ps[0:4, 0:384], reads=[Bp], writes=[Bvb])
        toe_write(vb, Bvb, 4, 8)
        sb.release(m)

    def mixer_res(nS):
        r = {}
        r['x'] = k.ring('mx', 2, [128, 8, 512], F32)
        r['h'] = k.ring('mh', 2, [128, 8, 512], BF16)
        r['sq'] = k.ring('msq', 3, [128, 512], BF16)
        r['rs'] = k.tile('mrs', [128, 512], F32)
        r['ss'] = psum[7]
        r['proj'] = Ring([psum[7], psum[0], psum[1], psum[2], psum[3]])
        r['S'] = Ring(psum[0:4])
        r['E'] = k.ring('mE', 8, [128, 512], BF16)
        r['Sp'] = k.ring('mSp', 3, [128, 512], F32)
        r['hl'] = k.ring('mhl', 4, [128, 512], BF16)
        r['rd'] = k.ring('mrd', 2, [128, 512], F32)
        r['stg'] = k.ring('mstg', 3, [128, 512], F32)
        return r

    def load_norm(mx, t0, T, gidx):
        tx, Bx = mx['x'].next()
        k.dma(tx[:, :, :T], xT_pc[:, :, t0:t0 + T], reads=[B_xT], writes=[Bx])
        hT, BhT = mx['h'].next()
        rms_fm(tx, Bx, 8, T, lambda c: gains[:, gidx, c:c + 1], hT, BhT, mx['sq'], mx['ss'], mx['rs'][0], mx['rs'][1], D)
        return tx, Bx, hT, BhT

    def proj_fm(mx, w, Bw, col0, hT, BhT, T, dst_ap, Bdst, mrows=128):
        ps, Bp = mx['proj'].next()
        k.mm(ps[:mrows, :T], [(w[:, c, col0:col0 + mrows], hT[:, c, :T]) for c in range(8)], reads=[Bw, BhT], wbuf=Bp)
        k.copy_rr(dst_ap, ps[:mrows, :T], reads=[Bp], writes=[Bdst])

    def proj_tm(mx, w, Bw, col0, ncol, hT, BhT, tcol0, ntok):
        ps, Bp = mx['proj'].next()
        k.mm(ps[:ntok, :ncol], [(hT[:, c, tcol0:tcol0 + ntok], w[:, c, col0:col0 + ncol]) for c in range(8)],
             reads=[Bw, BhT], wbuf=Bp)
        return ps, Bp

    def out_rows(mx, ps, Bp, ntok, ncol, dst_ap):
        stg, Bst = mx['stg'].next()
        k.copy_rr(stg[:ntok, :ncol], ps[:ntok, :ncol], reads=[Bp], writes=[Bst])
        k.dma(dst_ap, stg[:ntok, :ncol], reads=[Bst], eng='pool')

    def attn_run(mx, jobs, nq, scale, la=2, pair=True):
        steps = []
        gsz = 2 if pair else 1
        for j0 in range(0, len(jobs), gsz):
            grp = jobs[j0:j0 + gsz]
            nb = len(grp[0]['blocks'])
            for g_ in grp:
                assert len(g_['blocks']) == nb
            for bi in range(nb):
                steps.append([(g_, bi, nb, g_['blocks'][bi]) for g_ in grp])
        n = len(steps)
        st = {}

        def stA(t):
            ents = steps[t]
            S = [mx['S'].next() for _ in ents]
            items = []
            reads = []
            nqk = max(len(e_[3]['qk']) for e_ in ents)
            for qi in range(nqk):
                for ei, e_ in enumerate(ents):
                    b = e_[3]
                    if qi < len(b['qk']):
                        l, r = b['qk'][qi]
                        c0, c1 = b.get('cols', (0, nq))
                        items.append((S[ei][0][:b['nk'], :c1 - c0], l, r[:, c0:c1], qi == 0, qi == len(b['qk']) - 1))
            for e_ in ents:
                reads += e_[3]['rd']
            k.mm_multi(items, reads, [s_[1] for s_ in S])
            st[t] = S

        def stB(t):
            ents = steps[t]
            S = st[t]
            Es = []
            for ei, e_ in enumerate(ents):
                b = e_[3]
                nk = b['nk']
                c0, c1 = b.get('cols', (0, nq))
                n_ = c1 - c0
                ps, Bs = S[ei]
                E, BE = mx['E'].next()
                if b['bias'][0] == 't':
                    sp, Bsp = mx['Sp'].next()
                    k.stt(sp[:nk, :n_], ps[:nk, :n_], scale, b['bias'][1], ALU.mult, ALU.add,
                          reads=[Bs, b['bias'][2]], writes=[Bsp])
                    k.act(E[:nk, :n_], sp[:nk, :n_], AF.Exp, reads=[Bsp], writes=[BE])
                else:
                    k.act(E[:nk, :n_], ps[:nk, :n_], AF.Exp, reads=[Bs] + b['bias'][2], writes=[BE],
                          scale=scale, bias=b['bias'][1])
                for (pat, op, base, cm) in b['masks']:
                    def fsel(e, E=E, pat=pat, op=op, base=base, cm=cm, nk=nk, n_=n_):
                        return e.affine_select(out=E[:nk, :n_], in_=E[:nk, :n_], pattern=pat, compare_op=op, fill=0.0,
                                               base=base, channel_multiplier=cm)
                    P.op('pool', fsel, reads=[BE], writes=[BE])
                Es.append((E, BE))
            st[t] = Es

        def stC(t):
            ents = steps[t]
            Es = st.pop(t)
            for ei, (j, bi, nb, b) in enumerate(ents):
                nk = b['nk']
                c0, c1 = b.get('cols', (0, nq))
                E, BE = Es[ei]
                O_, BO = j['O']
                M = j['M']
                k.mm(O_[:M, c0:c1], [(b['v'], E[:nk, :c1 - c0])], reads=[b['Bv'], BE], wbuf=BO, start=(bi == 0), stop=(bi == nb - 1))
                if j.get('D') is not None:
                    Dap, BD = j['D']
                    k.mm(Dap[:, c0:c1], [(ones_bf[:nk, :], E[:nk, :c1 - c0])], reads=[B_ones, BE], wbuf=BD,
                         start=(bi == 0), stop=(bi == nb - 1))
            for ei, (j, bi, nb, b) in enumerate(ents):
                if bi == nb - 1:
                    r_ = j['epi']()
                    if isinstance(r_, tuple):
                        pending.append((cur_t[0] + r_[0], r_[1]))
                        pending.sort(key=lambda x_: x_[0])
                    elif callable(r_):
                        pending.append((cur_t[0] + 2, r_))

        pending = []
        cur_t = [0]
        for t in range(n + la):
            cur_t[0] = t
            if t < n:
                stA(t)
            if 0 <= t - 1 < n:
                stB(t - 1)
            if 0 <= t - la < n:
                stC(t - la)
            while pending and pending[0][0] <= t:
                pending.pop(0)[1]()
        while pending:
            pending.pop(0)[1]()

    def epilogue_std(mx, bc_ps, O_, BO, nq, dst_ap, Bdst):
        p0, Mout = 64, 64
        den_ap = O_[64:65, :nq]
        hi, Bhi = mx['hl'].next()
        lo, Blo = mx['hl'].next()
        k.copy('dve', hi[p0:p0 + 1, :nq], den_ap, reads=[BO], writes=[Bhi])
        k.tt(lo[p0:p0 + 1, :nq], den_ap, hi[p0:p0 + 1, :nq], ALU.subtract, reads=[BO, Bhi], writes=[Blo])

        def part2():
            bc, Bbc = bc_ps
            k.mm(bc[0:Mout, :nq], [(ones_bf[p0:p0 + 1, 0:Mout], hi[p0:p0 + 1, :nq]), (ones_bf[p0:p0 + 1, 0:Mout], lo[p0:p0 + 1, :nq])],
                 reads=[B_ones, Bhi, Blo], wbuf=Bbc)
            rd, Brd = mx['rd'].next()
            k.recip(rd[0:Mout, :nq], bc[0:Mout, :nq], reads=[Bbc], writes=[Brd])
            k.tt(dst_ap, O_[0:64, :nq], rd[0:64, :nq], ALU.mult, reads=[BO, Brd], writes=[Bdst])
        return part2

    def load_cache_T(mx, cache_ap, ntok, dstT, BdstT, nchunk=4, width=128):
        for tt in range(ntok // 128):
            stg, Bst = mx['stg'].next()
            k.dma(stg[:, :nchunk * width], cache_ap[tt * 128:(tt + 1) * 128, :], writes=[Bst])
            ps, Bp = mx['proj'].next()
            k.trs([(ps[:width, cc * 128:(cc + 1) * 128], stg[:, cc * width:(cc + 1) * width], ident[:, :]) for cc in range(nchunk)],
                  reads=[Bst, B_ident], wbuf=Bp)
            k.copy_rr(dstT[:width, 0:nchunk, tt * 128:(tt + 1) * 128],
                      ps[:width, 0:nchunk * 128].rearrange("p (c t) -> p c t", c=nchunk), reads=[Bp], writes=[BdstT])

    def mixA():
        m = sb.mark()
        k.act_every = 3
        gidx = G_IDX[('mix', 0)]
        scale = A_D ** -0.5
        win, Bwin = k.tile('winA', [128, 8, 1536], BF16)
        Bw3 = [Buf('winA%d' % i) for i in range(3)]
        wv = I['e_w_in'].rearrange("(c p) n -> p c n", p=128)
        for i in (1, 2, 0):
            k.dma(win[:, :, i * 512:(i + 1) * 512], wv[:, :, i * 512:(i + 1) * 512], writes=[Bw3[i]], eng='pool')
        Bq, Bk_, Bv_ = Bw3
        arel_bc, Barel = k.tile('arelbc', [128, 8 * 129], F32)
        k.dma(arel_bc[:, :], dram_ap(I['a_rel_bias'], 0, [[0, 128], [1, 8 * 129]]), writes=[Barel])
        TA = [k.tile('TA%d' % h, [128, WM], F32) for h in range(8)]
        toe_read(8, 0, TA)
        KT, BKT = k.tile('KTA', [128, 4, SEQ], BF16)
        QTr = k.ring('QTAp', 2, [128, 8, 512], BF16)
        for (QT_, BQT_) in QTr.items:
            k.memset('pool', QT_[:, :, :], 0.0, [BQT_])
        KTn, BKTn = k.tile('KTAn', [128, 4, 32], BF16)
        VA, BVA = k.tile('VA', [128, 16, 8, 65], BF16)
        k.memset('pool', VA[:, :, :, 64:65], 1.0, [BVA])
        oT, BoT = k.tile('oTA', [128, 4, 512], BF16)
        mx = mixer_res(3)
        mx['S'] = Ring(psum[0:3])
        Oring = Ring(psum[3:6])
        bc_ps = psum[6]

        def heads(QT, BQT, nq, qcol0, blocks_fn, ocol0):
            jobs = []
            for h in range(8):
                pb, oc = (h % 2) * 64, h // 2
                qap = QT[:, h, qcol0:qcol0 + nq]
                Oo = Oring.next()
                jobs.append(dict(blocks=blocks_fn(h, pb, oc, qap, BQT), O=Oo, M=65, D=None,
                                 epi=(lambda Oo=Oo, pb=pb, oc=oc: epilogue_std(mx, bc_ps, Oo[0], Oo[1], nq,
                                                                               oT[pb:pb + 64, oc, ocol0:ocol0 + nq], BoT))))
            attn_run(mx, jobs, nq, scale, la=4, pair=False)

        def q_projA(QT, BQT, hT, BhT, T):
            for oc in range(4):
                ps, Bp = mx['proj'].next()
                k.mm(ps[:, :T], [(win[:, c, oc * 128:(oc + 1) * 128], hT[:, c, :T]) for c in range(8)], reads=[Bq, BhT], wbuf=Bp)
                k.copy_rr(QT[0:64, 2 * oc, :T], ps[0:64, :T], reads=[Bp], writes=[BQT])
                k.copy_rr(QT[64:128, 2 * oc + 1, :T], ps[64:128, :T], reads=[Bp], writes=[BQT])

        def bias_for(h, d, nk, nq):
            if -384 <= d <= 128:
                return ('t', TA[h][0][:nk, d + 384:d + 384 + nq], TA[h][1])
            return ('c', arel_bc[:nk, h * 129 + 128:h * 129 + 129], [Barel])

        DBG = os.environ.get('MIXDBG', '')
        for s in range(NSEQ):
            if DBG == 'setup' or (DBG in ('pass1', 'pass2', 'pass2h1') and s > 0):
                break
            tb = s * SEQ
            nxt = load_norm(mx, tb, 512, gidx)
            for tt in range(4):
                tx, Bx, hT, BhT = nxt
                if tt < 3:
                    nxt = load_norm(mx, tb + (tt + 1) * 512, 512, gidx)
                for oc in range(4):
                    proj_fm(mx, win, Bk_, 512 + oc * 128, hT, BhT, 512, KT[:, oc, tt * 512:(tt + 1) * 512], BKT)
                for su in range(4):
                    ps, Bp = proj_tm(mx, win, Bv_, 1024, 512, hT, BhT, su * 128, 128)
                    k.copy_rr(VA[:, tt * 4 + su, :, 0:64], ps[:, 0:512].rearrange("p (h d) -> p h d", h=8), reads=[Bp], writes=[BVA])
                    if tt == 3:
                        out_rows(mx, ps, Bp, 128, 512, O['p_a_v'][s, su * 128:(su + 1) * 128, :])
                        ps, Bp = proj_tm(mx, win, Bk_, 512, 512, hT, BhT, su * 128, 128)
                        out_rows(mx, ps, Bp, 128, 512, O['p_a_k'][s, su * 128:(su + 1) * 128, :])
            def prologue(qt, tb=tb):
                tx, Bx, hT, BhT = load_norm(mx, tb + qt * 512, 512, gidx)
                QT, BQT = QTr.next()
                q_projA(QT, BQT, hT, BhT, 512)
                return QT, BQT
            nxtq = prologue(0)
            for qt in range(4):
                q0 = qt * 512
                QT, BQT = nxtq
                if qt < 3:
                    nxtq = prologue(qt + 1)

                def blocks_fn(h, pb, oc, qap, BQT, q0=q0):
                    bl = []
                    k0s = list(range(max(0, q0 - 512), q0 + 512, 128))
                    k0s.sort(key=lambda k0_: (k0_ != q0))
                    for k0 in k0s:
                        d = q0 - k0
                        c0, c1 = 0, 512
                        if d < 0:
                            c0 = -d
                        if d >= 256:
                            c1 = 640 - d
                        n_ = c1 - c0
                        masks = []
                        if d <= 0:
                            masks.append(([[64, n_ // 64], [0, 64]], ALU.is_gt, d + 64 + c0, -1))
                        if d >= 128:
                            masks.append(([[-64, n_ // 64], [0, 64]], ALU.is_ge, 512 - d, 1))
                        if -384 <= d <= 128:
                            bias = ('t', TA[h][0][:128, d + 384 + c0:d + 384 + c1], TA[h][1])
                        else:
                            bias = bias_for(h, d, 128, 512)
                        bl.append(dict(nk=128, qk=[(KT[:, oc, k0:k0 + 128], qap)], rd=[BKT, BQT], cols=(c0, c1),
                                       v=VA[:, k0 // 128, h, :], Bv=BVA, bias=bias, masks=masks))
                    return bl
                heads(QT, BQT, 512, 0, blocks_fn, 0)
                k.dma(oTA_scr[:, :, tb + q0:tb + q0 + 512], oT[:, :, :], reads=[BoT], writes=[B_oTA], eng='pool')
        tx, Bx, hT, BhT = load_norm(mx, SOFF, 32, gidx)
        QT, BQT = QTr.next()
        q_projA(QT, BQT, hT, BhT, 32)
        for oc in range(4):
            proj_fm(mx, win, Bk_, 512 + oc * 128, hT, BhT, 32, KTn[:, oc, :], BKTn)
        for s in range(NSEQ):
            if DBG in ('setup', 'pass1', 'pass2', 'pass2h1'):
                break
            ps, Bp = proj_tm(mx, win, Bk_, 512, 512, hT, BhT, s * TS, TS)
            out_rows(mx, ps, Bp, TS, 512, O['s_a_k'][s * TS:(s + 1) * TS, :])
            ps, Bp = proj_tm(mx, win, Bv_, 1024, 512, hT, BhT, s * TS, TS)
            out_rows(mx, ps, Bp, TS, 512, O['s_a_v'][s * TS:(s + 1) * TS, :])
            k.copy_rr(VA[0:TS, 4, :, 0:64], ps[0:TS, 0:512].rearrange("p (h d) -> p h d", h=8), reads=[Bp], writes=[BVA])
            load_cache_T(mx, I['cache_a_k'][s], 512, KT, BKT)
            k.copy('dve', KT[:, :, 512:512 + TS], KTn[:, :, s * TS:(s + 1) * TS], reads=[BKTn], writes=[BKT])
            for kt in range(4):
                stg, Bst = mx['stg'].next()
                k.dma(stg[:, :], I['cache_a_v'][s, kt * 128:(kt + 1) * 128, :], writes=[Bst])
                k.copy_rr(VA[:, kt, :, 0:64], stg[:, :].rearrange("p (h d) -> p h d", h=8), reads=[Bst], writes=[BVA])

            def blocks_fn(h, pb, oc, qap, BQT):
                bl = []
                for kt in range(4):
                    d = 512 - 128 * kt
                    bl.append(dict(nk=128, qk=[(KT[:, oc, kt * 128:(kt + 1) * 128], qap)], rd=[BKT, BQT],
                                   v=VA[:, kt, h, :], Bv=BVA, bias=bias_for(h, d, 128, TS), masks=[]))
                bl.append(dict(nk=TS, qk=[(KT[:, oc, 512:512 + TS], qap)], rd=[BKT, BQT],
                               v=VA[0:TS, 4, h, :], Bv=BVA, bias=bias_for(h, 0, TS, TS), masks=[]))
                return bl
            heads(QT, BQT, TS, s * TS, blocks_fn, s * TS)
        k.dma(oTA_scr[:, :, SOFF:SOFF + 32], oT[:, :, 0:32], reads=[BoT], writes=[B_oTA], eng='pool')
        sb.release(m)

    def mixB():
        m = sb.mark()
        k.act_every = 3
        gidx = G_IDX[('mix', 0)]
        scale = B_D ** -0.5
        lam_init = 0.8 - 0.6 * math.exp(-0.3 * 0)
        win, Bwin = k.tile('winB', [128, 8, 1536], BF16)
        Bw3 = [Buf('winB%d' % i) for i in range(3)]
        wv = I['e_w_in'].rearrange("(c p) n -> p c n", p=128)
        for i in (1, 2, 0):
            k.dma(win[:, :, i * 512:(i + 1) * 512], wv[:, :, 1536 + i * 512:1536 + (i + 1) * 512], writes=[Bw3[i]], eng='pool')
        Bq, Bk_, Bv_ = Bw3
        wout, Bwout = k.tile('woutE', [128, 8, D], BF16)
        k.dma(wout[:, :, :], I['e_w_out'].rearrange("(c p) n -> p c n", p=128), writes=[Bwout], eng='pool')
        lv, Blv = k.tile('lamv', [128, 4, 64], F32)
        for i, nm in enumerate(('b_lambda_q1', 'b_lambda_k1', 'b_lambda_q2', 'b_lambda_k2')):
            k.dma(lv[:, i, :], dram_ap(I[nm], 0, [[0, 128], [1, 64]]), writes=[Blv])
        lsc, Blsc = k.tile('lamsc', [128, 8], F32)
        ljunk, Bljunk = k.tile('lamjunk', [128, 64], F32)
        for i in range(2):
            P.op('dve', (lambda e, i=i: e.scalar_tensor_tensor(out=ljunk[:, :], in0=lv[:, 2 * i, :], scalar=1.0, in1=lv[:, 2 * i + 1, :],
                                                                op0=ALU.mult, op1=ALU.mult, accum_out=lsc[:, i:i + 1])),
                 reads=[Blv], writes=[Bljunk, Blsc])
        k.act(lsc[:, 2:4], lsc[:, 0:2], AF.Exp, reads=[Blsc], writes=[Blsc])
        k.tt(lsc[:, 4:5], lsc[:, 3:4], lsc[:, 2:3], ALU.subtract, reads=[Blsc], writes=[Blsc])
        k.ts(lsc[:, 5:6], lsc[:, 4:5], -lam_init, None, ALU.add, ALU.bypass, reads=[Blsc], writes=[Blsc])
        nlam = lsc[:, 5:6]
        gsub, Bgsub = k.tile('gsub', [128, 2], F32)
        k.dma(gsub[:, 0:1], I['b_subln'].rearrange("o (p u) -> (o p) u", u=1), writes=[Bgsub])
        k.ts(gsub[:, 1:2], gsub[:, 0:1], 1.0 - lam_init, None, ALU.mult, ALU.bypass, reads=[Bgsub], writes=[Bgsub])
        t5_bc, Bt5bc = k.tile('t5bc', [128, 128], F32)
        k.dma(t5_bc[:, :], dram_ap(I['t5_bias'], 0, [[0, 128], [1, 128]]), writes=[Bt5bc])
        TB = [k.tile('TB%d' % h, [128, WM], F32) for h in range(4)]
        toe_read(4, 8, TB)
        KT, BKT = k.tile('KTB', [128, 4, SEQ], BF16)
        QTr = k.ring('QTBp', 2, [128, 8, 512], BF16)
        for (QT_, BQT_) in QTr.items:
            k.memset('pool', QT_[:, :, :], 0.0, [BQT_])
        KTn, BKTn = k.tile('KTBn', [128, 4, 32], BF16)
        VB, BVB = k.tile('VB', [128, 16, 512], BF16)
        oT, BoT = k.tile('oTB', [128, 8, 512], BF16)
        on_r = k.ring('mOn', 2, [128, 512], F32)
        ob_t, Bob = k.tile('mob', [128, 512], F32)
        mx = mixer_res(2)
        mx['S'] = Ring(psum[0:3])
        mx['proj'] = Ring([psum[7], psum[0], psum[1], psum[2]])
        Ops = [psum[3], psum[4]]
        Dps = [psum[5], psum[6]]
        bc_ps = psum[7]

        def q_proj(hT, BhT, T):
            QT, BQT = QTr.next()
            q_proj2(QT, BQT, hT, BhT, T)
            return QT, BQT

        def q_proj2(QT, BQT, hT, BhT, T):
            for h in range(4):
                ps, Bp = mx['proj'].next()
                k.mm(ps[:, :T], [(win[:, c, h * 128:(h + 1) * 128], hT[:, c, :T]) for c in range(8)], reads=[Bq, BhT], wbuf=Bp)
                k.copy_rr(QT[0:64, 2 * h, :T], ps[0:64, :T], reads=[Bp], writes=[BQT])
                k.copy_rr(QT[64:128, 2 * h + 1, :T], ps[64:128, :T], reads=[Bp], writes=[BQT])

        def heads(QT, BQT, nq, qcol0, blocks_fn, ocol0):
            jobs = []
            for h in range(4):
                Ons = [on_r.next(), on_r.next()]

                def epi_map(mp_, Ons=Ons):
                    O_, BO = Ops[mp_]
                    rd, Brd = mx['rd'].next()
                    k.recip(rd[:, :nq], Dps[mp_][0][:, :nq], reads=[Dps[mp_][1]], writes=[Brd])
                    On, BOn = Ons[mp_]
                    k.tt(On[:, :nq], O_[:, :nq], rd[:, :nq], ALU.mult, reads=[BO, Brd], writes=[BOn])

                def epi1(h=h, Ons=Ons, epi_map=epi_map):
                    epi_map(1)
                    k.stt(ob_t[:, :nq], Ons[1][0][:, :nq], nlam, Ons[0][0][:, :nq], ALU.mult, ALU.add,
                          reads=[Ons[0][1], Ons[1][1], Blsc], writes=[Bob])
                    sq, Bsq = mx['sq'].next()
                    k.act(sq[:, :nq], ob_t[:, :nq], AF.Square, reads=[Bob], writes=[Bsq])

                    def part2(h=h, sq=sq, Bsq=Bsq):
                        bc, Bbc = bc_ps
                        k.mm(bc[:, :nq], [(ones_bf[:, :], sq[:, :nq])], reads=[B_ones, Bsq], wbuf=Bbc)
                        rs, Brs = mx['rd'].next()
                        k.rstd(rs[:, :nq], bc[:, :nq], [Bbc, B_eps], [Brs], 1.0 / 128, eps_t[:, 0:1])
                        k.stt(oT[:, 4 + h, ocol0:ocol0 + nq], ob_t[:, :nq], gsub[:, 1:2], rs[:, :nq], ALU.mult, ALU.mult,
                              reads=[Bob, Brs, Bgsub], writes=[BoT])
                    return (4, part2)

                for mp_ in range(2):
                    qap = QT[:, 2 * h + mp_, qcol0:qcol0 + nq]
                    jobs.append(dict(blocks=blocks_fn(h, mp_, qap, BQT), O=Ops[mp_], M=128, D=Dps[mp_],
                                     epi=((lambda epi_map=epi_map: epi_map(0)) if mp_ == 0 else epi1)))
            attn_run(mx, jobs, nq, scale, la=4, pair=False)

        def bias_for(h, dk, nk, nq):
            if dk >= -128:
                return ('t', TB[h][0][:nk, 384 - dk:384 - dk + nq], TB[h][1])
            return ('c', t5_bc[:nk, 15 * 4 + h:15 * 4 + h + 1], [Bt5bc])

        def out_proj(tx, Bx, T, t0):
            for c in range(8):
                ps, Bp = mx['proj'].next()
                k.mm(ps[:, :T], [(wout[:, kc, c * 128:(c + 1) * 128], oT[:, kc, :T]) for kc in range(8)], reads=[Bwout, BoT], wbuf=Bp)
                k.stt(tx[:, c, :T], ps[:, :T], 1.0, tx[:, c, :T], ALU.mult, ALU.add, reads=[Bp, Bx], writes=[Bx])
            k.dma(xT_pc[:, :, t0:t0 + T], tx[:, :, :T], reads=[Bx], writes=[B_xT], eng='pool')

        for s in range(NSEQ):
            tb = s * SEQ
            nxt = load_norm(mx, tb, 512, gidx)
            for tt in range(4):
                tx, Bx, hT, BhT = nxt
                if tt < 3:
                    nxt = load_norm(mx, tb + (tt + 1) * 512, 512, gidx)
                for oc in range(4):
                    proj_fm(mx, win, Bk_, 512 + oc * 128, hT, BhT, 512, KT[:, oc, tt * 512:(tt + 1) * 512], BKT)
                for su in range(4):
                    r0 = tt * 512 + su * 128
                    ps, Bp = proj_tm(mx, win, Bv_, 1024, 512, hT, BhT, su * 128, 128)
                    k.copy_rr(VB[:, tt * 4 + su, :], ps[:, 0:512], reads=[Bp], writes=[BVB])
                    out_rows(mx, ps, Bp, 128, 512, O['p_b_v'][s, r0:r0 + 128, :])
                    ps, Bp = proj_tm(mx, win, Bk_, 512, 512, hT, BhT, su * 128, 128)
                    out_rows(mx, ps, Bp, 128, 512, O['p_b_k'][s, r0:r0 + 128, :])
            def prologue(qt, tb=tb):
                tx, Bx, hT, BhT = load_norm(mx, tb + qt * 512, 512, gidx)
                QT, BQT = q_proj(hT, BhT, 512)
                return tx, Bx, QT, BQT
            nxtq = prologue(0)
            for qt in range(4):
                q0 = qt * 512
                tx, Bx, QT, BQT = nxtq
                if qt < 3:
                    nxtq = prologue(qt + 1)
                k.dma(oT[:, 0:4, :], oTA_scr[:, :, tb + q0:tb + q0 + 512], reads=[B_oTA], writes=[BoT])

                def blocks_fn(h, mp_, qap, BQT, q0=q0):
                    bl = []
                    for k0 in range(0, q0 + 512, 128):
                        dk = k0 - q0
                        c0 = max(0, dk)
                        n_ = 512 - c0
                        masks = [([[64, n_ // 64], [0, 64]], ALU.is_gt, 64, -1)] if dk >= 0 else []
                        if dk >= -128:
                            bias = ('t', TB[h][0][:128, 384 - dk + c0:384 - dk + 512], TB[h][1])
                        else:
                            bias = bias_for(h, dk, 128, 512)
                        bl.append(dict(nk=128, qk=[(KT[:, h, k0:k0 + 128], qap)], rd=[BKT, BQT], cols=(c0, 512),
                                       v=VB[:, k0 // 128, h * 128:(h + 1) * 128], Bv=BVB, bias=bias, masks=masks))
                    return bl
                heads(QT, BQT, 512, 0, blocks_fn, 0)
                out_proj(tx, Bx, 512, tb + q0)
        tx, Bx, hT, BhT = load_norm(mx, SOFF, 32, gidx)
        QT, BQT = q_proj(hT, BhT, 32)
        for oc in range(4):
            proj_fm(mx, win, Bk_, 512 + oc * 128, hT, BhT, 32, KTn[:, oc, :], BKTn)
        k.dma(oT[:, 0:4, 0:32], oTA_scr[:, :, SOFF:SOFF + 32], reads=[B_oTA], writes=[BoT])
        for s in range(NSEQ):
            ps, Bp = proj_tm(mx, win, Bk_, 512, 512, hT, BhT, s * TS, TS)
            out_rows(mx, ps, Bp, TS, 512, O['s_b_k'][s * TS:(s + 1) * TS, :])
            ps, Bp = proj_tm(mx, win, Bv_, 1024, 512, hT, BhT, s * TS, TS)
            out_rows(mx, ps, Bp, TS, 512, O['s_b_v'][s * TS:(s + 1) * TS, :])
            k.copy_rr(VB[0:TS, 8, :], ps[0:TS, 0:512], reads=[Bp], writes=[BVB])
            load_cache_T(mx, I['cache_b_k'][s], PAST, KT, BKT)
            k.copy('dve', KT[:, :, PAST:PAST + TS], KTn[:, :, s * TS:(s + 1) * TS], reads=[BKTn], writes=[BKT])
            for kt in range(8):
                stg, Bst = mx['stg'].next()
                k.dma(stg[:, :], I['cache_b_v'][s, kt * 128:(kt + 1) * 128, :], writes=[Bst])
                k.copy_rr(VB[:, kt, :], stg[:, :], reads=[Bst], writes=[BVB])

            def blocks_fn(h, mp_, qap, BQT):
                bl = []
                for kt in range(8):
                    dk = 128 * kt - PAST
                    bl.append(dict(nk=128, qk=[(KT[:, h, kt * 128:(kt + 1) * 128], qap)], rd=[BKT, BQT],
                                   v=VB[:, kt, h * 128:(h + 1) * 128], Bv=BVB, bias=bias_for(h, dk, 128, TS), masks=[]))
                bl.append(dict(nk=TS, qk=[(KT[:, h, PAST:PAST + TS], qap)], rd=[BKT, BQT],
                               v=VB[0:TS, 8, h * 128:(h + 1) * 128], Bv=BVB, bias=bias_for(h, 0, TS, TS), masks=[]))
                return bl
            heads(QT, BQT, TS, s * TS, blocks_fn, s * TS)
        out_proj(tx, Bx, 32, SOFF)
        sb.release(m)

    def mixC():
        m = sb.mark()
        k.act_every = 3
        gidx = G_IDX[('mix', 1)]
        scale = (C_NOPE + C_ROPE) ** -0.5
        win, Bwin = k.tile('winC', [128, 8, 672], BF16)
        k.dma(win[:, :, :], I['c_w_in'].rearrange("(c p) n -> p c n", p=128), writes=[Bwin], eng='pool')
        wkr3, Bwkr3 = k.tile('wkr3', [128, 8, 96], BF16)
        wkr3r, Bwkr3r = k.tile('wkr3r', [128, 8, 96], BF16)
        for g in range(3):
            k.copy('dve', wkr3[:, :, g * 32:(g + 1) * 32], win[:, :, 640:672], reads=[Bwin], writes=[Bwkr3])
            k.ts(wkr3r[:, :, g * 32:g * 32 + 16], win[:, :, 656:672], -1.0, None, ALU.mult, ALU.bypass, reads=[Bwin], writes=[Bwkr3r])
            k.copy('dve', wkr3r[:, :, g * 32 + 16:g * 32 + 32], win[:, :, 640:656], reads=[Bwin], writes=[Bwkr3r])
        wq_v = I['c_w_q_up'].rearrange("(k p) (h e) -> p k h e", p=128, e=96)
        wqn, Bwqn = k.tile('wqn', [128, 3, 16, 64], BF16)
        wqr, Bwqr = k.tile('wqr', [128, 3, 16, 32], BF16)
        wqrr, Bwqrr = k.tile('wqrr', [128, 3, 16, 32], BF16)
        for k3 in range(3):
            k.dma(wqn[:, k3, :, :], wq_v[:, k3, :, 0:64], writes=[Bwqn], eng='pool')
            k.dma(wqr[:, k3, :, :], wq_v[:, k3, :, 64:96], writes=[Bwqr], eng='pool')
        for k3 in range(3):
            k.ts(wqrr[:, k3, :, 0:16], wqr[:, k3, :, 16:32], -1.0, None, ALU.mult, ALU.bypass, reads=[Bwqr], writes=[Bwqrr])
            k.copy('dve', wqrr[:, k3, :, 16:32], wqr[:, k3, :, 0:16], reads=[Bwqr], writes=[Bwqrr])
        wqn2 = wqn[:, :, :, :].rearrange("p k h e -> p k (h e)")
        wqr2 = wqr[:, :, :, :].rearrange("p k h e -> p k (h e)")
        wqrr2 = wqrr[:, :, :, :].rearrange("p k h e -> p k (h e)")
        wkv_v = I['c_w_kv_up'].rearrange("(k p) (h e) -> p k h e", p=128, e=128)
        wkn, Bwkn = k.tile('wkn', [128, 2, 16, 64], BF16)
        wvv, Bwvv = k.tile('wvv', [128, 2, 16, 64], BF16)
        for k2 in range(2):
            k.dma(wkn[:, k2, :, :], wkv_v[:, k2, :, 0:64], writes=[Bwkn], eng='pool')
            k.dma(wvv[:, k2, :, :], wkv_v[:, k2, :, 64:128], writes=[Bwvv], eng='pool')
        wkn2 = wkn[:, :, :, :].rearrange("p k h e -> p k (h e)")
        wvv2 = wvv[:, :, :, :].rearrange("p k h e -> p k (h e)")
        wout, Bwout = k.tile('woutC', [128, 8, D], BF16)
        k.dma(wout[:, :, :], I['c_w_out'].rearrange("(c p) n -> p c n", p=128), writes=[Bwout], eng='pool')
        gq, Bgq = k.tile('gq', [128, 5], F32)
        k.dma(gq[:, 0:3], I['c_q_norm'].rearrange("o (c p) -> (o p) c", p=128), writes=[Bgq], allow_slow_non_contiguous=True)
        k.dma(gq[:, 3:5], I['c_kv_norm'].rearrange("o (c p) -> (o p) c", p=128), writes=[Bgq], allow_slow_non_contiguous=True)
        knT, BknT = k.tile('knT', [128, 8, SEQ], BF16)
        krr3, Bkrr3 = k.tile('krrp', [128, 3, SEQ], BF16)
        k.memset('pool', krr3[:, :, :], 0.0, [Bkrr3])
        xT2 = nc.dram_tensor("xT2_scr", [8, 128, NTOK], F32).ap()
        xT2_pc = xT2.rearrange("c p t -> p c t")
        B_xT2 = Buf('xT2')

        def kr_write(c0, n, src, Bsrc):
            for g_ in range(3):
                k.copy_rr(krr3[g_ * 32:(g_ + 1) * 32, g_, c0:c0 + n], src[g_ * 32:(g_ + 1) * 32, :n], reads=[Bsrc], writes=[Bkrr3])
        VC, BVC = k.tile('VC', [128, 16, 16, 65], BF16)
        k.memset('pool', VC[:, :, :, 64:65], 1.0, [BVC])
        cqn, Bcqn = k.tile('cqn', [128, 3, SEQ], BF16)
        S_ring = Ring(psum[0:3])
        Oring = Ring(psum[3:6])
        bc_ps = psum[6]
        proj = Ring([psum[7], psum[0], psum[1], psum[2], psum[3]])
        ss_ps = psum[7]

        def work1(T, with_cache=False):
            w = {}
            w['x'] = k.tile('c1x', [128, 8, T], F32)
            w['h'] = k.ring('c1h', 2, [128, 8, T], BF16)
            w['sq'] = k.ring('c1sq', 3, [128, T], BF16)
            w['rs'] = k.tile('c1rs', [128, T], F32)
            w['cqf'] = k.tile('c1cqf', [128, 3, T], F32)
            w['ckvf'] = k.tile('c1ckvf', [128, 2, T], F32)
            w['ckvn'] = w['ckvf']
            w['ckvb'] = k.tile('c1ckvb', [128, 2, T], BF16)
            w['t1'] = k.tile('c1t1', [96, T], F32)
            w['krrf'] = k.tile('c1krrf', [96, T], F32)
            w['cos'] = k.ring('c1cos', 2, [96, T], F32)
            w['sin'] = k.ring('c1sin', 2, [96, T], F32)
            w['stg'] = k.ring('c1stg', 2, [128, 512], F32)
            return w

        def work2(T):
            w = {}
            w['xc'] = k.ring('c2xc', 3, [128, T], F32)
            w['cos'] = k.tile('c2cos', [96, T], F32)
            w['sin'] = k.tile('c2sin', [96, T], F32)
            w['qnT'] = k.tile('c2qnp', [128, 16, T], BF16)
            w['qrT'] = k.tile('c2qr', [128, 6, T], BF16)
            k.memset('pool', w['qnT'][0][:, :, :], 0.0, [w['qnT'][1]])
            k.memset('pool', w['qrT'][0][:, :, :], 0.0, [w['qrT'][1]])
            w['oT'] = k.tile('c2oT', [128, 8, T], BF16)
            w['t1'] = k.tile('c2t1', [96, T], F32)
            w['t2'] = k.tile('c2t2', [96, T], F32)
            w['E'] = k.ring('c2E', 6, [128, T], BF16)
            w['hl'] = k.ring('c2hl', 2, [128, T], BF16)
            w['rd'] = k.ring('c2rd', 2, [128, T], F32)
            w['S'] = S_ring
            return w

        def p1_load(w, t0, T, pcol):
            tx, Bx = w['x']
            k.dma(tx[:, :, :T], xT_pc[:, :, t0:t0 + T], reads=[B_xT], writes=[Bx])
            cs, Bcs = w['cos'].next()
            sn, Bsn = w['sin'].next()
            k.dma(cs[:, :T], I['c_rope_cos'][:, pcol:pcol + T], writes=[Bcs])
            k.dma(sn[:, :T], I['c_rope_sin'][:, pcol:pcol + T], writes=[Bsn])
            hT, BhT = w['h'].next()
            rms_fm(tx, Bx, 8, T, lambda c: gains[:, gidx, c:c + 1], hT, BhT, w['sq'], ss_ps, w['rs'][0], w['rs'][1], D)
            return hT, BhT, cs, Bcs, sn, Bsn

        def pass1_tile(w, ld, T, tcol, kr_dst, kv_out, kr_out):
            hT, BhT, cs, Bcs, sn, Bsn = ld
            cqf, Bcqf = w['cqf']
            ckvf, Bckvf = w['ckvf']
            for kq in range(3):
                ps, Bp = proj.next()
                k.mm(ps[:, :T], [(win[:, c, kq * 128:(kq + 1) * 128], hT[:, c, :T]) for c in range(8)], reads=[Bwin, BhT], wbuf=Bp)
                k.copy_rr(cqf[:, kq, :T], ps[:, :T], reads=[Bp], writes=[Bcqf])
            for k2 in range(2):
                ps, Bp = proj.next()
                k.mm(ps[:, :T], [(win[:, c, 384 + k2 * 128:384 + (k2 + 1) * 128], hT[:, c, :T]) for c in range(8)], reads=[Bwin, BhT], wbuf=Bp)
                k.copy_rr(ckvf[:, k2, :T], ps[:, :T], reads=[Bp], writes=[Bckvf])
            t1, Bt1 = w['t1']
            krrf, Bkrrf = w['krrf']
            ps, Bp = proj.next()
            k.mm(ps[0:96, :T], [(wkr3[:, c, :], hT[:, c, :T]) for c in range(8)], reads=[Bwkr3, BhT], wbuf=Bp)
            k.tt(t1[:, :T], ps[0:96, :T], cs[:, :T], ALU.mult, reads=[Bp, Bcs], writes=[Bt1])
            ps, Bp = proj.next()
            k.mm(ps[0:96, :T], [(wkr3r[:, c, :], hT[:, c, :T]) for c in range(8)], reads=[Bwkr3r, BhT], wbuf=Bp)
            k.tt(krrf[:, :T], ps[0:96, :T], sn[:, :T], ALU.mult, reads=[Bp, Bsn], writes=[Bkrrf])
            k.tt(krrf[:, :T], krrf[:, :T], t1[:, :T], ALU.add, reads=[Bkrrf, Bt1], writes=[Bkrrf])
            kr_dst(krrf, Bkrrf)
            rms_fm(cqf, Bcqf, 3, T, lambda c: gq[:, c:c + 1], (lambda c: cqn[:, c, tcol:tcol + T]), Bcqn, w['sq'], ss_ps,
                   w['rs'][0], w['rs'][1], C_QL, gbuf=Bgq)
            ckvn, Bckvn = w['ckvn']
            ckvb, Bckvb = w['ckvb']
            rms_fm(ckvf, Bckvf, 2, T, lambda c: gq[:, 3 + c:4 + c], ckvn, Bckvn, w['sq'], ss_ps, w['rs'][0], w['rs'][1], C_KVL, gbuf=Bgq)
            for k2 in range(2):
                k.copy_rr(ckvb[:, k2, :T], ckvn[:, k2, :T], reads=[Bckvn], writes=[Bckvb])
            for su in range((T + 127) // 128):
                nt = min(128, T - su * 128)
                ps, Bp = proj.next()
                items = [(ps[0:nt, k2 * 128:(k2 + 1) * 128], ckvn[:, k2, su * 128:su * 128 + nt], ident[:, :]) for k2 in range(2)]
                items.append((ps[0:nt, 256:288], krrf[0:32, su * 128:su * 128 + nt], ident[0:32, 0:32]))
                k.trs(items, reads=[Bckvn, Bkrrf, B_ident], wbuf=Bp)
                stg, Bst = w['stg'].next()
                k.copy_rr(stg[0:nt, 0:288], ps[0:nt, 0:288], reads=[Bp], writes=[Bst])
                k.dma(kv_out(su * 128, nt), stg[0:nt, 0:256], reads=[Bst], eng='pool')
                k.dma(kr_out(su * 128, nt), stg[0:nt, 256:288], reads=[Bst], eng='pool')
            return ckvb, Bckvb

        def expand_kv(src_fn, Bsrc, ncols, tcol):
            for c0 in range(0, ncols, 512):
                n = min(512, ncols - c0)
                for p in range(8):
                    ps, Bp = proj.next()
                    k.mm(ps[:, :n], [(wkn2[:, k2, p * 128:(p + 1) * 128], src_fn(k2, c0, n)) for k2 in range(2)], reads=[Bwkn, Bsrc], wbuf=Bp)
                    k.copy_rr(knT[:, p, tcol + c0:tcol + c0 + n], ps[:, :n], reads=[Bp], writes=[BknT])
            for c0 in range(0, ncols, 128):
                n = min(128, ncols - c0)
                kt = (tcol + c0) // 128
                for half in range(2):
                    ps, Bp = proj.next()
                    k.mm(ps[0:n, :], [(src_fn(k2, c0, n), wvv2[:, k2, half * 512:(half + 1) * 512]) for k2 in range(2)], reads=[Bwvv, Bsrc], wbuf=Bp)
                    k.copy_rr(VC[0:n, kt, half * 8:(half + 1) * 8, 0:64], ps[0:n, :].rearrange("p (h d) -> p h d", h=8), reads=[Bp], writes=[BVC])

        def q_proj(w, T, qcol, pcol):
            cs, Bcs = w['cos']
            sn, Bsn = w['sin']
            k.dma(cs[:, :T], I['c_rope_cos'][:, pcol:pcol + T], writes=[Bcs])
            k.dma(sn[:, :T], I['c_rope_sin'][:, pcol:pcol + T], writes=[Bsn])
            qnT, BqnT = w['qnT']
            qrT, BqrT = w['qrT']
            for p in range(8):
                ps, Bp = proj.next()
                k.mm(ps[:, :T], [(wqn2[:, k3, p * 128:(p + 1) * 128], cqn[:, k3, qcol:qcol + T]) for k3 in range(3)], reads=[Bwqn, Bcqn], wbuf=Bp)
                k.copy_rr(qnT[0:64, 2 * p, :T], ps[0:64, :T], reads=[Bp], writes=[BqnT])
                k.copy_rr(qnT[64:128, 2 * p + 1, :T], ps[64:128, :T], reads=[Bp], writes=[BqnT])
            t1, Bt1 = w['t1']
            t2, Bt2 = w['t2']
            for ch in range(6):
                nr = 96 if ch < 5 else 32
                ps, Bp = proj.next()
                k.mm(ps[0:nr, :T], [(wqr2[:, k3, ch * 96:ch * 96 + nr], cqn[:, k3, qcol:qcol + T]) for k3 in range(3)], reads=[Bwqr, Bcqn], wbuf=Bp)
                k.tt(t1[0:nr, :T], ps[0:nr, :T], cs[0:nr, :T], ALU.mult, reads=[Bp, Bcs], writes=[Bt1])
                ps, Bp = proj.next()
                k.mm(ps[0:nr, :T], [(wqrr2[:, k3, ch * 96:ch * 96 + nr], cqn[:, k3, qcol:qcol + T]) for k3 in range(3)], reads=[Bwqrr, Bcqn], wbuf=Bp)
                k.tt(t2[0:nr, :T], ps[0:nr, :T], sn[0:nr, :T], ALU.mult, reads=[Bp, Bsn], writes=[Bt2])
                k.tt(qrT[0:nr, ch, :T], t1[0:nr, :T], t2[0:nr, :T], ALU.add, reads=[Bt1, Bt2], writes=[BqrT])

        def heads(w, nq, qc0, blocks_fn):
            qnT, BqnT = w['qnT']
            qrT, BqrT = w['qrT']
            oT, BoT = w['oT']
            jobs = []
            for h in range(16):
                pb, p = (h % 2) * 64, h // 2
                g, ch = h % 3, h // 3
                qa = qnT[:, h, qc0:qc0 + nq]
                qb = qrT[:, ch, qc0:qc0 + nq]
                Oo = Oring.next()
                jobs.append(dict(blocks=blocks_fn(h, pb, p, g, qa, qb, [BknT, Bkrr3, BqnT, BqrT]), O=Oo, M=65, D=None,
                                 epi=(lambda Oo=Oo, pb=pb, p=p: epilogue_std(w, bc_ps, Oo[0], Oo[1], nq,
                                                                             oT[pb:pb + 64, p, qc0:qc0 + nq], BoT))))
            attn_run(w, jobs, nq, scale, la=4, pair=False)

        def out_proj(w, T, t0, load_x=True):
            oT, BoT = w['oT']
            for c in range(8):
                xc, Bxc = w['xc'].next()
                k.dma(xc[:, :T], xT_pc[:, c, t0:t0 + T], reads=[B_xT], writes=[Bxc])
                ps, Bp = proj.next()
                k.mm(ps[:, :T], [(wout[:, kc, c * 128:(c + 1) * 128], oT[:, kc, :T]) for kc in range(8)], reads=[Bwout, BoT], wbuf=Bp)
                k.stt(xc[:, :T], ps[:, :T], 1.0, xc[:, :T], ALU.mult, ALU.add, reads=[Bp, Bxc], writes=[Bxc])
                k.dma(xT2_pc[:, c, t0:t0 + T], xc[:, :T], reads=[Bxc], writes=[B_xT2], eng='pool')

        def mk_block(nk, kcols, vkt, h, pb, p, g, qa, qb, rd, masks):
            return dict(nk=nk, qk=[(knT[:, p, kcols[0]:kcols[1]], qa), (krr3[:, g, kcols[0]:kcols[1]], qb)],
                        rd=rd, v=VC[0:nk, vkt, h, :], Bv=BVC, bias=('c', 0.0, []), masks=masks)

        for s in range(NSEQ):
            tb = s * SEQ
            m1 = sb.mark()
            w = work1(512)
            nxt = p1_load(w, tb, 512, 0)
            for tt in range(4):
                ld = nxt
                if tt < 3:
                    nxt = p1_load(w, tb + (tt + 1) * 512, 512, (tt + 1) * 512)
                ckvb, Bckvb = pass1_tile(w, ld, 512, tt * 512, (lambda kf, Bkf, tt=tt: kr_write(tt * 512, 512, kf, Bkf)),
                                         lambda r0, n, tt=tt: O['p_c_kv'][s, tt * 512 + r0:tt * 512 + r0 + n, :],
                                         lambda r0, n, tt=tt: O['p_c_kr'][s, tt * 512 + r0:tt * 512 + r0 + n, :])
                expand_kv(lambda k2, c0, n: ckvb[:, k2, c0:c0 + n], Bckvb, 512, tt * 512)
            sb.release(m1)
            w = work2(512)
            for qt in range(4):
                q0 = qt * 512
                q_proj(w, 512, q0, q0)

                def blocks_fn(h, pb, p, g, qa, qb, rd, q0=q0):
                    bl = []
                    for k0 in range(0, q0 + 512, 128):
                        dk = k0 - q0
                        c0 = max(0, dk)
                        n_ = 512 - c0
                        masks = [([[64, n_ // 64], [0, 64]], ALU.is_gt, 64, -1)] if dk >= 0 else []
                        blk = mk_block(128, (k0, k0 + 128), k0 // 128, h, pb, p, g, qa, qb, rd, masks)
                        blk['cols'] = (c0, 512)
                        bl.append(blk)
                    return bl
                heads(w, 512, 0, blocks_fn)
                out_proj(w, 512, tb + q0)
            sb.release(m1)
        m1 = sb.mark()
        w = work1(32)
        krn, Bkrn = k.tile('krn', [96, 32], BF16)
        ckT, BckT = k.tile('ckT', [128, 2, PAST + TS], BF16)
        w2 = work2(32)
        ckvb, Bckvb = pass1_tile(w, p1_load(w, SOFF, 32, SEQ), 32, 0, (lambda kf, Bkf: k.copy('act', krn[:, :], kf[:, :32], reads=[Bkf], writes=[Bkrn])),
                                 lambda r0, n: O['s_c_kv'][r0:r0 + n, :], lambda r0, n: O['s_c_kr'][r0:r0 + n, :])
        q_proj(w2, 32, 0, SEQ)
        mxs = dict(stg=w['stg'], proj=proj)
        for s in range(NSEQ):
            load_cache_T(mxs, I['cache_c_kv'][s], PAST, ckT, BckT, nchunk=2, width=128)
            k.copy('dve', ckT[:, :, PAST:PAST + TS], ckvb[:, :, s * TS:(s + 1) * TS], reads=[Bckvb], writes=[BckT])
            expand_kv(lambda k2, c0, n: ckT[:, k2, c0:c0 + n], BckT, PAST + TS, 0)
            for tt in range(PAST // 128):
                stg, Bst = w['stg'].next()
                k.dma(stg[:, 0:96].rearrange("p (g e) -> p g e", g=3),
                      dram_ap(I['cache_c_kr'], (s * PAST + tt * 128) * C_ROPE, [[C_ROPE, 128], [0, 3], [1, C_ROPE]]), writes=[Bst])
                ps, Bp = proj.next()
                k.tr(ps[0:96, 0:128], stg[:, 0:96], ident[:, :], reads=[Bst, B_ident], wbuf=Bp)
                kr_write(tt * 128, 128, ps[0:96, 0:128], Bp)
            kr_write(PAST, TS, krn[:, s * TS:(s + 1) * TS], Bkrn)

            def blocks_fn(h, pb, p, g, qa, qb, rd):
                bl = [mk_block(128, (kt * 128, (kt + 1) * 128), kt, h, pb, p, g, qa, qb, rd, []) for kt in range(PAST // 128)]
                bl.append(mk_block(TS, (PAST, PAST + TS), PAST // 128, h, pb, p, g, qa, qb, rd, []))
                return bl
            heads(w2, TS, s * TS, blocks_fn)
        out_proj(w2, 32, SOFF, load_x=False)
        RES['pc'], RES['B'] = xT2_pc, B_xT2
        RES['flat'] = xT2
        sb.release(m1)
        sb.release(m)

    if 'p0' in phases:
        toe_prep()
        phase0()
    if 'ffn1_0' in phases:
        ffn_phase('ffn1', 0)
    if 'mixA' in phases or 'mix0' in phases:
        mixA()
    if 'mixB' in phases or 'mix0' in phases:
        mixB()
    if 'ffn2_0' in phases:
        ffn_phase('ffn2', 0)
    if 'ffn1_1' in phases:
        ffn_phase('ffn1', 1)
    if 'mix1' in phases:
        mixC()
    if 'ffn2_1' in phases:
        ffn_phase('ffn2', 1)
    if debug:
        k.dma(O['dbg_xT'], RES['flat'], reads=[RES['B']])
    phaseZ()
    P.final_wait_all('sp')
    P.emit()
    return nc, sorted(I.keys()), sorted(O.keys()), sb.peak


def _t5_bucket_host(rel):
    import jax
    import jax.numpy as jnp
    with jax.default_device(jax.devices('cpu')[0]):
        rel = jnp.asarray(np.asarray(rel, dtype=np.int32))
        nb = 16
        max_exact = nb // 2
        ret = jnp.where(rel > 0, nb, 0)
        n = jnp.abs(rel)
        n_f = jnp.maximum(n, 1).astype(jnp.float32)
        large = max_exact + (jnp.log(n_f / max_exact) / math.log(128 / max_exact) * (nb - max_exact)).astype(jnp.int32)
        large = jnp.minimum(large, nb - 1)
        return np.asarray(ret + jnp.where(n < max_exact, n, large))


def host_constants():
    c = {}
    c['c_ident'] = np.eye(128, dtype=np.float32)
    half = C_ROPE // 2
    inv = (10000.0 ** (-np.arange(half, dtype=np.float32) / half)).astype(np.float32)
    pos = np.concatenate([np.arange(SEQ), PAST + np.arange(TS), PAST + np.arange(TS)]).astype(np.float32)
    ang = pos[None, :] * inv[(np.arange(96) % 32) % half][:, None]
    c['c_rope_cos'] = np.cos(ang.astype(np.float32)).astype(np.float32)
    c['c_rope_sin'] = np.sin(ang.astype(np.float32)).astype(np.float32)
    u = np.arange(LT)
    rel = np.where(u < WM, 384 - u, 384 + LT - u)
    bidx = np.asarray(_t5_bucket_host(rel))
    oh = np.zeros((32, LT), dtype=np.float32)
    oh[bidx, u] = 1.0
    c['c_t5_onehot'] = oh
    return c


_CACHE = {}


def kernel(**inputs):
    n = 8
    if 'nc' not in _CACHE:
        _CACHE['nc'] = build_program()
    nc, in_names, out_names, _ = _CACHE['nc']
    consts = host_constants()
    f = lambda a: np.ascontiguousarray(np.asarray(a, dtype=np.float32))
    in_maps = []
    for i in range(n):
        b0, b1 = i * NSEQ, (i + 1) * NSEQ
        mp = {}
        mp['xp'] = f(inputs['x_prompt'][b0:b1]).reshape(NSEQ * SEQ, D)
        mp['xs'] = f(inputs['x_sample'][b0:b1]).reshape(NSEQ * TS, D)
        mp['cache_a_k'] = f(inputs['cache_a_k'][0, b0:b1]).reshape(NSEQ, 512, 512)
        mp['cache_a_v'] = f(inputs['cache_a_v'][0, b0:b1]).reshape(NSEQ, 512, 512)
        mp['cache_b_k'] = f(inputs['cache_b_k'][0, b0:b1]).reshape(NSEQ, PAST, 512)
        mp['cache_b_v'] = f(inputs['cache_b_v'][0, b0:b1]).reshape(NSEQ, PAST, 512)
        mp['cache_c_kv'] = f(inputs['cache_c_kv'][0, b0:b1])
        mp['cache_c_kr'] = f(inputs['cache_c_kr'][0, b0:b1])
        for nm in ('t5_bias', 'ffn1_norm', 'mix_norm', 'ffn2_norm', 'ffn1_w_gu', 'ffn2_w_gu', 'ffn1_w_down', 'ffn2_w_down'):
            mp[nm] = f(inputs[nm])
        for nm in ('e_w_in', 'a_rel_bias', 'e_w_out', 'c_w_in', 'c_w_q_up', 'c_w_kv_up', 'c_w_out'):
            mp[nm] = f(inputs[nm][0])
        for nm in ('b_lambda_q1', 'b_lambda_k1', 'b_lambda_q2', 'b_lambda_k2', 'b_subln', 'c_q_norm', 'c_kv_norm'):
            mp[nm] = f(inputs[nm]).reshape(1, -1)
        mp['final_norm'] = f(inputs['final_norm']).reshape(1, D)
        mp.update(consts)
        in_maps.append({kk: mp[kk] for kk in in_names})
    res = run_bass_kernel_spmd(nc, in_maps, core_ids=list(range(n)))
    R = res.results
    B = 16

    def cat(name, shape):
        return np.concatenate([np.asarray(R[i][name], dtype=np.float32) for i in range(n)], axis=0).reshape(shape)
    y_prompt = cat('y_prompt', (B, SEQ, D))
    y_sample = cat('y_sample', (B, TS, D))
    p_a_k = cat('p_a_k', (1, B, 512, A_H, A_D))
    p_a_v = cat('p_a_v', (1, B, 512, A_H, A_D))
    p_b_k = cat('p_b_k', (1, B, SEQ, B_H, 2 * B_D))
    p_b_v = cat('p_b_v', (1, B, SEQ, B_H, 2 * B_D))
    p_c_kv = cat('p_c_kv', (1, B, SEQ, C_KVL))
    p_c_kr = cat('p_c_kr', (1, B, SEQ, C_ROPE))
    s_a_k = cat('s_a_k', (1, B, TS, A_H, A_D))
    s_a_v = cat('s_a_v', (1, B, TS, A_H, A_D))
    s_b_k = cat('s_b_k', (1, B, TS, B_H, 2 * B_D))
    s_b_v = cat('s_b_v', (1, B, TS, B_H, 2 * B_D))
    s_c_kv = cat('s_c_kv', (1, B, TS, C_KVL))
    s_c_kr = cat('s_c_kr', (1, B, TS, C_ROPE))
    return (y_prompt, y_sample, p_a_k, p_a_v, p_b_k, p_b_v, p_c_kv, p_c_kr,
            s_a_k, s_a_v, s_b_k, s_b_v, s_c_kv, s_c_kr)
```
